# Optimizing a Trainium2 kernel written in Bass

```python
import math
import jax, jax.numpy as jnp
from jax import lax
import numpy as np

D_MODEL = 1024
BATCH = 4
SEQ = 4096
DEPTH = 1

N_SB_HEADS = 8
SB_HEAD_DIM = 64
SB_WIDTH = N_SB_HEADS * SB_HEAD_DIM
Q_BLOCK = 128
N_SG_GROUPS = 8
SG_GROUP_DIM = 64
SG_WIDTH = N_SG_GROUPS * SG_GROUP_DIM
CHUNK = 128
N_BRANCH = 2
COL_Q = 0
COL_K = SB_WIDTH
COL_V = 2 * SB_WIDTH
COL_U = 3 * SB_WIDTH
COL_VSG = 3 * SB_WIDTH + SG_WIDTH
COL_GATE = 3 * SB_WIDTH + 2 * SG_WIDTH
PROJ_COLS = COL_GATE + N_BRANCH * D_MODEL
D_FF = 2816
CONV_WIDTH = 3
LN_EPS = 1e-5
DN_ALPHA = (2 * DEPTH) ** 0.25
DN_BETA = (8 * DEPTH) ** -0.25

kernel_name = "hybrid_stickbreak_sgu_convffn_deepnorm"


def layer_norm(x, g, b):
    xf = x.astype(jnp.float32)
    mu = jnp.mean(xf, axis=-1, keepdims=True)
    var = jnp.mean(jnp.square(xf - mu), axis=-1, keepdims=True)
    return ((xf - mu) * lax.rsqrt(var + LN_EPS) * g + b).astype(x.dtype)


def stick_breaking_block(q_blk, k_pre, v_pre, q_start):
    qb = q_blk.shape[2]
    L = k_pre.shape[2]
    z = jnp.einsum('bhqd,bhkd->bhqk', q_blk, k_pre,
                   preferred_element_type=jnp.float32) / math.sqrt(SB_HEAD_DIM)
    q_pos = q_start + jnp.arange(qb)[:, None]
    k_pos = jnp.arange(L)[None, :]
    causal = k_pos < q_pos
    log_beta = jax.nn.log_sigmoid(z)
    log_1m = jnp.where(causal, jax.nn.log_sigmoid(-z), 0.0)
    tail = lax.cumsum(log_1m, axis=3, reverse=True) - log_1m
    w = jnp.where(causal, jnp.exp(log_beta + tail), 0.0)
    return jnp.einsum('bhqk,bhkd->bhqd', w.astype(v_pre.dtype), v_pre)


def stick_breaking_attention(q, k, v):
    S = q.shape[2]
    outs = []
    for i in range(S // Q_BLOCK):
        qs = i * Q_BLOCK
        end = qs + Q_BLOCK
        outs.append(stick_breaking_block(q[:, :, qs:end], k[:, :, :end], v[:, :, :end], qs))
    return jnp.concatenate(outs, axis=2)


def spatial_gating(u, v, ln_g, ln_b, w_s, b_s):
    B, S, _ = u.shape
    v = layer_norm(v, ln_g, ln_b)
    vc = v.reshape(B, S // CHUNK, CHUNK, N_SG_GROUPS, SG_GROUP_DIM)
    w_causal = w_s * jnp.tril(jnp.ones((CHUNK, CHUNK), dtype=w_s.dtype))
    mixed = jnp.einsum('gts,bnsgc->bntgc', w_causal, vc) + b_s.T[None, None, :, :, None]
    return u * mixed.reshape(B, S, SG_WIDTH)


def causal_dwconv(h, w, b):
    S = h.shape[1]
    hp = jnp.pad(h, ((0, 0), (CONV_WIDTH - 1, 0), (0, 0)))
    out = b
    for i in range(CONV_WIDTH):
        out = out + hp[:, i:i + S] * w[i]
    return out


def mixer_sublayer(h, w_in, b_gate, ln_sg_g, ln_sg_b, w_spatial, b_spatial,
                   w_branch_a, w_branch_b, w_out, b_out, ln1_g, ln1_b):
    B, S, _ = h.shape
    proj = h @ w_in
    q, k, v, u_sg, v_sg, gates = jnp.split(
        proj, [COL_K, COL_V, COL_U, COL_VSG, COL_GATE], axis=-1)

    def heads(t):
        return t.reshape(B, S, N_SB_HEADS, SB_HEAD_DIM).transpose(0, 2, 1, 3)

    y_a = stick_breaking_attention(heads(q), heads(k), heads(v))
    y_a = y_a.transpose(0, 2, 1, 3).reshape(B, S, SB_WIDTH)
    y_b = spatial_gating(jax.nn.gelu(u_sg), jax.nn.gelu(v_sg),
                         ln_sg_g, ln_sg_b, w_spatial, b_spatial)
    g = jax.nn.sigmoid(gates + b_gate)
    g_a, g_b = jnp.split(g, N_BRANCH, axis=-1)
    merged = g_a * (y_a @ w_branch_a) + g_b * (y_b @ w_branch_b)
    y = merged @ w_out + b_out
    return layer_norm(DN_ALPHA * h + y, ln1_g, ln1_b)


def ffn_sublayer(h, w_up, conv_w, conv_b, w_down, b_down, ln2_g, ln2_b):
    up = causal_dwconv(h @ w_up, conv_w, conv_b)
    a, bv = jnp.split(up, 2, axis=-1)
    y = (jax.nn.silu(a) * bv) @ w_down + b_down
    return layer_norm(DN_ALPHA * h + y, ln2_g, ln2_b)


def setup_inputs(seed: int = 0) -> dict:
    key = jax.random.key(seed)
    ks = jax.random.split(key, 24)
    f32 = jnp.float32

    def nrm(k, shape, scale):
        return jax.random.normal(k, shape, dtype=f32) * scale

    x = nrm(ks[0], (BATCH, SEQ, D_MODEL), 1.0)
    col_scale = jnp.ones((PROJ_COLS,), f32).at[COL_V:COL_U].set(DN_BETA)
    w_in = nrm(ks[1], (DEPTH, D_MODEL, PROJ_COLS), D_MODEL ** -0.5) * col_scale
    b_gate = nrm(ks[2], (DEPTH, N_BRANCH * D_MODEL), 0.02)
    ln_sg_g = 1.0 + nrm(ks[3], (DEPTH, SG_WIDTH), 0.02)
    ln_sg_b = nrm(ks[4], (DEPTH, SG_WIDTH), 0.02)
    w_spatial = nrm(ks[5], (DEPTH, N_SG_GROUPS, CHUNK, CHUNK), 0.5 * CHUNK ** -0.5)
    b_spatial = 1.0 + nrm(ks[6], (DEPTH, N_SG_GROUPS, CHUNK), 0.02)
    w_branch_a = nrm(ks[7], (DEPTH, SB_WIDTH, D_MODEL), SB_WIDTH ** -0.5)
    w_branch_b = nrm(ks[8], (DEPTH, SG_WIDTH, D_MODEL), SG_WIDTH ** -0.5)
    w_out = nrm(ks[9], (DEPTH, D_MODEL, D_MODEL), DN_BETA * D_MODEL ** -0.5)
    b_out = nrm(ks[10], (DEPTH, D_MODEL), 0.02)
    ln1_g = 1.0 + nrm(ks[11], (DEPTH, D_MODEL), 0.02)
    ln1_b = nrm(ks[12], (DEPTH, D_MODEL), 0.02)
    w_up = nrm(ks[13], (DEPTH, D_MODEL, 2 * D_FF), DN_BETA * D_MODEL ** -0.5)
    conv_w = nrm(ks[14], (DEPTH, CONV_WIDTH, 2 * D_FF), CONV_WIDTH ** -0.5)
    conv_b = nrm(ks[15], (DEPTH, 2 * D_FF), 0.02)
    w_down = nrm(ks[16], (DEPTH, D_FF, D_MODEL), DN_BETA * D_FF ** -0.5)
    b_down = nrm(ks[17], (DEPTH, D_MODEL), 0.02)
    ln2_g = 1.0 + nrm(ks[18], (DEPTH, D_MODEL), 0.02)
    ln2_b = nrm(ks[19], (DEPTH, D_MODEL), 0.02)
    return {"x": x, "w_in": w_in, "b_gate": b_gate, "ln_sg_g": ln_sg_g, "ln_sg_b": ln_sg_b,
            "w_spatial": w_spatial, "b_spatial": b_spatial, "w_branch_a": w_branch_a,
            "w_branch_b": w_branch_b, "w_out": w_out, "b_out": b_out,
            "ln1_g": ln1_g, "ln1_b": ln1_b, "w_up": w_up, "conv_w": conv_w, "conv_b": conv_b,
            "w_down": w_down, "b_down": b_down, "ln2_g": ln2_g, "ln2_b": ln2_b}


def reference(x, w_in, b_gate, ln_sg_g, ln_sg_b, w_spatial, b_spatial, w_branch_a,
              w_branch_b, w_out, b_out, ln1_g, ln1_b, w_up, conv_w, conv_b,
              w_down, b_down, ln2_g, ln2_b):
    h = x
    for l in range(DEPTH):
        h = mixer_sublayer(h, w_in[l], b_gate[l], ln_sg_g[l], ln_sg_b[l], w_spatial[l],
                           b_spatial[l], w_branch_a[l], w_branch_b[l], w_out[l], b_out[l],
                           ln1_g[l], ln1_b[l])
        h = ffn_sublayer(h, w_up[l], conv_w[l], conv_b[l], w_down[l], b_down[l],
                         ln2_g[l], ln2_b[l])
    return h
```

```python
import numpy as np
import ml_dtypes
import concourse.bass as bass
import concourse.mybir as mybir
from concourse.bass_utils import run_bass_kernel_spmd

F32 = mybir.dt.float32
BF16 = mybir.dt.bfloat16
AF = mybir.ActivationFunctionType
ALU = mybir.AluOpType

D = 1024
S = 4096
NSLOT = 4
SL = 512
T = NSLOT * SL
NHALO = 8
TT = T + NHALO
DFF = 2816
NCH = DFF // 128
ALPHA = 2.0 ** 0.25
LN_EPS = 1e-5
GELU_C = 1.5957691216057308
NKB_H = 28
ARENA_BYTES = 190 * 1024

DEBUG = {}
EMBED_WAITS = True


class Buf:
    __slots__ = ("name", "lo", "hi", "lw", "rd", "al", "inherit", "excl")

    def __init__(self, name, lo, hi):
        self.name, self.lo, self.hi = name, lo, hi
        self.lw = None
        self.rd = {}
        self.al = []
        self.inherit = None
        self.excl = False


class Op:
    __slots__ = ("eng", "fn", "deps", "sig", "flag", "dma", "idx", "nfree")


class Prog:
    ENGS = ("pe", "act", "dve", "pool", "sp")

    def __init__(self):
        self.ops = {e: [] for e in self.ENGS}
        self.abufs = []
        self.dead = []
        self.n = 0

    def buf(self, name, lo=None, hi=None):
        b = Buf(name, lo, hi)
        if lo is not None:
            inh = set()
            for d in self.dead:
                if d.lo < hi and lo < d.hi:
                    if d.lw is not None:
                        inh.add(d.lw)
                    inh.update(d.rd.values())
            b.inherit = inh or None
            for o in self.abufs:
                if o.lo < hi and lo < o.hi:
                    o.al.append(b)
                    b.al.append(o)
            self.abufs.append(b)
        return b

    def kill(self, b):
        if b.lo is None:
            return
        self.abufs.remove(b)
        for o in b.al:
            o.al.remove(b)
        b.al = []
        self.dead.append(b)

    def op(self, eng, fn, reads=(), writes=(), dma=False, nfree=0):
        o = Op()
        o.eng, o.fn, o.dma, o.flag, o.sig = eng, fn, dma, False, None
        o.nfree = nfree
        o.idx = self.n
        self.n += 1
        raw, oth = set(), set()
        for b in list(reads) + list(writes):
            if b.inherit:
                oth |= b.inherit
                b.inherit = None
        for b in reads:
            for bb in [b] + b.al:
                if bb.lw is not None:
                    raw.add(bb.lw)
            if b.excl:
                for r in b.rd.values():
                    if r.eng != eng:
                        oth.add(r)
        for b in writes:
            for bb in [b] + b.al:
                if bb.lw is not None:
                    oth.add(bb.lw)
                for r in bb.rd.values():
                    oth.add(r)
        deps = []
        for d in raw | oth:
            if d is o:
                continue
            if d.dma or o.dma:
                deps.append(d)
            elif d.eng != eng:
                deps.append(d)
            elif eng == "pool":
                deps.append(d)
            elif eng != "pe" and d in raw and d.nfree < 256:
                deps.append(d)
        best = {}
        out = []
        for d in deps:
            if d.dma:
                out.append(d)
            else:
                if d.eng not in best or best[d.eng].idx < d.idx:
                    best[d.eng] = d
        out.extend(best.values())
        o.deps = out
        for d in out:
            d.flag = True
        for b in reads:
            key = ("dma", o.idx) if dma else eng
            b.rd[key] = o
        for b in writes:
            for bb in [b] + b.al:
                bb.lw = o
                bb.rd = {}
        self.ops[eng].append(o)
        return o

    def emit(self, nc):
        NDS = 12
        sems = {}
        with nc.Block() as block:
            import contextlib
            with contextlib.ExitStack() as st:
                esem = {e: st.enter_context(nc.semaphore("s_" + e)) for e in self.ENGS}
                dsem = {e: [st.enter_context(nc.semaphore("d_%s%d" % (e, k))) for k in range(NDS)]
                        for e in ("pool", "sp")}
                for e in self.ENGS:
                    cnt = 0
                    dcnt = [0] * NDS
                    k = 0
                    for o in self.ops[e]:
                        if o.dma:
                            dcnt[k] += 16
                            o.sig = (dsem[e][k], dcnt[k])
                            k = (k + 1) % NDS
                        elif o.flag:
                            cnt += 1
                            o.sig = (esem[e], cnt)

                def run(engname, eng):
                    waited = {}
                    for o in self.ops[engname]:
                        need = {}
                        for d in o.deps:
                            sem, val = d.sig
                            if waited.get(id(sem), 0) < val and need.get(id(sem), (None, 0))[1] < val:
                                need[id(sem)] = (sem, val)
                        if o.dma:
                            sem, val = o.sig
                            if val > 16 and waited.get(id(sem), 0) < val - 16 and need.get(id(sem), (None, 0))[1] < val - 16:
                                need[id(sem)] = (sem, val - 16)
                        need = list(need.values())
                        embed = None
                        if need and not o.dma and EMBED_WAITS:
                            embed = need.pop()
                        for sem, val in need:
                            eng.wait_ge(sem, val)
                            waited[id(sem)] = val
                        n0 = nc.n_instructions() if embed is not None else 0
                        ins = o.fn(eng)
                        if embed is not None:
                            sem, val = embed
                            if nc.n_instructions() - n0 == 1:
                                ins._wait_ge(sem, val)
                            else:
                                raise RuntimeError("multi-instruction op cannot carry an embedded wait")
                            waited[id(sem)] = val
                        if o.dma:
                            ins.then_inc(o.sig[0], 16)
                        elif o.flag:
                            ins.then_inc(o.sig[0], 1)

                @block.tensor
                def _(e):
                    run("pe", e)

                @block.scalar
                def _(e):
                    run("act", e)

                @block.vector
                def _(e):
                    run("dve", e)

                @block.gpsimd
                def _(e):
                    run("pool", e)

                @block.sync
                def _(e):
                    run("sp", e)


class Tile:
    def __init__(self, ap, buf, off, nbytes):
        self.ap, self.buf, self.off, self.nbytes = ap, buf, off, nbytes


class Arena:
    def __init__(self, P, arena_ap, nbytes):
        self.P, self.a, self.nbytes = P, arena_ap, nbytes
        self.free = [(0, nbytes)]
        self.live = {}

    def alloc(self, name, shape, dtype, tracked=True, top=False):
        esz = 4 if dtype == F32 else 2
        n = 1
        for s in shape:
            n *= s
        nb = (n * esz + 63) // 64 * 64
        order = list(enumerate(self.free))
        if top:
            order = order[::-1]
        for k, (lo, hi) in order:
            if hi - lo >= nb:
                off = hi - nb if top else lo
                if hi - lo == nb:
                    self.free.pop(k)
                elif top:
                    self.free[k] = (lo, hi - nb)
                else:
                    self.free[k] = (lo + nb, hi)
                break
        else:
            raise RuntimeError("arena OOM for %s (%d B); free=%s" % (name, nb, self.free))
        ap = self.a[:, off // 2: off // 2 + n * esz // 2]
        if dtype == F32:
            ap = ap.bitcast(F32)
        if len(shape) == 2:
            ap = ap.rearrange("p (a b) -> p a b", a=shape[0])
        elif len(shape) == 3:
            ap = ap.rearrange("p (a b c) -> p a b c", a=shape[0], b=shape[1])
        t = Tile(ap, self.P.buf(name, off, off + nb) if tracked else None, off, nb)
        t.esz = esz
        t.name = name
        t.subs = []
        self.live[name] = t
        return t

    def sub(self, t, name, elem_lo, elem_hi):
        assert t.buf is None, "tiles with sub-buffers must be allocated with tracked=False"
        b = self.P.buf(name, t.off + elem_lo * t.esz, t.off + elem_hi * t.esz)
        t.subs.append(b)
        return b

    def release(self, *tiles):
        for t in tiles:
            del self.live[t.name]
            self.free.append((t.off, t.off + t.nbytes))
            if t.buf is not None:
                self.P.kill(t.buf)
            for b in t.subs:
                self.P.kill(b)
        self.free.sort()
        merged = []
        for lo, hi in self.free:
            if merged and merged[-1][1] == lo:
                merged[-1] = (merged[-1][0], hi)
            else:
                merged.append((lo, hi))
        self.free = merged


def build_program(phases_upto=99):
    nc = bass.Bass("TRN2", target_bir_lowering=False)
    P = Prog()

    def din(name, shape, dt=F32):
        return nc.dram_tensor(name, list(shape), dt, kind="ExternalInput").ap()

    xT_all = din("xT_all", [D, S])
    xT_own = din("xT_own", [D, TT])
    xT_hb = din("xT_hb", [D, 512])
    x_own = din("x_own", [TT, D])
    w_in = din("w_in", [D, 4608])
    w_bra = din("w_bra", [512, D])
    w_brb = din("w_brb", [512, D])
    w_out = din("w_out", [D, D])
    w_up = din("w_up", [D, 2 * DFF])
    w_down = din("w_down", [DFF, D])
    bgate_c = din("bgate_c", [128, 16])
    convp_c = din("convp_c", [128, 4 * 44])
    flags_c = din("flags_c", [128, NHALO])
    lnsg_g = din("lnsg_g", [128, 512])
    lnsg_b = din("lnsg_b", [128, 512])
    wsT = din("wsT", [128, 8, 128])
    trilT = din("trilT", [128, 128])
    bsp_c = din("bsp_c", [128, 4, 128])
    rowp = {k: din(k, [128, D]) for k in ("b_out", "ln1_g", "ln1_b", "b_down", "ln2_g", "ln2_b")}
    ident_c = din("ident_c", [128, 128], BF16)
    U_c = din("U_c", [128, 128], BF16)
    L_c = din("L_c", [128, 128], BF16)
    maskM_c = din("maskM_c", [128, 8, 512], BF16)
    maskH_c = din("maskH_c", [128, NKB_H, 64], BF16)
    out_d = nc.dram_tensor("out", [T, D], F32, kind="ExternalOutput").ap()
    h1s = nc.dram_tensor("h1s", [TT, D], F32).ap()
    h1s_buf = [P.buf("h1s%d" % r) for r in range(17)]
    dbg_outs = {}

    import contextlib
    with contextlib.ExitStack() as st:
        arena_t = st.enter_context(nc.sbuf_tensor("arena", [128, ARENA_BYTES // 2], BF16))
        A = Arena(P, arena_t[:], ARENA_BYTES)
        psum = []
        psall_t = st.enter_context(nc.psum_tensor("psall", [128, 4096], F32))
        psall = psall_t[:]
        for k in range(8):
            psum.append((psall[:, k * 512:(k + 1) * 512], P.buf("ps%d" % k)))
            psum[-1][1].excl = True

        def nfree_of(ap):
            n = 1
            for s in ap.shape[1:]:
                n *= s
            return n

        def mm(psb, out, lhsT, rhs, start, stop, reads):
            P.op("pe", lambda e: e.matmul(out, lhsT, rhs, start=start, stop=stop),
                 reads=reads, writes=[psb])

        def act(out, in_, func, reads, writes, bias=None, scale=None):
            kw = {}
            if bias is not None:
                kw["bias"] = bias
            if scale is not None:
                kw["scale"] = scale
            P.op("act", lambda e: e.activation(out, in_, func, **kw), reads=reads, writes=writes, nfree=nfree_of(out))

        def tt(eng, out, in0, in1, op, reads, writes):
            P.op(eng, lambda e: e.tensor_tensor(out, in0, in1, op), reads=reads, writes=writes, nfree=nfree_of(out))

        def ts(eng, out, in0, s1, s2, op0, op1, reads, writes):
            if s2 is None:
                P.op(eng, lambda e: e.tensor_scalar(out, in0, s1, None, op0), reads=reads, writes=writes, nfree=nfree_of(out))
            else:
                P.op(eng, lambda e: e.tensor_scalar(out, in0, s1, s2, op0, op1), reads=reads, writes=writes, nfree=nfree_of(out))

        def stt(eng, out, in0, scalar, in1, op0, op1, reads, writes):
            P.op(eng, lambda e: e.scalar_tensor_tensor(out, in0, scalar, in1, op0, op1),
                 reads=reads, writes=writes, nfree=nfree_of(out))

        def cp(eng, out, in_, reads, writes):
            if eng == "act":
                P.op("act", lambda e: e.copy(out, in_), reads=reads, writes=writes, nfree=nfree_of(out))
            else:
                P.op(eng, lambda e: e.tensor_copy(out, in_), reads=reads, writes=writes, nfree=nfree_of(out))

        def dma(eng, out, in_, reads, writes):
            return P.op(eng, lambda e: e.dma_start(out=out, in_=in_), reads=reads, writes=writes, dma=True)

        def dump(name, tile_ap, shape, dtype, reads):
            d = nc.dram_tensor("dbg_" + name, list(shape), dtype, kind="ExternalOutput").ap()
            dbg_outs[name] = dma("sp", d, tile_ap, reads, [P.buf("dbg_" + name)])

        def load_const(name, src, shape, dtype, eng="sp", parts=128):
            t = A.alloc(name, shape, dtype)
            dma(eng, t.ap[0:parts] if parts != 128 else t.ap, src, [], [t.buf])
            return t

        evac_rr = [0]

        def evac(out, in_, reads, writes):
            evac_rr[0] ^= 1
            cp("dve" if evac_rr[0] else "act", out, in_, reads, writes)

        pbank = [0]

        def next_bank():
            pbank[0] = (pbank[0] + 1) % 8
            return psum[pbank[0]]

        segs = [(i * SL, SL) for i in range(NSLOT)] + [(T, NHALO)]
        out_stores = []

        ident = A.alloc("ident", [128], BF16)
        Umat = A.alloc("U", [128], BF16)
        Lmat = A.alloc("L", [128], BF16)
        maskM = A.alloc("maskM", [8, 512], BF16)
        maskH = A.alloc("maskH", [NKB_H, 64], BF16)
        bgate = A.alloc("bgate", [16], F32)
        convp = A.alloc("convp", [4 * 44], F32)
        flags = A.alloc("flags", [NHALO], F32)

        def load_persistent_consts():
            for t_, s_ in ((ident, ident_c), (Umat, U_c), (Lmat, L_c), (maskM, maskM_c), (maskH, maskH_c),
                           (bgate, bgate_c), (convp, convp_c), (flags, flags_c)):
                dma("sp", t_.ap, s_, [], [t_.buf])

        kT = A.alloc("kT", [4, S], BF16, tracked=False)
        Vt = A.alloc("V", [32, 512], BF16, tracked=False)
        kT_b = [[A.sub(kT, "kT%d_%d" % (fc, t8), fc * S + t8 * 512, fc * S + t8 * 512 + 512)
                 for t8 in range(8)] for fc in range(4)]
        V_b = [A.sub(Vt, "V%d" % k, k * 512, k * 512 + 512) for k in range(32)]
        wk = A.alloc("wk", [8, 512], BF16)
        wv = A.alloc("wv", [8, 512], BF16)
        xa = [A.alloc("xa%d" % k, [8, 512], BF16) for k in range(2)]
        xTa = xT_all.rearrange("(c p) t -> p c t", p=128)
        w_in_r0 = w_in.rearrange("(c p) f -> p c f", p=128)
        dma("pool", wk.ap, w_in_r0[:, :, 512:1024], [], [wk.buf])
        xo = A.alloc("xo", [8, TT], BF16)
        wq = A.alloc("wq", [8, 512], BF16)
        NFRONT = 5 if phases_upto >= 3 else 8
        for t8 in range(NFRONT if phases_upto >= 1 else 0):
            x_ = xa[t8 % 2]
            dma("pool", x_.ap, xTa[:, :, t8 * 512:(t8 + 1) * 512], [], [x_.buf])
            if t8 == 0:
                dma("pool", wv.ap, w_in_r0[:, :, 1024:1536], [], [wv.buf])
            if t8 == 1:
                load_persistent_consts()
            xTo = xT_own.rearrange("(c p) t -> p c t", p=128)
            if 1 <= t8 <= 4:
                s_ = t8 - 1
                dma("pool", xo.ap[:, :, s_ * SL:(s_ + 1) * SL], xTo[:, :, s_ * SL:(s_ + 1) * SL], [], [xo.buf])
            if t8 == 4:
                dma("pool", xo.ap[:, :, T:TT], xTo[:, :, T:TT], [], [xo.buf])
                dma("pool", wq.ap, w_in_r0[:, :, 0:512], [], [wq.buf])
            for fc in range(4):
                ps, psb = next_bank()
                for kc in range(8):
                    mm(psb, ps, wk.ap[:, kc, fc * 128:(fc + 1) * 128], x_.ap[:, kc, :],
                       kc == 0, kc == 7, [wk.buf, x_.buf])
                evac(kT.ap[:, fc, t8 * 512:(t8 + 1) * 512], ps, [psb], [kT_b[fc][t8]])
            for tb in range(4):
                ps, psb = next_bank()
                for kc in range(8):
                    mm(psb, ps, x_.ap[:, kc, tb * 128:(tb + 1) * 128], wv.ap[:, kc, :],
                       kc == 0, kc == 7, [wv.buf, x_.buf])
                evac(Vt.ap[:, t8 * 4 + tb, :], ps, [psb], [V_b[t8 * 4 + tb]])
        A.release(wk, wv, xa[0], xa[1])
        if "kT" in DEBUG:
            dump("kT", kT.ap, [128, 4, S], BF16, [b for r in kT_b for b in r])
            dump("V", Vt.ap, [128, 32, 512], BF16, V_b)

        qT = A.alloc("qT", [4, TT], BF16, tracked=False)
        qT_b = [[A.sub(qT, "q%d_%d" % (fc, si), fc * TT + c0, fc * TT + c0 + n) for si, (c0, n) in enumerate(segs)]
                for fc in range(4)]
        side_jobs = []
        side_bank = [0]

        def q_group_jobs(si, fc, bank=None):
            c0, n = segs[si]
            holder = {}

            def job(kc):
                if kc == 0:
                    holder["b"] = psum[2 + side_bank[0]] if bank is None else bank
                    side_bank[0] ^= 1
                ps, psb = holder["b"]
                mm(psb, ps[:, :n], wq.ap[:, kc, fc * 128:(fc + 1) * 128], xo.ap[:, kc, c0:c0 + n],
                   kc == 0, kc == 7, [wq.buf, xo.buf])
                if kc == 7:
                    ts("dve", qT.ap[:, fc, c0:c0 + n], ps[:, :n], 0.125, None, ALU.mult, None, [psb], [qT_b[fc][si]])
            return [lambda p=p: job(p) for p in range(8)]

        if phases_upto >= 2:
            for fc in range(4):
                for j_ in q_group_jobs(0, fc, bank=next_bank()):
                    j_()
            late_segs = [1, 2, 3, 4] if phases_upto >= 3 else []
            for si in late_segs:
                for fc in range(4):
                    side_jobs.extend(q_group_jobs(si, fc))
            late = {}
            kv_done = set(range(NFRONT))

            def kv_switch():
                A.release(xo, wq)
                late["wk"] = A.alloc("wk2", [8, 512], BF16)
                late["wv"] = A.alloc("wv2", [8, 512], BF16)
                late["xa"] = [A.alloc("xa2_%d" % k, [8, 512], BF16) for k in range(2)]
                dma("pool", late["wk"].ap, w_in_r0[:, :, 512:1024], [], [late["wk"].buf])
                kv_load(NFRONT)
                dma("pool", late["wv"].ap, w_in_r0[:, :, 1024:1536], [], [late["wv"].buf])
                kv_load(NFRONT + 1)

            def kv_load(t8):
                x_ = late["xa"][t8 % 2]
                dma("pool", x_.ap, xTa[:, :, t8 * 512:(t8 + 1) * 512], [], [x_.buf])

            def kv_group_jobs(t8, grp):
                holder = {}

                def job(kc):
                    x_ = late["xa"][t8 % 2]
                    if kc == 0:
                        holder["b"] = psum[2 + side_bank[0]]
                        side_bank[0] ^= 1
                    ps, psb = holder["b"]
                    if grp < 4:
                        fc = grp
                        mm(psb, ps, late["wk"].ap[:, kc, fc * 128:(fc + 1) * 128], x_.ap[:, kc, :],
                           kc == 0, kc == 7, [late["wk"].buf, x_.buf])
                        if kc == 7:
                            cp("dve", kT.ap[:, fc, t8 * 512:(t8 + 1) * 512], ps, [psb], [kT_b[fc][t8]])
                    else:
                        tb = grp - 4
                        mm(psb, ps, x_.ap[:, kc, tb * 128:(tb + 1) * 128], late["wv"].ap[:, kc, :],
                           kc == 0, kc == 7, [late["wv"].buf, x_.buf])
                        if kc == 7:
                            cp("dve", Vt.ap[:, t8 * 4 + tb, :], ps, [psb], [V_b[t8 * 4 + tb]])
                    if grp == 7 and kc == 7:
                        kv_done.add(t8)
                        if t8 + 2 < 8:
                            kv_load(t8 + 2)
                return [lambda p=p: job(p) for p in range(8)]

            def kv_finish():
                A.release(late["wk"], late["wv"], late["xa"][0], late["xa"][1])
                late["xo"] = A.alloc("xo_b", [8, TT], BF16)
                dma("pool", late["xo"].ap, xT_own.rearrange("(c p) t -> p c t", p=128), [], [late["xo"].buf])

            if late_segs and NFRONT < 8:
                side_jobs.append(kv_switch)
                side_jobs.extend([(lambda: None)] * 27)
                for t8_ in range(NFRONT, 8):
                    for grp in range(8):
                        side_jobs.extend(kv_group_jobs(t8_, grp))
                side_jobs.append(kv_finish)
        if phases_upto < 3:
            A.release(wq)
        if "qT" in DEBUG:
            dump("qT", qT.ap, [128, 4, TT], BF16, [b for r in qT_b for b in r])

        ya = A.alloc("ya", [4, TT], BF16, tracked=False)
        ya_pb = [[A.sub(ya, "ya%d_%d" % (hp, si), hp * TT + c0, hp * TT + c0 + n) for si, (c0, n) in enumerate(segs)]
                 for hp in range(4)]
        ya_b = [ya_pb[h // 2] for h in range(8)]
        NE = 4
        e3 = [A.alloc("e3_%d" % k, [1024], F32) for k in range(NE)]
        qpad = [[A.alloc("qpad%d%d" % (b_, par), [SL], BF16) for par in range(2)] for b_ in range(2)]
        qh = A.alloc("qh", [8, NHALO], BF16)
        for b_ in range(2):
            for par in range(2):
                P.op("pool", lambda e, t=qpad[b_][par]: e.memset(t.ap, 0.0), reads=[], writes=[qpad[b_][par].buf], nfree=SL)
        P.op("pool", lambda e: e.memset(qh.ap, 0.0), reads=[], writes=[qh.buf], nfree=64)
        Zmat = A.alloc("Zmat", [128], BF16)
        P.op("pool", lambda e: e.memset(Zmat.ap, 0.0), reads=[], writes=[Zmat.buf], nfree=128)
        sp2 = [A.alloc("sp2_%d" % k, [1024], BF16) for k in range(2)]
        ex2 = [A.alloc("ex2_%d" % k, [1024], F32) for k in range(2)]
        w2 = [A.alloc("w2_%d" % k, [1024], BF16) for k in range(2)]

        def attn_stream(steps):
            N = len(steps)
            JOB3_UNTIL = 80
            ZB = 1

            def nch(i):
                return len(steps[i]["chains"])

            def TW(i):
                return 1024 if nch(i) == 2 else steps[i]["chains"][0]["W"]

            def CS(i):
                return steps[i].get("cs", 0)

            def vw(ap, i):
                if nch(i) == 1:
                    return ap[:, :TW(i)]
                cs = CS(i)
                if cs == 0:
                    return ap[:, :1024]
                return ap[:, :1024].rearrange("p (a b) -> p a b", a=2)[:, :, cs:]

            def Z(i):
                s = steps[i]
                cs = CS(i)
                if s.get("pre") is not None:
                    s["pre"]()
                if phases_upto >= 3:
                    while (s["kb"] // 4) not in kv_done:
                        side_jobs.pop(0)()
                for ci, ch in enumerate(s["chains"]):
                    ps, psb = psum[2 * (i % ZB) + ci]
                    for (h, q_ap, q_buf, n, col0) in ch["groups"]:
                        mm(psb, ps[:, col0 + cs:col0 + n], kT.ap[:, h // 2, s["kb"] * 128:(s["kb"] + 1) * 128],
                           q_ap[:, cs:n], True, True, [kT_b[h // 2][s["kb"] // 4], q_buf])

            def EXP(i):
                p = i % ZB
                act(vw(e3[i % NE].ap, i), vw(psall[:, 1024 * p:1024 * p + 1024], i), AF.Exp,
                    [psum[2 * p + ci][1] for ci in range(nch(i))], [e3[i % NE].buf])

            def MASK(i):
                m = steps[i]["mask"]
                if m is None:
                    return
                cs = CS(i)
                for ci, ch in enumerate(steps[i]["chains"]):
                    W = ch["W"]
                    ev = e3[i % NE].ap[:, ci * 512 + cs:ci * 512 + W]
                    tt("dve", ev, ev, m[0][:, cs:W], ALU.mult, [e3[i % NE].buf, m[1]], [e3[i % NE].buf])

            def LN(i):
                act(vw(sp2[i % 2].ap, i), vw(e3[i % NE].ap, i), AF.Ln, [e3[i % NE].buf], [sp2[i % 2].buf], bias=1.0)

            def ZERO(i, base):
                s = steps[i]
                for ci, ch in enumerate(s["chains"]):
                    W = ch["W"]
                    pz, pzb = psum[base + ci]
                    mm(pzb, pz[:, :W], Zmat.ap, Umat.ap[:, :W] if W <= 128 else maskM.ap[:, 0, :W], True, False,
                       [Zmat.buf, Umat.buf, maskM.buf])

            def U(i):
                s = steps[i]
                cs = CS(i)
                if s["first"]:
                    ZERO(i, 4)
                for ci, ch in enumerate(s["chains"]):
                    W = ch["W"]
                    pc, pcb = psum[4 + ci]
                    mm(pcb, pc[:, cs:W], Umat.ap, sp2[i % 2].ap[:, ci * 512 + cs:ci * 512 + W], False, s["last"],
                       [Umat.buf, sp2[i % 2].buf])

            def EXPC(i):
                act(vw(ex2[i % 2].ap, i), vw(psall[:, 2048:3072], i), AF.Exp,
                    [psum[4 + ci][1] for ci in range(nch(i))], [ex2[i % 2].buf], scale=-1.0)

            def L(i):
                s = steps[i]
                cs = CS(i)
                if s["last"]:
                    return
                for ci, ch in enumerate(s["chains"]):
                    W = ch["W"]
                    pc, pcb = psum[4 + ci]
                    mm(pcb, pc[:, cs:W], Lmat.ap, sp2[i % 2].ap[:, ci * 512 + cs:ci * 512 + W], False, False,
                       [Lmat.buf, sp2[i % 2].buf])

            def WW(i):
                tt("dve", vw(w2[i % 2].ap, i), vw(ex2[i % 2].ap, i), vw(e3[i % NE].ap, i), ALU.mult,
                   [ex2[i % 2].buf, e3[i % NE].buf], [w2[i % 2].buf])

            def PV(i):
                s = steps[i]
                cs = CS(i)
                if s["first"]:
                    ZERO(i, 6)
                for ci, ch in enumerate(s["chains"]):
                    po, pob = psum[6 + ci]
                    for (h, q_ap, q_buf, n, col0) in ch["groups"]:
                        mm(pob, po[:, col0 + cs:col0 + n], Vt.ap[:, s["kb"], (h // 2) * 128:(h // 2 + 1) * 128],
                           w2[i % 2].ap[:, ci * 512 + col0 + cs:ci * 512 + col0 + n], False, s["last"],
                           [V_b[s["kb"]], w2[i % 2].buf])
                if s["last"]:
                    for ci, ch in enumerate(s["chains"]):
                        po, pob = psum[6 + ci]
                        for (h, si, c0, n, col0) in ch["out"]:
                            pr = (h % 2) * 64
                            cp("dve", ya.ap[pr:pr + 64, h // 2, c0:c0 + n], po[pr:pr + 64, col0:col0 + n], [pob], [ya_b[h][si]])

            Z(0)
            EXP(0)
            MASK(0)
            Z(1)
            EXP(1)
            MASK(1)
            Z(2)
            LN(0)
            U(0)
            for i in range(N):
                if i + 2 < N:
                    EXP(i + 2)
                if i + 3 < N:
                    Z(i + 3)
                if side_jobs:
                    side_jobs.pop(0)()
                if side_jobs and i < JOB3_UNTIL:
                    side_jobs.pop(0)()
                if i + 2 < N:
                    MASK(i + 2)
                EXPC(i)
                WW(i)
                L(i)
                if side_jobs:
                    side_jobs.pop(0)()
                if i + 1 < N:
                    LN(i + 1)
                    U(i + 1)
                PV(i)

        if phases_upto >= 3:
            steps = []
            gidx = 0
            for i in range(NSLOT):
                nkb = 8 * (i + 1)
                for hp in range(4):
                    b_ = gidx % 2
                    gidx += 1
                    chains = []
                    for par in range(2):
                        h = 2 * hp + par
                        chains.append(dict(groups=[(h, qpad[b_][par].ap, qpad[b_][par].buf, SL, 0)], W=SL,
                                           out=[(h, i, i * SL, SL, 0)]))

                    def pre(i=i, hp=hp, b_=b_):
                        for par in range(2):
                            pr = par * 64
                            cp("dve", qpad[b_][par].ap[pr:pr + 64, :], qT.ap[pr:pr + 64, hp, i * SL:(i + 1) * SL],
                               [qT_b[hp][i]], [qpad[b_][par].buf])

                    for kb in range(nkb - 1, -1, -1):
                        steps.append(dict(chains=chains, kb=kb, first=(kb == nkb - 1), last=(kb == 0),
                                          pre=(pre if kb == nkb - 1 else None),
                                          cs=(128 * max(0, kb - (nkb - 8) - 4) if not DEBUG.get("notrim") else 0),
                                          mask=((maskM.ap[:, kb - (nkb - 8), :], maskM.buf) if kb >= nkb - 8 else None)))
            halo_chain = dict(groups=[(h, qh.ap[:, h, :], qh.buf, NHALO, h * NHALO) for h in range(8)], W=64,
                              out=[(h, 4, T, NHALO, h * NHALO) for h in range(8)])

            def pre_h():
                for par in range(2):
                    pr = par * 64
                    cp("dve", qh.ap[pr:pr + 64].rearrange("p (a two) b -> p a two b", two=2)[:, :, par, :],
                       qT.ap[pr:pr + 64, :, T:TT], [qT_b[fc][4] for fc in range(4)], [qh.buf])

            early = {}

            def post_q():
                A.release(qT)
                early["qT_released"] = True
                if phases_upto >= 4:
                    early["wuv"] = A.alloc("wuv", [8, 1024], BF16)
                    dma("pool", early["wuv"].ap, w_in.rearrange("(c p) f -> p c f", p=128)[:, :, 1536:2560], [],
                        [early["wuv"].buf])

            ih = 4 * 8 + 4 * 16
            slot1_pre = steps[ih]["pre"]
            steps[ih]["pre"] = lambda: (pre_h(), slot1_pre())
            for kb in range(NKB_H - 1, -1, -1):
                steps.append(dict(chains=[halo_chain], kb=kb, first=(kb == NKB_H - 1), last=(kb == 0),
                                  pre=(post_q if kb == NKB_H - 1 else None),
                                  mask=(maskH.ap[:, kb, :], maskH.buf)))
            attn_stream(steps)
            assert not side_jobs
            if "xo" in late:
                xo = late["xo"]
            else:
                A.release(wq)
        A.release(*e3, *sp2, *ex2, *w2, qh, *qpad[0], *qpad[1], Zmat)
        if phases_upto >= 3 and early.get("qT_released"):
            A.release(kT, Vt)
        else:
            A.release(kT, Vt, qT)
        if "ya" in DEBUG:
            dump("ya", ya.ap, [128, 4, TT], BF16, [b for r in ya_pb for b in r])

        yb = A.alloc("yb", [4, TT], BF16, tracked=False)
        yb_pb = [[A.sub(yb, "yb%d_%d" % (gp, si), gp * TT + c0, gp * TT + c0 + n) for si, (c0, n) in enumerate(segs)]
                 for gp in range(4)]
        if phases_upto >= 5:
            wg = [A.alloc("wg%d" % k, [2, 8, 128], BF16, top=True) for k in range(2)]
            wb = [A.alloc("wb%d" % k, [2, 4, 128], BF16, top=True) for k in range(2)]
            w_in_r = w_in.rearrange("(c p) f -> p c f", p=128)

            def load_wgb(fc):
                k = fc % 2
                dma("pool", wg[k].ap[:, 0], w_in_r[:, :, 2560 + fc * 128:2560 + (fc + 1) * 128], [], [wg[k].buf])
                dma("pool", wg[k].ap[:, 1], w_in_r[:, :, 3584 + fc * 128:3584 + (fc + 1) * 128], [], [wg[k].buf])
                dma("pool", wb[k].ap[:, 0], w_bra.rearrange("(c p) f -> p c f", p=128)[:, :, fc * 128:(fc + 1) * 128],
                    [], [wb[k].buf])
                dma("pool", wb[k].ap[:, 1], w_brb.rearrange("(c p) f -> p c f", p=128)[:, :, fc * 128:(fc + 1) * 128],
                    [], [wb[k].buf])
        if phases_upto >= 4:
            if phases_upto >= 3 and "wuv" in early:
                wuv = early["wuv"]
            else:
                wuv = A.alloc("wuv", [8, 1024], BF16)
                dma("pool", wuv.ap, w_in.rearrange("(c p) f -> p c f", p=128)[:, :, 1536:2560], [], [wuv.buf])
            xhb = A.alloc("xhb", [8, 512], BF16)
            dma("pool", xhb.ap, xT_hb.rearrange("(c p) t -> p c t", p=128), [], [xhb.buf])
            lg = load_const("lnsg_g", lnsg_g, [512], F32)
            lb = load_const("lnsg_b", lnsg_b, [512], F32)
            bsp = load_const("bsp", bsp_c, [4, 128], F32)
            wsf = load_const("wsf", wsT, [8, 128], F32)
            tril = load_const("tril", trilT, [128], F32)
            wsb = A.alloc("wsb", [8, 128], BF16)
            for g in range(8):
                tt("dve", wsb.ap[:, g, :], wsf.ap[:, g, :], tril.ap, ALU.mult, [wsf.buf, tril.buf], [wsb.buf])
            if phases_upto >= 5:
                load_wgb(0)
            ug2 = [A.alloc("ug_%d" % k, [4, SL], F32, tracked=False) for k in range(2)]
            ug_b2 = [[A.sub(ug2[k], "ug%d_%d" % (k, g), g * SL, (g + 1) * SL) for g in range(4)] for k in range(2)]
            gt = [A.alloc("gt%d" % k, [512], F32) for k in range(2)]
            vg2 = [[A.alloc("vg%d_%d" % (k, b_), [512], F32) for b_ in range(4)] for k in range(2)]
            vn = [A.alloc("vn%d" % k, [512], BF16) for k in range(4)]
            st2 = [(A.alloc("st6_%d" % k, [4, 6], F32), A.alloc("mv_%d" % k, [4, 2], F32), A.alloc("rs_%d" % k, [4], F32))
                   for k in range(2)]

            def D_proj(si):
                c0, n = segs[si]
                ug, ug_b, vg = ug2[si % 2], ug_b2[si % 2], vg2[si % 2]
                stt6, mv, rs = st2[si % 2]
                for g in range(4):
                    ps, psb = next_bank()
                    for kc in range(8):
                        mm(psb, ps[:, :n], wuv.ap[:, kc, g * 128:(g + 1) * 128], xo.ap[:, kc, c0:c0 + n],
                           kc == 0, kc == 7, [wuv.buf, xo.buf])
                    act(ug.ap[:, g, :n], ps[:, :n], AF.Gelu_apprx_tanh, [psb], [ug_b[g]])
                for bi in range(4):
                    xsrc, xbuf, col0 = (xo.ap, xo.buf, c0 + bi * 128) if si < 4 else (xhb.ap, xhb.buf, bi * 128)
                    ps, psb = next_bank()
                    for kc in range(8):
                        mm(psb, ps, xsrc[:, kc, col0:col0 + 128], wuv.ap[:, kc, 512:1024], kc == 0, kc == 7,
                           [xbuf, wuv.buf])
                    act(vg[bi].ap, ps, AF.Gelu_apprx_tanh, [psb], [vg[bi].buf])
                    P.op("dve", lambda e, bi=bi: e.bn_stats(stt6.ap[:, bi, :], vg[bi].ap), reads=[vg[bi].buf],
                         writes=[stt6.buf])
                    P.op("dve", lambda e, bi=bi: e.bn_aggr(mv.ap[:, bi, :], stt6.ap[:, bi, :]), reads=[stt6.buf],
                         writes=[mv.buf])
                act(rs.ap, mv.ap[:, :, 1], AF.Ln, [mv.buf], [rs.buf], bias=LN_EPS)
                act(rs.ap, rs.ap, AF.Exp, [rs.buf], [rs.buf], scale=-0.5)

            def D_ln(si):
                vg = vg2[si % 2]
                stt6, mv, rs = st2[si % 2]
                for bi in range(4):
                    stt("dve", vg[bi].ap, vg[bi].ap, mv.ap[:, bi, 0:1], lg.ap, ALU.subtract, ALU.mult,
                        [vg[bi].buf, mv.buf, lg.buf], [vg[bi].buf])
                    stt("dve", vn[bi].ap, vg[bi].ap, rs.ap[:, bi:bi + 1], lb.ap, ALU.mult, ALU.add,
                        [vg[bi].buf, rs.buf, lb.buf], [vn[bi].buf])

            def D_mix(si):
                c0, n = segs[si]
                ug, ug_b, vg = ug2[si % 2], ug_b2[si % 2], vg2[si % 2]
                for bi in range(4):
                    k = bi
                    t0, tn = (0, 128) if si < 4 else (126, 2)
                    for half in range(2):
                        ps, psb = next_bank()
                        for gg in range(4):
                            g = half * 4 + gg
                            mm(psb, ps[:, gg * tn:(gg + 1) * tn], vn[k].ap[:, (g // 2) * 128:(g // 2 + 1) * 128],
                               wsb.ap[:, g, t0:t0 + tn], True, True, [vn[k].buf, wsb.buf])
                        t1 = gt[half]
                        p0 = half * 2
                        psv = ps[:, :4 * tn].rearrange("p (a two b) -> p a two b", two=2, b=tn)
                        t1v = t1.ap[:, :2 * tn].rearrange("p (a b) -> p a b", a=2)
                        for par in range(2):
                            pr = par * 64
                            tt("dve", t1v[pr:pr + 64], psv[pr:pr + 64, :, par, :], bsp.ap[pr:pr + 64, p0:p0 + 2, t0:t0 + tn],
                               ALU.add, [psb, bsp.buf], [t1.buf])
                        tt("dve", yb.ap[:, p0:p0 + 2, c0 + bi * tn:c0 + (bi + 1) * tn], t1v,
                           ug.ap[:, p0:p0 + 2, bi * tn:(bi + 1) * tn], ALU.mult,
                           [t1.buf] + [ug_b[p0 + q] for q in range(2)], [yb_pb[p0 + q][si] for q in range(2)])

            D_proj(0)
            for si in range(len(segs)):
                D_ln(si)
                if si + 1 < len(segs):
                    D_proj(si + 1)
                D_mix(si)
            ug = ug2[0]
            vg = vg2[0] + vg2[1] + [ug2[1]]
            st6 = [t for tri in st2 for t in tri]
            A.release(wuv, xhb, lg, lb, bsp, wsf, tril, wsb, ug, *gt, *vg, *vn, *st6)
        if "yb" in DEBUG:
            dump("yb", yb.ap, [128, 4, TT], BF16, [b for r in yb_pb for b in r])

        mg = A.alloc("mg", [8, TT], BF16, tracked=False)
        mg_b = [[A.sub(mg, "mg%d_%d" % (fc, si), fc * TT + c0, fc * TT + c0 + n) for si, (c0, n) in enumerate(segs)]
                for fc in range(8)]
        if phases_upto >= 6:
            wo = A.alloc("wo", [8, D], BF16)
            dma("pool", wo.ap, w_out.rearrange("(c p) f -> p c f", p=128), [], [wo.buf])
            pb_out = load_const("b_out", rowp["b_out"], [D], F32)
            pg1 = load_const("ln1_g", rowp["ln1_g"], [D], F32)
            pb1 = load_const("ln1_b", rowp["ln1_b"], [D], F32)
        if phases_upto >= 5:
            ga = [A.alloc("ga%d" % k, [512], F32) for k in range(2)]
            gb = [A.alloc("gb%d" % k, [512], F32) for k in range(2)]
            it = 0
            for fc in range(8):
                k = fc % 2
                if fc + 1 < 8:
                    load_wgb(fc + 1)
                for si, (c0, n) in enumerate(segs):
                    kk = it % 2
                    it += 1
                    pga, pgb, pba, pbb = [next_bank() for _ in range(4)]
                    for kc in range(8):
                        mm(pga[1], pga[0][:, :n], wg[k].ap[:, 0, kc, :], xo.ap[:, kc, c0:c0 + n], kc == 0, kc == 7,
                           [wg[k].buf, xo.buf])
                    for kc in range(8):
                        mm(pgb[1], pgb[0][:, :n], wg[k].ap[:, 1, kc, :], xo.ap[:, kc, c0:c0 + n], kc == 0, kc == 7,
                           [wg[k].buf, xo.buf])
                    for hp in range(4):
                        mm(pba[1], pba[0][:, :n], wb[k].ap[:, 0, hp, :], ya.ap[:, hp, c0:c0 + n], hp == 0, hp == 3,
                           [wb[k].buf, ya_pb[hp][si]])
                    for hp in range(4):
                        mm(pbb[1], pbb[0][:, :n], wb[k].ap[:, 1, hp, :], yb.ap[:, hp, c0:c0 + n], hp == 0, hp == 3,
                           [wb[k].buf, yb_pb[hp][si]])
                    act(ga[kk].ap[:, :n], pga[0][:, :n], AF.Sigmoid, [pga[1], bgate.buf], [ga[kk].buf],
                        bias=bgate.ap[:, fc:fc + 1])
                    act(gb[kk].ap[:, :n], pgb[0][:, :n], AF.Sigmoid, [pgb[1], bgate.buf], [gb[kk].buf],
                        bias=bgate.ap[:, 8 + fc:9 + fc])
                    tt("dve", ga[kk].ap[:, :n], ga[kk].ap[:, :n], pba[0][:, :n], ALU.mult, [ga[kk].buf, pba[1]], [ga[kk].buf])
                    tt("dve", gb[kk].ap[:, :n], gb[kk].ap[:, :n], pbb[0][:, :n], ALU.mult, [gb[kk].buf, pbb[1]], [gb[kk].buf])
                    tt("dve", mg.ap[:, fc, c0:c0 + n], ga[kk].ap[:, :n], gb[kk].ap[:, :n], ALU.add,
                       [ga[kk].buf, gb[kk].buf], [mg_b[fc][si]])
            A.release(*wg, *wb, *ga, *gb)
        A.release(xo, ya, yb)
        if "mg" in DEBUG:
            dump("mg", mg.ap, [128, 8, TT], BF16, [b for r in mg_b for b in r])

        h1T = A.alloc("h1T", [8, TT], BF16, tracked=False, top=True)
        blocks = [(r * 128, 128) for r in range(16)] + [(T, NHALO)]
        h1T_b = [[A.sub(h1T, "h1T%d_%d" % (kc, r), kc * TT + r0, kc * TT + r0 + nr) for r, (r0, nr) in enumerate(blocks)]
                 for kc in range(8)]

        def ln_head(r_t, nr, st_t):
            for hlf in range(2):
                P.op("dve", lambda e, hlf=hlf: e.bn_stats(st_t.ap[:nr, hlf * 6:hlf * 6 + 6],
                                                         r_t.ap[:nr, hlf * 512:(hlf + 1) * 512]),
                     reads=[r_t.buf], writes=[st_t.buf])
            P.op("dve", lambda e: e.bn_aggr(st_t.ap[:nr, 12:14], st_t.ap[:nr, 0:12]), reads=[st_t.buf], writes=[st_t.buf])
            act(st_t.ap[:nr, 13:14], st_t.ap[:nr, 13:14], AF.Ln, [st_t.buf], [st_t.buf], bias=LN_EPS)
            act(st_t.ap[:nr, 13:14], st_t.ap[:nr, 13:14], AF.Exp, [st_t.buf], [st_t.buf], scale=-0.5)

        def ln_tail(r_t, nr, g_t, b_t, st_t, out_ap, out_bufs):
            stt("dve", r_t.ap[:nr], r_t.ap[:nr], st_t.ap[:nr, 12:13], g_t.ap[:nr], ALU.subtract, ALU.mult,
                [r_t.buf, st_t.buf, g_t.buf], [r_t.buf])
            stt("dve", out_ap, r_t.ap[:nr], st_t.ap[:nr, 13:14], b_t.ap[:nr], ALU.mult, ALU.add,
                [r_t.buf, st_t.buf, b_t.buf], out_bufs)

        if phases_upto >= 7:
            wu = [A.alloc("wu%d" % k, [2, 8, 256], BF16, top=True) for k in range(2)]
            w_up_r = w_up.rearrange("(c p) f -> p c f", p=128)

            def load_wu(jj):
                k = jj % 2
                dma("pool", wu[k].ap[:, 0], w_up_r[:, :, jj * 256:(jj + 1) * 256], [], [wu[k].buf])
                dma("pool", wu[k].ap[:, 1], w_up_r[:, :, DFF + jj * 256:DFF + (jj + 1) * 256], [], [wu[k].buf])

            load_wu(0)
        if phases_upto >= 6:
            NB3 = 3
            xb = [A.alloc("xb%d" % k, [D], F32) for k in range(NB3)]
            rr = [A.alloc("rr%d" % k, [D], F32) for k in range(NB3)]
            hh = [A.alloc("hh%d" % k, [D], F32) for k in range(2)]
            hb = [A.alloc("hb%d" % k, [D], BF16) for k in range(2)]
            stt_ = [A.alloc("stF%d" % k, [16], F32) for k in range(NB3)]

            ones1 = A.alloc("ones1", [128], BF16)
            bhi = A.alloc("bhi", [D], BF16)
            blo = A.alloc("blo", [D], BF16)
            P.op("dve", lambda e: e.memset(ones1.ap[0:1], 1.0), reads=[], writes=[ones1.buf], nfree=128)
            cp("dve", bhi.ap[0:1], pb_out.ap[0:1], [pb_out.buf], [bhi.buf])
            tt("dve", blo.ap[0:1], pb_out.ap[0:1], bhi.ap[0:1], ALU.subtract, [pb_out.buf, bhi.buf], [blo.buf])
            Fps = {}

            def F_mm(r):
                r0, nr = blocks[r]
                k = r % NB3
                dma("sp", xb[k].ap[:nr], x_own[r0:r0 + nr, :], [], [xb[k].buf])
                Fps[r] = []
                for hlf in range(2):
                    ps, psb = next_bank()
                    Fps[r].append((ps, psb))
                    for kc in range(8):
                        mm(psb, ps[:nr, :], mg.ap[:, kc, r0:r0 + nr], wo.ap[:, kc, hlf * 512:(hlf + 1) * 512],
                           kc == 0, False, [mg_b[kc][min(r // 4, 4)], wo.buf])
                    mm(psb, ps[:nr, :], ones1.ap[0:1, :nr], bhi.ap[0:1, hlf * 512:(hlf + 1) * 512], False, False,
                       [ones1.buf, bhi.buf])
                    mm(psb, ps[:nr, :], ones1.ap[0:1, :nr], blo.ap[0:1, hlf * 512:(hlf + 1) * 512], False, True,
                       [ones1.buf, blo.buf])

            def F_head(r):
                r0, nr = blocks[r]
                k = r % NB3
                for hlf in range(2):
                    ps, psb = Fps[r][hlf]
                    stt("dve", rr[k].ap[:nr, hlf * 512:(hlf + 1) * 512], xb[k].ap[:nr, hlf * 512:(hlf + 1) * 512], ALPHA,
                        ps[:nr, :], ALU.mult, ALU.add, [xb[k].buf, psb], [rr[k].buf])
                ln_head(rr[k], nr, stt_[k])

            def F_tail(r):
                r0, nr = blocks[r]
                k, k2 = r % NB3, r % 2
                ln_tail(rr[k], nr, pg1, pb1, stt_[k], hh[k2].ap[:nr], [hh[k2].buf])
                dma("sp", h1s[r0:r0 + nr, :], hh[k2].ap[:nr], [hh[k2].buf], [h1s_buf[r]])
                cp("act", hb[k2].ap[:nr], hh[k2].ap[:nr], [hh[k2].buf], [hb[k2].buf])

            def F_tr(r):
                r0, nr = blocks[r]
                k2 = r % 2
                for grp in range(2):
                    ps, psb = next_bank()
                    psv = ps.bitcast(BF16)
                    for q4 in range(4):
                        kc = grp * 4 + q4
                        P.op("pe", lambda e, kc=kc, q4=q4, psv=psv, k2=k2, nr=nr: e.transpose(
                            psv[:, q4 * 128:q4 * 128 + nr], hb[k2].ap[:nr, kc * 128:(kc + 1) * 128], ident.ap[:nr, :nr]),
                            reads=[hb[k2].buf, ident.buf], writes=[psb])
                    src_v = psv[:, 0:512].rearrange("p (a b) -> p a b", a=4)[:, :, :nr]
                    cp("act", h1T.ap[:, grp * 4:(grp + 1) * 4, r0:r0 + nr], src_v, [psb],
                                        [h1T_b[grp * 4 + q][r] for q in range(4)])

            nblk = len(blocks)
            F_mm(0)
            F_mm(1)
            F_head(0)
            for r in range(nblk):
                if r + 2 < nblk:
                    F_mm(r + 2)
                F_tail(r)
                if r + 1 < nblk:
                    F_head(r + 1)
                F_tr(r)
            A.release(wo, pb_out, pg1, pb1, *xb, *rr, *hh, *hb, *stt_, ones1, bhi, blo)
        A.release(mg)
        if "h1T" in DEBUG:
            dump("h1T", h1T.ap, [128, 8, TT], BF16, [b for r in h1T_b for b in r])

        actT = A.alloc("actT", [NCH, T], BF16, tracked=False)
        act_b = [[A.sub(actT, "act%d_%d" % (j, i), j * T + i * SL, j * T + (i + 1) * SL) for i in range(NSLOT)]
                 for j in range(NCH)]
        if phases_upto >= 8:
            wdA = A.alloc("wdA", [NCH // 2, D], BF16)
        if phases_upto >= 7:
            uh = A.alloc("uh", [44, NHALO], F32, tracked=False)
            bnd = A.alloc("bnd", [44, NHALO], F32, tracked=False)
            uh_b = [A.sub(uh, "uh%d" % c, c * NHALO, (c + 1) * NHALO) for c in range(44)]
            bnd_b = [A.sub(bnd, "bnd%d" % c, c * NHALO, (c + 1) * NHALO) for c in range(44)]
            cv = [[A.alloc("cv%d%d" % (s_, k), [SL], F32) for k in range(2)] for s_ in range(2)]
            sa = [A.alloc("sa%d" % k, [SL], F32) for k in range(2)]
            it = 0
            if phases_upto >= 8:
                dma("pool", wdA.ap, w_down.rearrange("(c p) f -> p c f", p=128)[:, 0:NCH // 2, :], [], [wdA.buf])
            for jj in range(NCH // 2):
                k = jj % 2
                if jj + 1 < NCH // 2:
                    load_wu(jj + 1)
                for sub in range(2):
                    j = 2 * jj + sub
                    for s_ in range(2):
                        c = s_ * NCH + j
                        ps, psb = next_bank()
                        for kc in range(8):
                            mm(psb, ps[:, :NHALO], wu[k].ap[:, s_, kc, sub * 128:(sub + 1) * 128],
                               h1T.ap[:, kc, T:TT], kc == 0, kc == 7, [wu[k].buf, h1T_b[kc][16]])
                        tt("dve", uh.ap[:, c, :], ps[:, :NHALO], flags.ap, ALU.mult, [psb, flags.buf], [uh_b[c]])
                        uv = uh.ap[:, c, :].rearrange("p (a b) -> p a b", b=2)
                        bv_ = bnd.ap[:, c, :].rearrange("p (a b) -> p a b", b=2)
                        ts("dve", bv_[:, :, 1:2], uv[:, :, 1:2], convp.ap[:, c:c + 1], None, ALU.mult, None,
                           [uh_b[c], convp.buf], [bnd_b[c]])
                        ts("dve", bv_[:, :, 0:1], uv[:, :, 1:2], convp.ap[:, 44 + c:45 + c], None, ALU.mult, None,
                           [uh_b[c], convp.buf], [bnd_b[c]])
                        stt("dve", bv_[:, :, 0:1], uv[:, :, 0:1], convp.ap[:, c:c + 1], bv_[:, :, 0:1], ALU.mult, ALU.add,
                            [uh_b[c], convp.buf, bnd_b[c]], [bnd_b[c]])
                for i in range(NSLOT):
                    for sub in range(2):
                        j = 2 * jj + sub
                        kk = it % 2
                        it += 1
                        for s_ in range(2):
                            c = s_ * NCH + j
                            ps, psb = next_bank()
                            for kc in range(8):
                                mm(psb, ps, wu[k].ap[:, s_, kc, sub * 128:(sub + 1) * 128],
                                   h1T.ap[:, kc, i * SL:(i + 1) * SL], kc == 0, kc == 7,
                                   [wu[k].buf] + [h1T_b[kc][4 * i + q] for q in range(4)])
                            c_ = cv[s_][kk]
                            act(c_.ap, ps, AF.Identity, [psb, convp.buf], [c_.buf],
                                bias=convp.ap[:, 132 + c:133 + c], scale=convp.ap[:, 88 + c:89 + c])
                            stt("dve", c_.ap[:, 1:SL], ps[:, 0:SL - 1], convp.ap[:, 44 + c:45 + c], c_.ap[:, 1:SL],
                                ALU.mult, ALU.add, [psb, convp.buf, c_.buf], [c_.buf])
                            stt("dve", c_.ap[:, 2:SL], ps[:, 0:SL - 2], convp.ap[:, c:c + 1], c_.ap[:, 2:SL],
                                ALU.mult, ALU.add, [psb, convp.buf, c_.buf], [c_.buf])
                            tt("dve", c_.ap[:, 0:2], c_.ap[:, 0:2], bnd.ap[:, c, 2 * i:2 * i + 2], ALU.add,
                               [c_.buf, bnd_b[c]], [c_.buf])
                        ca, cbv = cv[0][kk], cv[1][kk]
                        act(sa[kk].ap, ca.ap, AF.Silu, [ca.buf], [sa[kk].buf])
                        tt("pool", actT.ap[:, j, i * SL:(i + 1) * SL], sa[kk].ap, cbv.ap, ALU.mult, [sa[kk].buf, cbv.buf],
                           [act_b[j][i]])
            A.release(*wu, uh, bnd, *cv[0], *cv[1], *sa)
        A.release(h1T)
        if "act" in DEBUG:
            dump("act", actT.ap, [128, NCH, T], BF16, [b for r in act_b for b in r])

        if phases_upto >= 8:
            wdB = A.alloc("wdB", [NCH // 2, D], BF16)
            dma("pool", wdB.ap, w_down.rearrange("(c p) f -> p c f", p=128)[:, NCH // 2:NCH, :], [], [wdB.buf])
            pb_dn = load_const("b_down", rowp["b_down"], [D], F32)
            pg2 = load_const("ln2_g", rowp["ln2_g"], [D], F32)
            pb2 = load_const("ln2_b", rowp["ln2_b"], [D], F32)
            NB3 = 3
            hr = [A.alloc("hr%d" % k, [D], F32) for k in range(NB3)]
            rr = [A.alloc("rr%d" % k, [D], F32) for k in range(NB3)]
            oo = [A.alloc("oo%d" % k, [D], F32) for k in range(2)]
            stt_ = [A.alloc("stG%d" % k, [16], F32) for k in range(NB3)]

            def G_load(r):
                dma("sp", hr[r % NB3].ap, h1s[r * 128:(r + 1) * 128, :], [h1s_buf[r]], [hr[r % NB3].buf])

            Gps = {}

            def G_mm_a(r):
                r0 = r * 128
                Gps[r] = []
                for hlf in range(2):
                    ps, psb = next_bank()
                    Gps[r].append((ps, psb))
                    for j in range(NCH // 2):
                        mm(psb, ps, actT.ap[:, j, r0:r0 + 128], wdA.ap[:, j, hlf * 512:(hlf + 1) * 512],
                           j == 0, False, [act_b[j][r // 4], wdA.buf])

            def G_mm_b(r):
                k = r % NB3
                r0 = r * 128
                for hlf in range(2):
                    ps, psb = Gps[r][hlf]
                    for j in range(NCH // 2, NCH):
                        mm(psb, ps, actT.ap[:, j, r0:r0 + 128], wdB.ap[:, j - NCH // 2, hlf * 512:(hlf + 1) * 512],
                           False, j == NCH - 1, [act_b[j][r // 4], wdB.buf])
                    tt("dve", rr[k].ap[:, hlf * 512:(hlf + 1) * 512], ps, pb_dn.ap[:, hlf * 512:(hlf + 1) * 512],
                       ALU.add, [psb, pb_dn.buf], [rr[k].buf])
                stt("dve", rr[k].ap, hr[k].ap, ALPHA, rr[k].ap, ALU.mult, ALU.add, [hr[k].buf, rr[k].buf], [rr[k].buf])
                ln_head(rr[k], 128, stt_[k])

            def G_tail(r):
                k, k2 = r % NB3, r % 2
                ln_tail(rr[k], 128, pg2, pb2, stt_[k], oo[k2].ap, [oo[k2].buf])
                out_stores.append(dma("sp", out_d[r * 128:(r + 1) * 128, :], oo[k2].ap, [oo[k2].buf], [P.buf("out%d" % r)]))

            G_load(0)
            G_load(1)
            NPRE = 4
            for r in range(NPRE):
                G_mm_a(r)
            for r in range(16):
                if r + 2 < 16:
                    G_load(r + 2)
                if r >= NPRE:
                    G_mm_a(r)
                G_mm_b(r)
                if r >= 1:
                    G_tail(r - 1)
            G_tail(15)
        else:
            z_t = A.alloc("zt", [D], F32)
            P.op("dve", lambda e: e.memset(z_t.ap, 0.0), reads=[], writes=[z_t.buf])
            for r in range(16):
                out_stores.append(dma("sp", out_d[r * 128:(r + 1) * 128, :], z_t.ap, [z_t.buf], [P.buf("out%d" % r)]))

        fence_bufs = []
        fin = Op()
        fin.eng, fin.dma, fin.flag, fin.sig, fin.idx = "sp", False, False, None, P.n
        fin.deps = list(out_stores) + list(dbg_outs.values())
        fin.fn = lambda e: e.nop()
        P.ops["sp"].append(fin)

        P.emit(nc)
    return nc, list(dbg_outs.keys())


def _bf16(a):
    return np.asarray(a, dtype=np.float32).astype(ml_dtypes.bfloat16)


def make_in_maps(x, w_in, b_gate, ln_sg_g, ln_sg_b, w_spatial, b_spatial, w_branch_a, w_branch_b, w_out, b_out,
                 ln1_g, ln1_b, w_up, conv_w, conv_b, w_down, b_down, ln2_g, ln2_b):
    f = lambda a: np.ascontiguousarray(np.asarray(a, dtype=np.float32))
    x = f(x)
    common = {
        "w_in": f(w_in[0]), "w_bra": f(w_branch_a[0]), "w_brb": f(w_branch_b[0]), "w_out": f(w_out[0]),
        "w_up": f(w_up[0]), "w_down": f(w_down[0]),
        "bgate_c": f(np.asarray(b_gate[0]).reshape(16, 128).T),
        "lnsg_g": f(np.broadcast_to(np.asarray(ln_sg_g[0])[None, :], (128, 512))),
        "lnsg_b": f(np.broadcast_to(np.asarray(ln_sg_b[0])[None, :], (128, 512))),
        "wsT": f(np.transpose(np.asarray(w_spatial[0]), (2, 0, 1))),
        "bsp_c": f(np.repeat(np.asarray(b_spatial[0]).reshape(4, 2, 128).transpose(1, 0, 2), 64, axis=0)),
    }
    cw = np.asarray(conv_w[0])
    cb = np.asarray(conv_b[0])
    cols = [cw[0].reshape(44, 128).T, cw[1].reshape(44, 128).T, cw[2].reshape(44, 128).T, cb.reshape(44, 128).T]
    common["convp_c"] = f(np.concatenate(cols, axis=1))
    for k, v in (("b_out", b_out), ("ln1_g", ln1_g), ("ln1_b", ln1_b), ("b_down", b_down), ("ln2_g", ln2_g),
                 ("ln2_b", ln2_b)):
        common[k] = f(np.broadcast_to(np.asarray(v[0])[None, :], (128, D)))
    p = np.arange(128)
    common["ident_c"] = _bf16(np.eye(128))
    common["U_c"] = _bf16((p[:, None] >= p[None, :]))
    common["L_c"] = _bf16((p[:, None] < p[None, :]))
    common["trilT"] = f((p[:, None] <= p[None, :]))
    in_maps = []
    for c in range(8):
        b, j = c // 2, c % 2
        starts = [1024 * i + 512 * j for i in range(NSLOT)]
        own = np.concatenate([np.arange(s0, s0 + SL) for s0 in starts])
        halo, hvalid, hb = [], [], []
        for s0 in starts:
            if s0 >= 2:
                halo += [s0 - 2, s0 - 1]
                hvalid += [1.0, 1.0]
                hb.append(np.arange(s0 - 128, s0))
            else:
                halo += [0, 1]
                hvalid += [0.0, 0.0]
                hb.append(np.arange(0, 128))
        halo = np.array(halo)
        cols_all = np.concatenate([own, halo])
        xb = x[b]
        m = dict(common)
        m["xT_all"] = np.ascontiguousarray(xb.T)
        m["xT_own"] = np.ascontiguousarray(xb[cols_all].T)
        m["xT_hb"] = np.ascontiguousarray(xb[np.concatenate(hb)].T)
        m["x_own"] = np.ascontiguousarray(xb[cols_all])
        m["flags_c"] = f(np.broadcast_to(np.array(hvalid, dtype=np.float32)[None, :], (128, NHALO)))
        r = np.arange(8)
        cq = np.arange(512)
        mm_ = (128 * r[None, :, None] + p[:, None, None]) < (512 * j + cq[None, None, :])
        m["maskM_c"] = _bf16(mm_)
        kb = np.arange(NKB_H)
        tq = np.where(np.array(hvalid) > 0, halo, -1)
        mh = (128 * kb[None, :, None] + p[:, None, None]) < tq[None, None, :]
        mh = np.broadcast_to(mh[:, :, None, :], (128, NKB_H, 8, NHALO)).reshape(128, NKB_H, 64)
        m["maskH_c"] = _bf16(mh)
        in_maps.append(m)
    return in_maps


_CACHE = {}


def kernel(**inputs):
    in_maps = make_in_maps(**inputs)
    if "nc" not in _CACHE:
        _CACHE["nc"] = build_program()
    nc, dbg = _CACHE["nc"]
    res = run_bass_kernel_spmd(nc, in_maps, core_ids=list(range(8)))
    out = np.zeros((4, S, D), dtype=np.float32)
    for c in range(8):
        b, j = c // 2, c % 2
        o = np.asarray(res.results[c]["out"], dtype=np.float32)
        for i in range(NSLOT):
            s0 = 1024 * i + 512 * j
            out[b, s0:s0 + SL] = o[i * SL:(i + 1) * SL]
    return out
```

```python
import numpy as np
import ml_dtypes
import concourse.bass as bass
import concourse.mybir as mybir
from concourse.bass_utils import run_bass_kernel_spmd

F32 = mybir.dt.float32
BF16 = mybir.dt.bfloat16
AF = mybir.ActivationFunctionType
ALU = mybir.AluOpType

D = 1024
S = 4096
NSLOT = 4
SL = 512
T = NSLOT * SL
NHALO = 8
TT = T + NHALO
DFF = 2816
NCH = DFF // 128
ALPHA = 2.0 ** 0.25
LN_EPS = 1e-5
GELU_C = 1.5957691216057308
NKB_H = 28
ARENA_BYTES = 190 * 1024

DEBUG = {}
EMBED_WAITS = True


class Buf:
    __slots__ = ("name", "lo", "hi", "lw", "rd", "al", "inherit", "excl")

    def __init__(self, name, lo, hi):
        self.name, self.lo, self.hi = name, lo, hi
        self.lw = None
        self.rd = {}
        self.al = []
        self.inherit = None
        self.excl = False


class Op:
    __slots__ = ("eng", "fn", "deps", "sig", "flag", "dma", "idx", "nfree", "pe_embed")


class Prog:
    ENGS = ("pe", "act", "dve", "pool", "sp")

    def __init__(self):
        self.ops = {e: [] for e in self.ENGS}
        self.abufs = []
        self.dead = []
        self.n = 0
        self.pe_embed = False

    def buf(self, name, lo=None, hi=None):
        b = Buf(name, lo, hi)
        if lo is not None:
            inh = set()
            for d in self.dead:
                if d.lo < hi and lo < d.hi:
                    if d.lw is not None:
                        inh.add(d.lw)
                    inh.update(d.rd.values())
            b.inherit = inh or None
            for o in self.abufs:
                if o.lo < hi and lo < o.hi:
                    o.al.append(b)
                    b.al.append(o)
            self.abufs.append(b)
        return b

    def kill(self, b):
        if b.lo is None:
            return
        self.abufs.remove(b)
        for o in b.al:
            o.al.remove(b)
        b.al = []
        self.dead.append(b)

    def op(self, eng, fn, reads=(), writes=(), dma=False, nfree=0):
        o = Op()
        o.eng, o.fn, o.dma, o.flag, o.sig = eng, fn, dma, False, None
        o.nfree = nfree
        o.pe_embed = self.pe_embed
        o.idx = self.n
        self.n += 1
        raw, oth = set(), set()
        for b in list(reads) + list(writes):
            if b.inherit:
                oth |= b.inherit
                b.inherit = None
        for b in reads:
            for bb in [b] + b.al:
                if bb.lw is not None:
                    raw.add(bb.lw)
            if b.excl:
                for r in b.rd.values():
                    if r.eng != eng:
                        oth.add(r)
        for b in writes:
            for bb in [b] + b.al:
                if bb.lw is not None:
                    oth.add(bb.lw)
                for r in bb.rd.values():
                    oth.add(r)
        deps = []
        for d in raw | oth:
            if d is o:
                continue
            if d.dma or o.dma:
                deps.append(d)
            elif d.eng != eng:
                deps.append(d)
            elif eng == "pool":
                deps.append(d)
            elif eng != "pe" and d in raw and d.nfree < 256:
                deps.append(d)
        best = {}
        out = []
        for d in deps:
            if d.dma:
                out.append(d)
            else:
                if d.eng not in best or best[d.eng].idx < d.idx:
                    best[d.eng] = d
        out.extend(best.values())
        o.deps = out
        for d in out:
            d.flag = True
        for b in reads:
            key = ("dma", o.idx) if dma else eng
            b.rd[key] = o
        for b in writes:
            for bb in [b] + b.al:
                bb.lw = o
                bb.rd = {}
        self.ops[eng].append(o)
        return o

    def emit(self, nc):
        NDS = 12
        sems = {}
        with nc.Block() as block:
            import contextlib
            with contextlib.ExitStack() as st:
                esem = {e: st.enter_context(nc.semaphore("s_" + e)) for e in self.ENGS}
                dsem = {e: [st.enter_context(nc.semaphore("d_%s%d" % (e, k))) for k in range(NDS)]
                        for e in ("pool", "sp")}
                for e in self.ENGS:
                    cnt = 0
                    dcnt = [0] * NDS
                    k = 0
                    for o in self.ops[e]:
                        if o.dma:
                            dcnt[k] += 16
                            o.sig = (dsem[e][k], dcnt[k])
                            k = (k + 1) % NDS
                        elif o.flag:
                            cnt += 1
                            o.sig = (esem[e], cnt)

                def run(engname, eng):
                    waited = {}
                    for o in self.ops[engname]:
                        need = {}
                        for d in o.deps:
                            sem, val = d.sig
                            if waited.get(id(sem), 0) < val and need.get(id(sem), (None, 0))[1] < val:
                                need[id(sem)] = (sem, val)
                        if o.dma:
                            sem, val = o.sig
                            if val > 16 and waited.get(id(sem), 0) < val - 16 and need.get(id(sem), (None, 0))[1] < val - 16:
                                need[id(sem)] = (sem, val - 16)
                        need = list(need.values())
                        embed = None
                        if need and not o.dma and EMBED_WAITS and (engname != "pe" or getattr(o, "pe_embed", False)):
                            embed = need.pop()
                        for sem, val in need:
                            eng.wait_ge(sem, val)
                            waited[id(sem)] = val
                        n0 = nc.n_instructions() if embed is not None else 0
                        ins = o.fn(eng)
                        if embed is not None:
                            sem, val = embed
                            if nc.n_instructions() - n0 == 1:
                                ins._wait_ge(sem, val)
                            else:
                                raise RuntimeError("multi-instruction op cannot carry an embedded wait")
                            waited[id(sem)] = val
                        if o.dma:
                            ins.then_inc(o.sig[0], 16)
                        elif o.flag:
                            ins.then_inc(o.sig[0], 1)

                @block.tensor
                def _(e):
                    run("pe", e)

                @block.scalar
                def _(e):
                    run("act", e)

                @block.vector
                def _(e):
                    run("dve", e)

                @block.gpsimd
                def _(e):
                    run("pool", e)

                @block.sync
                def _(e):
                    run("sp", e)


class Tile:
    def __init__(self, ap, buf, off, nbytes):
        self.ap, self.buf, self.off, self.nbytes = ap, buf, off, nbytes


class Arena:
    def __init__(self, P, arena_ap, nbytes):
        self.P, self.a, self.nbytes = P, arena_ap, nbytes
        self.free = [(0, nbytes)]
        self.live = {}

    def alloc(self, name, shape, dtype, tracked=True, top=False):
        esz = 4 if dtype == F32 else 2
        n = 1
        for s in shape:
            n *= s
        nb = (n * esz + 63) // 64 * 64
        order = list(enumerate(self.free))
        if top:
            order = order[::-1]
        for k, (lo, hi) in order:
            if hi - lo >= nb:
                off = hi - nb if top else lo
                if hi - lo == nb:
                    self.free.pop(k)
                elif top:
                    self.free[k] = (lo, hi - nb)
                else:
                    self.free[k] = (lo + nb, hi)
                break
        else:
            raise RuntimeError("arena OOM for %s (%d B); free=%s" % (name, nb, self.free))
        ap = self.a[:, off // 2: off // 2 + n * esz // 2]
        if dtype == F32:
            ap = ap.bitcast(F32)
        if len(shape) == 2:
            ap = ap.rearrange("p (a b) -> p a b", a=shape[0])
        elif len(shape) == 3:
            ap = ap.rearrange("p (a b c) -> p a b c", a=shape[0], b=shape[1])
        t = Tile(ap, self.P.buf(name, off, off + nb) if tracked else None, off, nb)
        t.esz = esz
        t.name = name
        t.subs = []
        self.live[name] = t
        return t

    def sub(self, t, name, elem_lo, elem_hi):
        assert t.buf is None, "tiles with sub-buffers must be allocated with tracked=False"
        b = self.P.buf(name, t.off + elem_lo * t.esz, t.off + elem_hi * t.esz)
        t.subs.append(b)
        return b

    def release(self, *tiles):
        for t in tiles:
            del self.live[t.name]
            self.free.append((t.off, t.off + t.nbytes))
            if t.buf is not None:
                self.P.kill(t.buf)
            for b in t.subs:
                self.P.kill(b)
        self.free.sort()
        merged = []
        for lo, hi in self.free:
            if merged and merged[-1][1] == lo:
                merged[-1] = (merged[-1][0], hi)
            else:
                merged.append((lo, hi))
        self.free = merged


def build_program(phases_upto=99):
    nc = bass.Bass("TRN2", target_bir_lowering=False)
    P = Prog()

    def din(name, shape, dt=F32):
        return nc.dram_tensor(name, list(shape), dt, kind="ExternalInput").ap()

    xT_all = din("xT_all", [D, S])
    xT_own = din("xT_own", [D, TT])
    xT_hb = din("xT_hb", [D, 512])
    x_own = din("x_own", [TT, D])
    w_in = din("w_in", [D, 4608])
    w_bra = din("w_bra", [512, D])
    w_brb = din("w_brb", [512, D])
    w_out = din("w_out", [D, D])
    w_up = din("w_up", [D, 2 * DFF])
    w_down = din("w_down", [DFF, D])
    bgate_c = din("bgate_c", [128, 16])
    convp_c = din("convp_c", [128, 4 * 44])
    flags_c = din("flags_c", [128, NHALO])
    lnsg_g = din("lnsg_g", [128, 512])
    lnsg_b = din("lnsg_b", [128, 512])
    wsT = din("wsT", [128, 8, 128])
    trilT = din("trilT", [128, 128])
    bsp_c = din("bsp_c", [128, 4, 128])
    rowp = {k: din(k, [128, D]) for k in ("b_out", "ln1_g", "ln1_b", "b_down", "ln2_g", "ln2_b")}
    ident_c = din("ident_c", [128, 128], BF16)
    U_c = din("U_c", [128, 128], BF16)
    L_c = din("L_c", [128, 128], BF16)
    maskM_c = din("maskM_c", [128, 8, 512], BF16)
    maskH_c = din("maskH_c", [128, NKB_H, 64], BF16)
    out_d = nc.dram_tensor("out", [T, D], F32, kind="ExternalOutput").ap()
    h1s = nc.dram_tensor("h1s", [TT, D], F32).ap()
    h1s_buf = [P.buf("h1s%d" % r) for r in range(17)]
    dbg_outs = {}

    import contextlib
    with contextlib.ExitStack() as st:
        arena_t = st.enter_context(nc.sbuf_tensor("arena", [128, ARENA_BYTES // 2], BF16))
        A = Arena(P, arena_t[:], ARENA_BYTES)
        psum = []
        psall_t = st.enter_context(nc.psum_tensor("psall", [128, 4096], F32))
        psall = psall_t[:]
        for k in range(8):
            psum.append((psall[:, k * 512:(k + 1) * 512], P.buf("ps%d" % k)))
            psum[-1][1].excl = True

        def nfree_of(ap):
            n = 1
            for s in ap.shape[1:]:
                n *= s
            return n

        def mm(psb, out, lhsT, rhs, start, stop, reads):
            P.op("pe", lambda e: e.matmul(out, lhsT, rhs, start=start, stop=stop),
                 reads=reads, writes=[psb])

        def act(out, in_, func, reads, writes, bias=None, scale=None):
            kw = {}
            if bias is not None:
                kw["bias"] = bias
            if scale is not None:
                kw["scale"] = scale
            P.op("act", lambda e: e.activation(out, in_, func, **kw), reads=reads, writes=writes, nfree=nfree_of(out))

        def tt(eng, out, in0, in1, op, reads, writes):
            P.op(eng, lambda e: e.tensor_tensor(out, in0, in1, op), reads=reads, writes=writes, nfree=nfree_of(out))

        def ts(eng, out, in0, s1, s2, op0, op1, reads, writes):
            if s2 is None:
                P.op(eng, lambda e: e.tensor_scalar(out, in0, s1, None, op0), reads=reads, writes=writes, nfree=nfree_of(out))
            else:
                P.op(eng, lambda e: e.tensor_scalar(out, in0, s1, s2, op0, op1), reads=reads, writes=writes, nfree=nfree_of(out))

        def stt(eng, out, in0, scalar, in1, op0, op1, reads, writes):
            P.op(eng, lambda e: e.scalar_tensor_tensor(out, in0, scalar, in1, op0, op1),
                 reads=reads, writes=writes, nfree=nfree_of(out))

        def cp(eng, out, in_, reads, writes):
            if eng == "act":
                P.op("act", lambda e: e.copy(out, in_), reads=reads, writes=writes, nfree=nfree_of(out))
            else:
                P.op(eng, lambda e: e.tensor_copy(out, in_), reads=reads, writes=writes, nfree=nfree_of(out))

        def dma(eng, out, in_, reads, writes):
            return P.op(eng, lambda e: e.dma_start(out=out, in_=in_), reads=reads, writes=writes, dma=True)

        def dump(name, tile_ap, shape, dtype, reads):
            d = nc.dram_tensor("dbg_" + name, list(shape), dtype, kind="ExternalOutput").ap()
            dbg_outs[name] = dma("sp", d, tile_ap, reads, [P.buf("dbg_" + name)])

        def load_const(name, src, shape, dtype, eng="sp", parts=128):
            t = A.alloc(name, shape, dtype)
            dma(eng, t.ap[0:parts] if parts != 128 else t.ap, src, [], [t.buf])
            return t

        evac_rr = [0]

        def evac(out, in_, reads, writes):
            evac_rr[0] ^= 1
            cp("dve" if evac_rr[0] else "act", out, in_, reads, writes)

        pbank = [0]

        def next_bank():
            pbank[0] = (pbank[0] + 1) % 8
            return psum[pbank[0]]

        segs = [(i * SL, SL) for i in range(NSLOT)] + [(T, NHALO)]
        out_stores = []

        ident = A.alloc("ident", [128], BF16)
        Umat = A.alloc("U", [128], BF16)
        Lmat = A.alloc("L", [128], BF16)
        maskM = A.alloc("maskM", [8, 512], BF16)
        maskH = A.alloc("maskH", [NKB_H, 64], BF16)
        bgate = A.alloc("bgate", [16], F32)
        convp = A.alloc("convp", [4 * 44], F32)
        flags = A.alloc("flags", [NHALO], F32)

        def load_persistent_consts():
            for t_, s_ in ((ident, ident_c), (Umat, U_c), (Lmat, L_c), (maskM, maskM_c), (maskH, maskH_c),
                           (bgate, bgate_c), (convp, convp_c), (flags, flags_c)):
                dma("sp", t_.ap, s_, [], [t_.buf])

        kT = A.alloc("kT", [4, S], BF16, tracked=False)
        Vt = A.alloc("V", [32, 512], BF16, tracked=False)
        kT_b = [[A.sub(kT, "kT%d_%d" % (fc, t8), fc * S + t8 * 512, fc * S + t8 * 512 + 512)
                 for t8 in range(8)] for fc in range(4)]
        V_b = [A.sub(Vt, "V%d" % k, k * 512, k * 512 + 512) for k in range(32)]
        wk = A.alloc("wk", [8, 512], BF16)
        wv = A.alloc("wv", [8, 512], BF16)
        xa = [A.alloc("xa%d" % k, [8, 512], BF16) for k in range(2)]
        xTa = xT_all.rearrange("(c p) t -> p c t", p=128)
        w_in_r0 = w_in.rearrange("(c p) f -> p c f", p=128)
        dma("pool", wk.ap, w_in_r0[:, :, 512:1024], [], [wk.buf])
        xo = A.alloc("xo", [8, TT], BF16)
        wq = A.alloc("wq", [8, 512], BF16)
        NFRONT = 6 if phases_upto >= 3 else 8
        for t8 in range(NFRONT if phases_upto >= 1 else 0):
            x_ = xa[t8 % 2]
            dma("pool", x_.ap, xTa[:, :, t8 * 512:(t8 + 1) * 512], [], [x_.buf])
            if t8 == 0:
                dma("pool", wv.ap, w_in_r0[:, :, 1024:1536], [], [wv.buf])
            if t8 == 1:
                load_persistent_consts()
            xTo = xT_own.rearrange("(c p) t -> p c t", p=128)
            if 2 <= t8 <= 5:
                s_ = t8 - 2
                dma("pool", xo.ap[:, :, s_ * SL:(s_ + 1) * SL], xTo[:, :, s_ * SL:(s_ + 1) * SL], [], [xo.buf])
            if t8 == 5:
                dma("pool", xo.ap[:, :, T:TT], xTo[:, :, T:TT], [], [xo.buf])
                dma("pool", wq.ap, w_in_r0[:, :, 0:512], [], [wq.buf])
            for fc in range(4):
                ps, psb = next_bank()
                for kc in range(8):
                    mm(psb, ps, wk.ap[:, kc, fc * 128:(fc + 1) * 128], x_.ap[:, kc, :],
                       kc == 0, kc == 7, [wk.buf, x_.buf])
                evac(kT.ap[:, fc, t8 * 512:(t8 + 1) * 512], ps, [psb], [kT_b[fc][t8]])
            for tb in range(4):
                ps, psb = next_bank()
                for kc in range(8):
                    mm(psb, ps, x_.ap[:, kc, tb * 128:(tb + 1) * 128], wv.ap[:, kc, :],
                       kc == 0, kc == 7, [wv.buf, x_.buf])
                evac(Vt.ap[:, t8 * 4 + tb, :], ps, [psb], [V_b[t8 * 4 + tb]])
        A.release(wk, wv, xa[0], xa[1])
        if "kT" in DEBUG:
            dump("kT", kT.ap, [128, 4, S], BF16, [b for r in kT_b for b in r])
            dump("V", Vt.ap, [128, 32, 512], BF16, V_b)

        qT = A.alloc("qT", [4, TT], BF16, tracked=False)
        qT_b = [[A.sub(qT, "q%d_%d" % (fc, si), fc * TT + c0, fc * TT + c0 + n) for si, (c0, n) in enumerate(segs)]
                for fc in range(4)]
        side_jobs = []
        side_bank = [0]

        def q_group_jobs(si, fc, bank=None):
            c0, n = segs[si]
            holder = {}

            def job(kc):
                if kc == 0:
                    holder["b"] = psum[2 + side_bank[0]] if bank is None else bank
                    side_bank[0] ^= 1
                ps, psb = holder["b"]
                mm(psb, ps[:, :n], wq.ap[:, kc, fc * 128:(fc + 1) * 128], xo.ap[:, kc, c0:c0 + n],
                   kc == 0, kc == 7, [wq.buf, xo.buf])
                if kc == 7:
                    ts("dve", qT.ap[:, fc, c0:c0 + n], ps[:, :n], 0.125, None, ALU.mult, None, [psb], [qT_b[fc][si]])
            return [lambda p=p: job(p) for p in range(8)]

        if phases_upto >= 2:
            for fc in range(4):
                for j_ in q_group_jobs(0, fc, bank=next_bank()):
                    j_()
            late_segs = [1, 2, 3, 4] if phases_upto >= 3 else []
            for si in late_segs:
                for fc in range(4):
                    side_jobs.extend(q_group_jobs(si, fc))
            late = {}

            def kv_switch():
                A.release(xo, wq)
                late["wk"] = A.alloc("wk2", [8, 512], BF16)
                late["wv"] = A.alloc("wv2", [8, 512], BF16)
                late["xa"] = [A.alloc("xa2_%d" % k, [8, 512], BF16) for k in range(2)]
                dma("pool", late["wk"].ap, w_in_r0[:, :, 512:1024], [], [late["wk"].buf])
                dma("pool", late["xa"][0].ap, xTa[:, :, 6 * 512:7 * 512], [], [late["xa"][0].buf])
                dma("pool", late["wv"].ap, w_in_r0[:, :, 1024:1536], [], [late["wv"].buf])
                dma("pool", late["xa"][1].ap, xTa[:, :, 7 * 512:8 * 512], [], [late["xa"][1].buf])

            def kv_group_jobs(t8, grp):
                holder = {}

                def job(kc):
                    x_ = late["xa"][t8 % 2]
                    if kc == 0:
                        holder["b"] = psum[2 + side_bank[0]]
                        side_bank[0] ^= 1
                    ps, psb = holder["b"]
                    if grp < 4:
                        fc = grp
                        mm(psb, ps, late["wk"].ap[:, kc, fc * 128:(fc + 1) * 128], x_.ap[:, kc, :],
                           kc == 0, kc == 7, [late["wk"].buf, x_.buf])
                        if kc == 7:
                            cp("dve", kT.ap[:, fc, t8 * 512:(t8 + 1) * 512], ps, [psb], [kT_b[fc][t8]])
                    else:
                        tb = grp - 4
                        mm(psb, ps, x_.ap[:, kc, tb * 128:(tb + 1) * 128], late["wv"].ap[:, kc, :],
                           kc == 0, kc == 7, [late["wv"].buf, x_.buf])
                        if kc == 7:
                            cp("dve", Vt.ap[:, t8 * 4 + tb, :], ps, [psb], [V_b[t8 * 4 + tb]])
                return [lambda p=p: job(p) for p in range(8)]

            def kv_finish():
                A.release(late["wk"], late["wv"], late["xa"][0], late["xa"][1])
                late["xo"] = A.alloc("xo_b", [8, TT], BF16)
                dma("pool", late["xo"].ap, xT_own.rearrange("(c p) t -> p c t", p=128), [], [late["xo"].buf])

            if late_segs and NFRONT < 8:
                side_jobs.append(kv_switch)
                side_jobs.extend([(lambda: None)] * 24)
                for t8_ in range(NFRONT, 8):
                    for grp in range(8):
                        side_jobs.extend(kv_group_jobs(t8_, grp))
                side_jobs.append(kv_finish)
        if phases_upto < 3:
            A.release(wq)
        if "qT" in DEBUG:
            dump("qT", qT.ap, [128, 4, TT], BF16, [b for r in qT_b for b in r])

        ya = A.alloc("ya", [4, TT], BF16, tracked=False)
        ya_pb = [[A.sub(ya, "ya%d_%d" % (hp, si), hp * TT + c0, hp * TT + c0 + n) for si, (c0, n) in enumerate(segs)]
                 for hp in range(4)]
        ya_b = [ya_pb[h // 2] for h in range(8)]
        NE = 4
        e3 = [A.alloc("e3_%d" % k, [1024], F32) for k in range(NE)]
        qpad = [[A.alloc("qpad%d%d" % (b_, par), [SL], BF16) for par in range(2)] for b_ in range(2)]
        qh = A.alloc("qh", [8, NHALO], BF16)
        for b_ in range(2):
            for par in range(2):
                P.op("pool", lambda e, t=qpad[b_][par]: e.memset(t.ap, 0.0), reads=[], writes=[qpad[b_][par].buf], nfree=SL)
        P.op("pool", lambda e: e.memset(qh.ap, 0.0), reads=[], writes=[qh.buf], nfree=64)
        Zmat = A.alloc("Zmat", [128], BF16)
        P.op("pool", lambda e: e.memset(Zmat.ap, 0.0), reads=[], writes=[Zmat.buf], nfree=128)
        sp2 = [A.alloc("sp2_%d" % k, [1024], BF16) for k in range(2)]
        ex2 = [A.alloc("ex2_%d" % k, [1024], F32) for k in range(2)]
        w2 = [A.alloc("w2_%d" % k, [1024], BF16) for k in range(2)]

        def attn_stream(steps):
            N = len(steps)
            ZB = 1

            def nch(i):
                return len(steps[i]["chains"])

            def TW(i):
                return 1024 if nch(i) == 2 else steps[i]["chains"][0]["W"]

            def CS(i):
                return steps[i].get("cs", 0)

            def vw(ap, i):
                if nch(i) == 1:
                    return ap[:, :TW(i)]
                cs = CS(i)
                if cs == 0:
                    return ap[:, :1024]
                return ap[:, :1024].rearrange("p (a b) -> p a b", a=2)[:, :, cs:]

            def Z(i):
                s = steps[i]
                cs = CS(i)
                if s.get("pre") is not None:
                    s["pre"]()
                for ci, ch in enumerate(s["chains"]):
                    ps, psb = psum[2 * (i % ZB) + ci]
                    for (h, q_ap, q_buf, n, col0) in ch["groups"]:
                        mm(psb, ps[:, col0 + cs:col0 + n], kT.ap[:, h // 2, s["kb"] * 128:(s["kb"] + 1) * 128],
                           q_ap[:, cs:n], True, True, [kT_b[h // 2][s["kb"] // 4], q_buf])

            def EXP(i):
                p = i % ZB
                act(vw(e3[i % NE].ap, i), vw(psall[:, 1024 * p:1024 * p + 1024], i), AF.Exp,
                    [psum[2 * p + ci][1] for ci in range(nch(i))], [e3[i % NE].buf])

            def MASK(i):
                m = steps[i]["mask"]
                if m is None:
                    return
                cs = CS(i)
                for ci, ch in enumerate(steps[i]["chains"]):
                    W = ch["W"]
                    ev = e3[i % NE].ap[:, ci * 512 + cs:ci * 512 + W]
                    tt("dve", ev, ev, m[0][:, cs:W], ALU.mult, [e3[i % NE].buf, m[1]], [e3[i % NE].buf])

            def LN(i):
                act(vw(sp2[i % 2].ap, i), vw(e3[i % NE].ap, i), AF.Ln, [e3[i % NE].buf], [sp2[i % 2].buf], bias=1.0)

            def ZERO(i, base):
                s = steps[i]
                for ci, ch in enumerate(s["chains"]):
                    W = ch["W"]
                    pz, pzb = psum[base + ci]
                    mm(pzb, pz[:, :W], Zmat.ap, Umat.ap[:, :W] if W <= 128 else maskM.ap[:, 0, :W], True, False,
                       [Zmat.buf, Umat.buf, maskM.buf])

            def U(i):
                s = steps[i]
                cs = CS(i)
                if s["first"]:
                    ZERO(i, 4)
                for ci, ch in enumerate(s["chains"]):
                    W = ch["W"]
                    pc, pcb = psum[4 + ci]
                    mm(pcb, pc[:, cs:W], Umat.ap, sp2[i % 2].ap[:, ci * 512 + cs:ci * 512 + W], False, s["last"],
                       [Umat.buf, sp2[i % 2].buf])

            def EXPC(i):
                act(vw(ex2[i % 2].ap, i), vw(psall[:, 2048:3072], i), AF.Exp,
                    [psum[4 + ci][1] for ci in range(nch(i))], [ex2[i % 2].buf], scale=-1.0)

            def L(i):
                s = steps[i]
                cs = CS(i)
                if s["last"]:
                    return
                for ci, ch in enumerate(s["chains"]):
                    W = ch["W"]
                    pc, pcb = psum[4 + ci]
                    mm(pcb, pc[:, cs:W], Lmat.ap, sp2[i % 2].ap[:, ci * 512 + cs:ci * 512 + W], False, False,
                       [Lmat.buf, sp2[i % 2].buf])

            def WW(i):
                tt("dve", vw(w2[i % 2].ap, i), vw(ex2[i % 2].ap, i), vw(e3[i % NE].ap, i), ALU.mult,
                   [ex2[i % 2].buf, e3[i % NE].buf], [w2[i % 2].buf])

            def PV(i):
                s = steps[i]
                cs = CS(i)
                if s["first"]:
                    ZERO(i, 6)
                for ci, ch in enumerate(s["chains"]):
                    po, pob = psum[6 + ci]
                    for (h, q_ap, q_buf, n, col0) in ch["groups"]:
                        mm(pob, po[:, col0 + cs:col0 + n], Vt.ap[:, s["kb"], (h // 2) * 128:(h // 2 + 1) * 128],
                           w2[i % 2].ap[:, ci * 512 + col0 + cs:ci * 512 + col0 + n], False, s["last"],
                           [V_b[s["kb"]], w2[i % 2].buf])
                if s["last"]:
                    for ci, ch in enumerate(s["chains"]):
                        po, pob = psum[6 + ci]
                        for (h, si, c0, n, col0) in ch["out"]:
                            pr = (h % 2) * 64
                            cp("dve", ya.ap[pr:pr + 64, h // 2, c0:c0 + n], po[pr:pr + 64, col0:col0 + n], [pob], [ya_b[h][si]])

            Z(0)
            EXP(0)
            MASK(0)
            Z(1)
            EXP(1)
            MASK(1)
            Z(2)
            LN(0)
            U(0)
            for i in range(N):
                if i + 2 < N:
                    EXP(i + 2)
                if i + 3 < N:
                    Z(i + 3)
                if side_jobs:
                    side_jobs.pop(0)()
                if i + 2 < N:
                    MASK(i + 2)
                EXPC(i)
                WW(i)
                L(i)
                if side_jobs:
                    side_jobs.pop(0)()
                if i + 1 < N:
                    LN(i + 1)
                    U(i + 1)
                PV(i)

        if phases_upto >= 3:
            steps = []
            gidx = 0
            for i in range(NSLOT):
                nkb = 8 * (i + 1)
                for hp in range(4):
                    b_ = gidx % 2
                    gidx += 1
                    chains = []
                    for par in range(2):
                        h = 2 * hp + par
                        chains.append(dict(groups=[(h, qpad[b_][par].ap, qpad[b_][par].buf, SL, 0)], W=SL,
                                           out=[(h, i, i * SL, SL, 0)]))

                    def pre(i=i, hp=hp, b_=b_):
                        for par in range(2):
                            pr = par * 64
                            cp("dve", qpad[b_][par].ap[pr:pr + 64, :], qT.ap[pr:pr + 64, hp, i * SL:(i + 1) * SL],
                               [qT_b[hp][i]], [qpad[b_][par].buf])

                    for kb in range(nkb - 1, -1, -1):
                        steps.append(dict(chains=chains, kb=kb, first=(kb == nkb - 1), last=(kb == 0),
                                          pre=(pre if kb == nkb - 1 else None),
                                          cs=(128 * max(0, kb - (nkb - 8) - 4) if not DEBUG.get("notrim") else 0),
                                          mask=((maskM.ap[:, kb - (nkb - 8), :], maskM.buf) if kb >= nkb - 8 else None)))
            halo_chain = dict(groups=[(h, qh.ap[:, h, :], qh.buf, NHALO, h * NHALO) for h in range(8)], W=64,
                              out=[(h, 4, T, NHALO, h * NHALO) for h in range(8)])

            def pre_h():
                for par in range(2):
                    pr = par * 64
                    cp("dve", qh.ap[pr:pr + 64].rearrange("p (a two) b -> p a two b", two=2)[:, :, par, :],
                       qT.ap[pr:pr + 64, :, T:TT], [qT_b[fc][4] for fc in range(4)], [qh.buf])

            early = {}

            def post_q():
                A.release(qT)
                early["qT_released"] = True
                if phases_upto >= 4:
                    early["wuv"] = A.alloc("wuv", [8, 1024], BF16)
                    dma("pool", early["wuv"].ap, w_in.rearrange("(c p) f -> p c f", p=128)[:, :, 1536:2560], [],
                        [early["wuv"].buf])

            ih = 4 * 8 + 4 * 16
            slot1_pre = steps[ih]["pre"]
            steps[ih]["pre"] = lambda: (pre_h(), slot1_pre())
            for kb in range(NKB_H - 1, -1, -1):
                steps.append(dict(chains=[halo_chain], kb=kb, first=(kb == NKB_H - 1), last=(kb == 0),
                                  pre=(post_q if kb == NKB_H - 1 else None),
                                  mask=(maskH.ap[:, kb, :], maskH.buf)))
            P.pe_embed = True
            attn_stream(steps)
            P.pe_embed = False
            assert not side_jobs
            if "xo" in late:
                xo = late["xo"]
            else:
                A.release(wq)
        A.release(*e3, *sp2, *ex2, *w2, qh, *qpad[0], *qpad[1], Zmat)
        if phases_upto >= 3 and early.get("qT_released"):
            A.release(kT, Vt)
        else:
            A.release(kT, Vt, qT)
        if "ya" in DEBUG:
            dump("ya", ya.ap, [128, 4, TT], BF16, [b for r in ya_pb for b in r])

        yb = A.alloc("yb", [4, TT], BF16, tracked=False)
        yb_pb = [[A.sub(yb, "yb%d_%d" % (gp, si), gp * TT + c0, gp * TT + c0 + n) for si, (c0, n) in enumerate(segs)]
                 for gp in range(4)]
        if phases_upto >= 5:
            wg = [A.alloc("wg%d" % k, [2, 8, 128], BF16, top=True) for k in range(2)]
            wb = [A.alloc("wb%d" % k, [2, 4, 128], BF16, top=True) for k in range(2)]
            w_in_r = w_in.rearrange("(c p) f -> p c f", p=128)

            def load_wgb(fc):
                k = fc % 2
                dma("pool", wg[k].ap[:, 0], w_in_r[:, :, 2560 + fc * 128:2560 + (fc + 1) * 128], [], [wg[k].buf])
                dma("pool", wg[k].ap[:, 1], w_in_r[:, :, 3584 + fc * 128:3584 + (fc + 1) * 128], [], [wg[k].buf])
                dma("pool", wb[k].ap[:, 0], w_bra.rearrange("(c p) f -> p c f", p=128)[:, :, fc * 128:(fc + 1) * 128],
                    [], [wb[k].buf])
                dma("pool", wb[k].ap[:, 1], w_brb.rearrange("(c p) f -> p c f", p=128)[:, :, fc * 128:(fc + 1) * 128],
                    [], [wb[k].buf])
        if phases_upto >= 4:
            if phases_upto >= 3 and "wuv" in early:
                wuv = early["wuv"]
            else:
                wuv = A.alloc("wuv", [8, 1024], BF16)
                dma("pool", wuv.ap, w_in.rearrange("(c p) f -> p c f", p=128)[:, :, 1536:2560], [], [wuv.buf])
            xhb = A.alloc("xhb", [8, 512], BF16)
            dma("pool", xhb.ap, xT_hb.rearrange("(c p) t -> p c t", p=128), [], [xhb.buf])
            lg = load_const("lnsg_g", lnsg_g, [512], F32)
            lb = load_const("lnsg_b", lnsg_b, [512], F32)
            bsp = load_const("bsp", bsp_c, [4, 128], F32)
            wsf = load_const("wsf", wsT, [8, 128], F32)
            tril = load_const("tril", trilT, [128], F32)
            wsb = A.alloc("wsb", [8, 128], BF16)
            for g in range(8):
                tt("dve", wsb.ap[:, g, :], wsf.ap[:, g, :], tril.ap, ALU.mult, [wsf.buf, tril.buf], [wsb.buf])
            if phases_upto >= 5:
                load_wgb(0)
            ug2 = [A.alloc("ug_%d" % k, [4, SL], F32, tracked=False) for k in range(2)]
            ug_b2 = [[A.sub(ug2[k], "ug%d_%d" % (k, g), g * SL, (g + 1) * SL) for g in range(4)] for k in range(2)]
            gt = [A.alloc("gt%d" % k, [512], F32) for k in range(2)]
            vg2 = [[A.alloc("vg%d_%d" % (k, b_), [512], F32) for b_ in range(4)] for k in range(2)]
            vn = [A.alloc("vn%d" % k, [512], BF16) for k in range(4)]
            st2 = [(A.alloc("st6_%d" % k, [4, 6], F32), A.alloc("mv_%d" % k, [4, 2], F32), A.alloc("rs_%d" % k, [4], F32))
                   for k in range(2)]

            def D_proj(si):
                c0, n = segs[si]
                ug, ug_b, vg = ug2[si % 2], ug_b2[si % 2], vg2[si % 2]
                stt6, mv, rs = st2[si % 2]
                for g in range(4):
                    ps, psb = next_bank()
                    for kc in range(8):
                        mm(psb, ps[:, :n], wuv.ap[:, kc, g * 128:(g + 1) * 128], xo.ap[:, kc, c0:c0 + n],
                           kc == 0, kc == 7, [wuv.buf, xo.buf])
                    act(ug.ap[:, g, :n], ps[:, :n], AF.Gelu_apprx_tanh, [psb], [ug_b[g]])
                for bi in range(4):
                    xsrc, xbuf, col0 = (xo.ap, xo.buf, c0 + bi * 128) if si < 4 else (xhb.ap, xhb.buf, bi * 128)
                    ps, psb = next_bank()
                    for kc in range(8):
                        mm(psb, ps, xsrc[:, kc, col0:col0 + 128], wuv.ap[:, kc, 512:1024], kc == 0, kc == 7,
                           [xbuf, wuv.buf])
                    act(vg[bi].ap, ps, AF.Gelu_apprx_tanh, [psb], [vg[bi].buf])
                    P.op("dve", lambda e, bi=bi: e.bn_stats(stt6.ap[:, bi, :], vg[bi].ap), reads=[vg[bi].buf],
                         writes=[stt6.buf])
                    P.op("dve", lambda e, bi=bi: e.bn_aggr(mv.ap[:, bi, :], stt6.ap[:, bi, :]), reads=[stt6.buf],
                         writes=[mv.buf])
                act(rs.ap, mv.ap[:, :, 1], AF.Ln, [mv.buf], [rs.buf], bias=LN_EPS)
                act(rs.ap, rs.ap, AF.Exp, [rs.buf], [rs.buf], scale=-0.5)

            def D_ln(si):
                vg = vg2[si % 2]
                stt6, mv, rs = st2[si % 2]
                for bi in range(4):
                    stt("dve", vg[bi].ap, vg[bi].ap, mv.ap[:, bi, 0:1], lg.ap, ALU.subtract, ALU.mult,
                        [vg[bi].buf, mv.buf, lg.buf], [vg[bi].buf])
                    stt("dve", vn[bi].ap, vg[bi].ap, rs.ap[:, bi:bi + 1], lb.ap, ALU.mult, ALU.add,
                        [vg[bi].buf, rs.buf, lb.buf], [vn[bi].buf])

            def D_mix(si):
                c0, n = segs[si]
                ug, ug_b, vg = ug2[si % 2], ug_b2[si % 2], vg2[si % 2]
                for bi in range(4):
                    k = bi
                    t0, tn = (0, 128) if si < 4 else (126, 2)
                    for half in range(2):
                        ps, psb = next_bank()
                        for gg in range(4):
                            g = half * 4 + gg
                            mm(psb, ps[:, gg * tn:(gg + 1) * tn], vn[k].ap[:, (g // 2) * 128:(g // 2 + 1) * 128],
                               wsb.ap[:, g, t0:t0 + tn], True, True, [vn[k].buf, wsb.buf])
                        t1 = gt[half]
                        p0 = half * 2
                        psv = ps[:, :4 * tn].rearrange("p (a two b) -> p a two b", two=2, b=tn)
                        t1v = t1.ap[:, :2 * tn].rearrange("p (a b) -> p a b", a=2)
                        for par in range(2):
                            pr = par * 64
                            tt("dve", t1v[pr:pr + 64], psv[pr:pr + 64, :, par, :], bsp.ap[pr:pr + 64, p0:p0 + 2, t0:t0 + tn],
                               ALU.add, [psb, bsp.buf], [t1.buf])
                        tt("dve", yb.ap[:, p0:p0 + 2, c0 + bi * tn:c0 + (bi + 1) * tn], t1v,
                           ug.ap[:, p0:p0 + 2, bi * tn:(bi + 1) * tn], ALU.mult,
                           [t1.buf] + [ug_b[p0 + q] for q in range(2)], [yb_pb[p0 + q][si] for q in range(2)])

            D_proj(0)
            for si in range(len(segs)):
                D_ln(si)
                if si + 1 < len(segs):
                    D_proj(si + 1)
                D_mix(si)
            ug = ug2[0]
            vg = vg2[0] + vg2[1] + [ug2[1]]
            st6 = [t for tri in st2 for t in tri]
            A.release(wuv, xhb, lg, lb, bsp, wsf, tril, wsb, ug, *gt, *vg, *vn, *st6)
        if "yb" in DEBUG:
            dump("yb", yb.ap, [128, 4, TT], BF16, [b for r in yb_pb for b in r])

        mg = A.alloc("mg", [8, TT], BF16, tracked=False)
        mg_b = [[A.sub(mg, "mg%d_%d" % (fc, si), fc * TT + c0, fc * TT + c0 + n) for si, (c0, n) in enumerate(segs)]
                for fc in range(8)]
        if phases_upto >= 6:
            wo = A.alloc("wo", [8, D], BF16)
            dma("pool", wo.ap, w_out.rearrange("(c p) f -> p c f", p=128), [], [wo.buf])
            pb_out = load_const("b_out", rowp["b_out"], [D], F32)
            pg1 = load_const("ln1_g", rowp["ln1_g"], [D], F32)
            pb1 = load_const("ln1_b", rowp["ln1_b"], [D], F32)
        if phases_upto >= 5:
            ga = [A.alloc("ga%d" % k, [512], F32) for k in range(2)]
            gb = [A.alloc("gb%d" % k, [512], F32) for k in range(2)]
            it = 0
            for fc in range(8):
                k = fc % 2
                if fc + 1 < 8:
                    load_wgb(fc + 1)
                for si, (c0, n) in enumerate(segs):
                    kk = it % 2
                    it += 1
                    pga, pgb, pba, pbb = [next_bank() for _ in range(4)]
                    for kc in range(8):
                        mm(pga[1], pga[0][:, :n], wg[k].ap[:, 0, kc, :], xo.ap[:, kc, c0:c0 + n], kc == 0, kc == 7,
                           [wg[k].buf, xo.buf])
                    for kc in range(8):
                        mm(pgb[1], pgb[0][:, :n], wg[k].ap[:, 1, kc, :], xo.ap[:, kc, c0:c0 + n], kc == 0, kc == 7,
                           [wg[k].buf, xo.buf])
                    for hp in range(4):
                        mm(pba[1], pba[0][:, :n], wb[k].ap[:, 0, hp, :], ya.ap[:, hp, c0:c0 + n], hp == 0, hp == 3,
                           [wb[k].buf, ya_pb[hp][si]])
                    for hp in range(4):
                        mm(pbb[1], pbb[0][:, :n], wb[k].ap[:, 1, hp, :], yb.ap[:, hp, c0:c0 + n], hp == 0, hp == 3,
                           [wb[k].buf, yb_pb[hp][si]])
                    act(ga[kk].ap[:, :n], pga[0][:, :n], AF.Sigmoid, [pga[1], bgate.buf], [ga[kk].buf],
                        bias=bgate.ap[:, fc:fc + 1])
                    act(gb[kk].ap[:, :n], pgb[0][:, :n], AF.Sigmoid, [pgb[1], bgate.buf], [gb[kk].buf],
                        bias=bgate.ap[:, 8 + fc:9 + fc])
                    tt("dve", ga[kk].ap[:, :n], ga[kk].ap[:, :n], pba[0][:, :n], ALU.mult, [ga[kk].buf, pba[1]], [ga[kk].buf])
                    tt("dve", gb[kk].ap[:, :n], gb[kk].ap[:, :n], pbb[0][:, :n], ALU.mult, [gb[kk].buf, pbb[1]], [gb[kk].buf])
                    tt("dve", mg.ap[:, fc, c0:c0 + n], ga[kk].ap[:, :n], gb[kk].ap[:, :n], ALU.add,
                       [ga[kk].buf, gb[kk].buf], [mg_b[fc][si]])
            A.release(*wg, *wb, *ga, *gb)
        A.release(xo, ya, yb)
        if "mg" in DEBUG:
            dump("mg", mg.ap, [128, 8, TT], BF16, [b for r in mg_b for b in r])

        h1T = A.alloc("h1T", [8, TT], BF16, tracked=False, top=True)
        blocks = [(r * 128, 128) for r in range(16)] + [(T, NHALO)]
        h1T_b = [[A.sub(h1T, "h1T%d_%d" % (kc, r), kc * TT + r0, kc * TT + r0 + nr) for r, (r0, nr) in enumerate(blocks)]
                 for kc in range(8)]

        def ln_head(r_t, nr, st_t):
            for hlf in range(2):
                P.op("dve", lambda e, hlf=hlf: e.bn_stats(st_t.ap[:nr, hlf * 6:hlf * 6 + 6],
                                                         r_t.ap[:nr, hlf * 512:(hlf + 1) * 512]),
                     reads=[r_t.buf], writes=[st_t.buf])
            P.op("dve", lambda e: e.bn_aggr(st_t.ap[:nr, 12:14], st_t.ap[:nr, 0:12]), reads=[st_t.buf], writes=[st_t.buf])
            act(st_t.ap[:nr, 13:14], st_t.ap[:nr, 13:14], AF.Ln, [st_t.buf], [st_t.buf], bias=LN_EPS)
            act(st_t.ap[:nr, 13:14], st_t.ap[:nr, 13:14], AF.Exp, [st_t.buf], [st_t.buf], scale=-0.5)

        def ln_tail(r_t, nr, g_t, b_t, st_t, out_ap, out_bufs):
            stt("dve", r_t.ap[:nr], r_t.ap[:nr], st_t.ap[:nr, 12:13], g_t.ap[:nr], ALU.subtract, ALU.mult,
                [r_t.buf, st_t.buf, g_t.buf], [r_t.buf])
            stt("dve", out_ap, r_t.ap[:nr], st_t.ap[:nr, 13:14], b_t.ap[:nr], ALU.mult, ALU.add,
                [r_t.buf, st_t.buf, b_t.buf], out_bufs)

        if phases_upto >= 7:
            wu = [A.alloc("wu%d" % k, [2, 8, 256], BF16, top=True) for k in range(2)]
            w_up_r = w_up.rearrange("(c p) f -> p c f", p=128)

            def load_wu(jj):
                k = jj % 2
                dma("pool", wu[k].ap[:, 0], w_up_r[:, :, jj * 256:(jj + 1) * 256], [], [wu[k].buf])
                dma("pool", wu[k].ap[:, 1], w_up_r[:, :, DFF + jj * 256:DFF + (jj + 1) * 256], [], [wu[k].buf])

            load_wu(0)
        if phases_upto >= 6:
            NB3 = 3
            xb = [A.alloc("xb%d" % k, [D], F32) for k in range(NB3)]
            rr = [A.alloc("rr%d" % k, [D], F32) for k in range(NB3)]
            hh = [A.alloc("hh%d" % k, [D], F32) for k in range(2)]
            hb = [A.alloc("hb%d" % k, [D], BF16) for k in range(2)]
            stt_ = [A.alloc("stF%d" % k, [16], F32) for k in range(NB3)]

            ones1 = A.alloc("ones1", [128], BF16)
            bhi = A.alloc("bhi", [D], BF16)
            blo = A.alloc("blo", [D], BF16)
            P.op("dve", lambda e: e.memset(ones1.ap[0:1], 1.0), reads=[], writes=[ones1.buf], nfree=128)
            cp("dve", bhi.ap[0:1], pb_out.ap[0:1], [pb_out.buf], [bhi.buf])
            tt("dve", blo.ap[0:1], pb_out.ap[0:1], bhi.ap[0:1], ALU.subtract, [pb_out.buf, bhi.buf], [blo.buf])
            Fps = {}

            def F_mm(r):
                r0, nr = blocks[r]
                k = r % NB3
                dma("sp", xb[k].ap[:nr], x_own[r0:r0 + nr, :], [], [xb[k].buf])
                Fps[r] = []
                for hlf in range(2):
                    ps, psb = next_bank()
                    Fps[r].append((ps, psb))
                    for kc in range(8):
                        mm(psb, ps[:nr, :], mg.ap[:, kc, r0:r0 + nr], wo.ap[:, kc, hlf * 512:(hlf + 1) * 512],
                           kc == 0, False, [mg_b[kc][min(r // 4, 4)], wo.buf])
                    mm(psb, ps[:nr, :], ones1.ap[0:1, :nr], bhi.ap[0:1, hlf * 512:(hlf + 1) * 512], False, False,
                       [ones1.buf, bhi.buf])
                    mm(psb, ps[:nr, :], ones1.ap[0:1, :nr], blo.ap[0:1, hlf * 512:(hlf + 1) * 512], False, True,
                       [ones1.buf, blo.buf])

            def F_head(r):
                r0, nr = blocks[r]
                k = r % NB3
                for hlf in range(2):
                    ps, psb = Fps[r][hlf]
                    stt("dve", rr[k].ap[:nr, hlf * 512:(hlf + 1) * 512], xb[k].ap[:nr, hlf * 512:(hlf + 1) * 512], ALPHA,
                        ps[:nr, :], ALU.mult, ALU.add, [xb[k].buf, psb], [rr[k].buf])
                ln_head(rr[k], nr, stt_[k])

            def F_tail(r):
                r0, nr = blocks[r]
                k, k2 = r % NB3, r % 2
                ln_tail(rr[k], nr, pg1, pb1, stt_[k], hh[k2].ap[:nr], [hh[k2].buf])
                dma("sp", h1s[r0:r0 + nr, :], hh[k2].ap[:nr], [hh[k2].buf], [h1s_buf[r]])
                cp("act", hb[k2].ap[:nr], hh[k2].ap[:nr], [hh[k2].buf], [hb[k2].buf])

            def F_tr(r):
                r0, nr = blocks[r]
                k2 = r % 2
                for grp in range(2):
                    ps, psb = next_bank()
                    psv = ps.bitcast(BF16)
                    for q4 in range(4):
                        kc = grp * 4 + q4
                        P.op("pe", lambda e, kc=kc, q4=q4, psv=psv, k2=k2, nr=nr: e.transpose(
                            psv[:, q4 * 128:q4 * 128 + nr], hb[k2].ap[:nr, kc * 128:(kc + 1) * 128], ident.ap[:nr, :nr]),
                            reads=[hb[k2].buf, ident.buf], writes=[psb])
                    src_v = psv[:, 0:512].rearrange("p (a b) -> p a b", a=4)[:, :, :nr]
                    cp("act", h1T.ap[:, grp * 4:(grp + 1) * 4, r0:r0 + nr], src_v, [psb],
                                        [h1T_b[grp * 4 + q][r] for q in range(4)])

            nblk = len(blocks)
            F_mm(0)
            F_mm(1)
            F_head(0)
            for r in range(nblk):
                if r + 2 < nblk:
                    F_mm(r + 2)
                F_tail(r)
                if r + 1 < nblk:
                    F_head(r + 1)
                F_tr(r)
            A.release(wo, pb_out, pg1, pb1, *xb, *rr, *hh, *hb, *stt_, ones1, bhi, blo)
        A.release(mg)
        if "h1T" in DEBUG:
            dump("h1T", h1T.ap, [128, 8, TT], BF16, [b for r in h1T_b for b in r])

        actT = A.alloc("actT", [NCH, T], BF16, tracked=False)
        act_b = [[A.sub(actT, "act%d_%d" % (j, i), j * T + i * SL, j * T + (i + 1) * SL) for i in range(NSLOT)]
                 for j in range(NCH)]
        if phases_upto >= 8:
            wdA = A.alloc("wdA", [NCH // 2, D], BF16)
        if phases_upto >= 7:
            P.pe_embed = True
            uh = A.alloc("uh", [44, NHALO], F32, tracked=False)
            bnd = A.alloc("bnd", [44, NHALO], F32, tracked=False)
            uh_b = [A.sub(uh, "uh%d" % c, c * NHALO, (c + 1) * NHALO) for c in range(44)]
            bnd_b = [A.sub(bnd, "bnd%d" % c, c * NHALO, (c + 1) * NHALO) for c in range(44)]
            cv = [[A.alloc("cv%d%d" % (s_, k), [SL], F32) for k in range(2)] for s_ in range(2)]
            sa = [A.alloc("sa%d" % k, [SL], F32) for k in range(2)]
            it = 0
            if phases_upto >= 8:
                dma("pool", wdA.ap, w_down.rearrange("(c p) f -> p c f", p=128)[:, 0:NCH // 2, :], [], [wdA.buf])
            for jj in range(NCH // 2):
                k = jj % 2
                if jj + 1 < NCH // 2:
                    load_wu(jj + 1)
                for sub in range(2):
                    j = 2 * jj + sub
                    for s_ in range(2):
                        c = s_ * NCH + j
                        ps, psb = next_bank()
                        for kc in range(8):
                            mm(psb, ps[:, :NHALO], wu[k].ap[:, s_, kc, sub * 128:(sub + 1) * 128],
                               h1T.ap[:, kc, T:TT], kc == 0, kc == 7, [wu[k].buf, h1T_b[kc][16]])
                        tt("dve", uh.ap[:, c, :], ps[:, :NHALO], flags.ap, ALU.mult, [psb, flags.buf], [uh_b[c]])
                        uv = uh.ap[:, c, :].rearrange("p (a b) -> p a b", b=2)
                        bv_ = bnd.ap[:, c, :].rearrange("p (a b) -> p a b", b=2)
                        ts("dve", bv_[:, :, 1:2], uv[:, :, 1:2], convp.ap[:, c:c + 1], None, ALU.mult, None,
                           [uh_b[c], convp.buf], [bnd_b[c]])
                        ts("dve", bv_[:, :, 0:1], uv[:, :, 1:2], convp.ap[:, 44 + c:45 + c], None, ALU.mult, None,
                           [uh_b[c], convp.buf], [bnd_b[c]])
                        stt("dve", bv_[:, :, 0:1], uv[:, :, 0:1], convp.ap[:, c:c + 1], bv_[:, :, 0:1], ALU.mult, ALU.add,
                            [uh_b[c], convp.buf, bnd_b[c]], [bnd_b[c]])
                for i in range(NSLOT):
                    for sub in range(2):
                        j = 2 * jj + sub
                        kk = it % 2
                        it += 1
                        for s_ in range(2):
                            c = s_ * NCH + j
                            ps, psb = next_bank()
                            for kc in range(8):
                                mm(psb, ps, wu[k].ap[:, s_, kc, sub * 128:(sub + 1) * 128],
                                   h1T.ap[:, kc, i * SL:(i + 1) * SL], kc == 0, kc == 7,
                                   [wu[k].buf] + [h1T_b[kc][4 * i + q] for q in range(4)])
                            c_ = cv[s_][kk]
                            act(c_.ap, ps, AF.Identity, [psb, convp.buf], [c_.buf],
                                bias=convp.ap[:, 132 + c:133 + c], scale=convp.ap[:, 88 + c:89 + c])
                            stt("dve", c_.ap[:, 1:SL], ps[:, 0:SL - 1], convp.ap[:, 44 + c:45 + c], c_.ap[:, 1:SL],
                                ALU.mult, ALU.add, [psb, convp.buf, c_.buf], [c_.buf])
                            stt("dve", c_.ap[:, 2:SL], ps[:, 0:SL - 2], convp.ap[:, c:c + 1], c_.ap[:, 2:SL],
                                ALU.mult, ALU.add, [psb, convp.buf, c_.buf], [c_.buf])
                            tt("dve", c_.ap[:, 0:2], c_.ap[:, 0:2], bnd.ap[:, c, 2 * i:2 * i + 2], ALU.add,
                               [c_.buf, bnd_b[c]], [c_.buf])
                        ca, cbv = cv[0][kk], cv[1][kk]
                        act(sa[kk].ap, ca.ap, AF.Silu, [ca.buf], [sa[kk].buf])
                        tt("pool", actT.ap[:, j, i * SL:(i + 1) * SL], sa[kk].ap, cbv.ap, ALU.mult, [sa[kk].buf, cbv.buf],
                           [act_b[j][i]])
            A.release(*wu, uh, bnd, *cv[0], *cv[1], *sa)
            P.pe_embed = False
        A.release(h1T)
        if "act" in DEBUG:
            dump("act", actT.ap, [128, NCH, T], BF16, [b for r in act_b for b in r])

        if phases_upto >= 8:
            wdB = A.alloc("wdB", [NCH // 2, D], BF16)
            dma("pool", wdB.ap, w_down.rearrange("(c p) f -> p c f", p=128)[:, NCH // 2:NCH, :], [], [wdB.buf])
            pb_dn = load_const("b_down", rowp["b_down"], [D], F32)
            pg2 = load_const("ln2_g", rowp["ln2_g"], [D], F32)
            pb2 = load_const("ln2_b", rowp["ln2_b"], [D], F32)
            NB3 = 3
            hr = [A.alloc("hr%d" % k, [D], F32) for k in range(NB3)]
            rr = [A.alloc("rr%d" % k, [D], F32) for k in range(NB3)]
            oo = [A.alloc("oo%d" % k, [D], F32) for k in range(2)]
            stt_ = [A.alloc("stG%d" % k, [16], F32) for k in range(NB3)]

            def G_load(r):
                dma("sp", hr[r % NB3].ap, h1s[r * 128:(r + 1) * 128, :], [h1s_buf[r]], [hr[r % NB3].buf])

            Gps = {}

            def G_mm_a(r):
                r0 = r * 128
                Gps[r] = []
                for hlf in range(2):
                    ps, psb = next_bank()
                    Gps[r].append((ps, psb))
                    for j in range(NCH // 2):
                        mm(psb, ps, actT.ap[:, j, r0:r0 + 128], wdA.ap[:, j, hlf * 512:(hlf + 1) * 512],
                           j == 0, False, [act_b[j][r // 4], wdA.buf])

            def G_mm_b(r):
                k = r % NB3
                r0 = r * 128
                for hlf in range(2):
                    ps, psb = Gps[r][hlf]
                    for j in range(NCH // 2, NCH):
                        mm(psb, ps, actT.ap[:, j, r0:r0 + 128], wdB.ap[:, j - NCH // 2, hlf * 512:(hlf + 1) * 512],
                           False, j == NCH - 1, [act_b[j][r // 4], wdB.buf])
                    tt("dve", rr[k].ap[:, hlf * 512:(hlf + 1) * 512], ps, pb_dn.ap[:, hlf * 512:(hlf + 1) * 512],
                       ALU.add, [psb, pb_dn.buf], [rr[k].buf])
                stt("dve", rr[k].ap, hr[k].ap, ALPHA, rr[k].ap, ALU.mult, ALU.add, [hr[k].buf, rr[k].buf], [rr[k].buf])
                ln_head(rr[k], 128, stt_[k])

            def G_tail(r):
                k, k2 = r % NB3, r % 2
                ln_tail(rr[k], 128, pg2, pb2, stt_[k], oo[k2].ap, [oo[k2].buf])
                out_stores.append(dma("sp", out_d[r * 128:(r + 1) * 128, :], oo[k2].ap, [oo[k2].buf], [P.buf("out%d" % r)]))

            G_load(0)
            G_load(1)
            NPRE = 4
            for r in range(NPRE):
                G_mm_a(r)
            for r in range(16):
                if r + 2 < 16:
                    G_load(r + 2)
                if r >= NPRE:
                    G_mm_a(r)
                G_mm_b(r)
                if r >= 1:
                    G_tail(r - 1)
            G_tail(15)
        else:
            z_t = A.alloc("zt", [D], F32)
            P.op("dve", lambda e: e.memset(z_t.ap, 0.0), reads=[], writes=[z_t.buf])
            for r in range(16):
                out_stores.append(dma("sp", out_d[r * 128:(r + 1) * 128, :], z_t.ap, [z_t.buf], [P.buf("out%d" % r)]))

        fence_bufs = []
        fin = Op()
        fin.eng, fin.dma, fin.flag, fin.sig, fin.idx = "sp", False, False, None, P.n
        fin.deps = list(out_stores) + list(dbg_outs.values())
        fin.fn = lambda e: e.nop()
        P.ops["sp"].append(fin)

        P.emit(nc)
    return nc, list(dbg_outs.keys())


def _bf16(a):
    return np.asarray(a, dtype=np.float32).astype(ml_dtypes.bfloat16)


def make_in_maps(x, w_in, b_gate, ln_sg_g, ln_sg_b, w_spatial, b_spatial, w_branch_a, w_branch_b, w_out, b_out,
                 ln1_g, ln1_b, w_up, conv_w, conv_b, w_down, b_down, ln2_g, ln2_b):
    f = lambda a: np.ascontiguousarray(np.asarray(a, dtype=np.float32))
    x = f(x)
    common = {
        "w_in": f(w_in[0]), "w_bra": f(w_branch_a[0]), "w_brb": f(w_branch_b[0]), "w_out": f(w_out[0]),
        "w_up": f(w_up[0]), "w_down": f(w_down[0]),
        "bgate_c": f(np.asarray(b_gate[0]).reshape(16, 128).T),
        "lnsg_g": f(np.broadcast_to(np.asarray(ln_sg_g[0])[None, :], (128, 512))),
        "lnsg_b": f(np.broadcast_to(np.asarray(ln_sg_b[0])[None, :], (128, 512))),
        "wsT": f(np.transpose(np.asarray(w_spatial[0]), (2, 0, 1))),
        "bsp_c": f(np.repeat(np.asarray(b_spatial[0]).reshape(4, 2, 128).transpose(1, 0, 2), 64, axis=0)),
    }
    cw = np.asarray(conv_w[0])
    cb = np.asarray(conv_b[0])
    cols = [cw[0].reshape(44, 128).T, cw[1].reshape(44, 128).T, cw[2].reshape(44, 128).T, cb.reshape(44, 128).T]
    common["convp_c"] = f(np.concatenate(cols, axis=1))
    for k, v in (("b_out", b_out), ("ln1_g", ln1_g), ("ln1_b", ln1_b), ("b_down", b_down), ("ln2_g", ln2_g),
                 ("ln2_b", ln2_b)):
        common[k] = f(np.broadcast_to(np.asarray(v[0])[None, :], (128, D)))
    p = np.arange(128)
    common["ident_c"] = _bf16(np.eye(128))
    common["U_c"] = _bf16((p[:, None] >= p[None, :]))
    common["L_c"] = _bf16((p[:, None] < p[None, :]))
    common["trilT"] = f((p[:, None] <= p[None, :]))
    in_maps = []
    for c in range(8):
        b, j = c // 2, c % 2
        starts = [1024 * i + 512 * j for i in range(NSLOT)]
        own = np.concatenate([np.arange(s0, s0 + SL) for s0 in starts])
        halo, hvalid, hb = [], [], []
        for s0 in starts:
            if s0 >= 2:
                halo += [s0 - 2, s0 - 1]
                hvalid += [1.0, 1.0]
                hb.append(np.arange(s0 - 128, s0))
            else:
                halo += [0, 1]
                hvalid += [0.0, 0.0]
                hb.append(np.arange(0, 128))
        halo = np.array(halo)
        cols_all = np.concatenate([own, halo])
        xb = x[b]
        m = dict(common)
        m["xT_all"] = np.ascontiguousarray(xb.T)
        m["xT_own"] = np.ascontiguousarray(xb[cols_all].T)
        m["xT_hb"] = np.ascontiguousarray(xb[np.concatenate(hb)].T)
        m["x_own"] = np.ascontiguousarray(xb[cols_all])
        m["flags_c"] = f(np.broadcast_to(np.array(hvalid, dtype=np.float32)[None, :], (128, NHALO)))
        r = np.arange(8)
        cq = np.arange(512)
        mm_ = (128 * r[None, :, None] + p[:, None, None]) < (512 * j + cq[None, None, :])
        m["maskM_c"] = _bf16(mm_)
        kb = np.arange(NKB_H)
        tq = np.where(np.array(hvalid) > 0, halo, -1)
        mh = (128 * kb[None, :, None] + p[:, None, None]) < tq[None, None, :]
        mh = np.broadcast_to(mh[:, :, None, :], (128, NKB_H, 8, NHALO)).reshape(128, NKB_H, 64)
        m["maskH_c"] = _bf16(mh)
        in_maps.append(m)
    return in_maps


_CACHE = {}


def kernel(**inputs):
    in_maps = make_in_maps(**inputs)
    if "nc" not in _CACHE:
        _CACHE["nc"] = build_program()
    nc, dbg = _CACHE["nc"]
    res = run_bass_kernel_spmd(nc, in_maps, core_ids=list(range(8)))
    out = np.zeros((4, S, D), dtype=np.float32)
    for c in range(8):
        b, j = c // 2, c % 2
        o = np.asarray(res.results[c]["out"], dtype=np.float32)
        for i in range(NSLOT):
            s0 = 1024 * i + 512 * j
            out[b, s0:s0 + SL] = o[i * SL:(i + 1) * SL]
    return out
```

```python
import numpy as np
import ml_dtypes
import concourse.bass as bass
import concourse.mybir as mybir
from concourse.bass_utils import run_bass_kernel_spmd

F32 = mybir.dt.float32
BF16 = mybir.dt.bfloat16
AF = mybir.ActivationFunctionType
ALU = mybir.AluOpType

D = 1024
S = 4096
NSLOT = 4
SL = 512
T = NSLOT * SL
NHALO = 8
TT = T + NHALO
DFF = 2816
NCH = DFF // 128
ALPHA = 2.0 ** 0.25
LN_EPS = 1e-5
GELU_C = 1.5957691216057308
NKB_H = 28
ARENA_BYTES = 190 * 1024

DEBUG = {}
EMBED_WAITS = True


class Buf:
    __slots__ = ("name", "lo", "hi", "lw", "rd", "al", "inherit", "excl")

    def __init__(self, name, lo, hi):
        self.name, self.lo, self.hi = name, lo, hi
        self.lw = None
        self.rd = {}
        self.al = []
        self.inherit = None
        self.excl = False


class Op:
    __slots__ = ("eng", "fn", "deps", "sig", "flag", "dma", "idx", "nfree")


class Prog:
    ENGS = ("pe", "act", "dve", "pool", "sp")

    def __init__(self):
        self.ops = {e: [] for e in self.ENGS}
        self.abufs = []
        self.dead = []
        self.n = 0

    def buf(self, name, lo=None, hi=None):
        b = Buf(name, lo, hi)
        if lo is not None:
            inh = set()
            for d in self.dead:
                if d.lo < hi and lo < d.hi:
                    if d.lw is not None:
                        inh.add(d.lw)
                    inh.update(d.rd.values())
            b.inherit = inh or None
            for o in self.abufs:
                if o.lo < hi and lo < o.hi:
                    o.al.append(b)
                    b.al.append(o)
            self.abufs.append(b)
        return b

    def kill(self, b):
        if b.lo is None:
            return
        self.abufs.remove(b)
        for o in b.al:
            o.al.remove(b)
        b.al = []
        self.dead.append(b)

    def op(self, eng, fn, reads=(), writes=(), dma=False, nfree=0):
        o = Op()
        o.eng, o.fn, o.dma, o.flag, o.sig = eng, fn, dma, False, None
        o.nfree = nfree
        o.idx = self.n
        self.n += 1
        raw, oth = set(), set()
        for b in list(reads) + list(writes):
            if b.inherit:
                oth |= b.inherit
                b.inherit = None
        for b in reads:
            for bb in [b] + b.al:
                if bb.lw is not None:
                    raw.add(bb.lw)
            if b.excl:
                for r in b.rd.values():
                    if r.eng != eng:
                        oth.add(r)
        for b in writes:
            for bb in [b] + b.al:
                if bb.lw is not None:
                    oth.add(bb.lw)
                for r in bb.rd.values():
                    oth.add(r)
        deps = []
        for d in raw | oth:
            if d is o:
                continue
            if d.dma or o.dma:
                deps.append(d)
            elif d.eng != eng:
                deps.append(d)
            elif eng == "pool":
                deps.append(d)
            elif eng != "pe" and d in raw and d.nfree < 256:
                deps.append(d)
        best = {}
        out = []
        for d in deps:
            if d.dma:
                out.append(d)
            else:
                if d.eng not in best or best[d.eng].idx < d.idx:
                    best[d.eng] = d
        out.extend(best.values())
        o.deps = out
        for d in out:
            d.flag = True
        for b in reads:
            key = ("dma", o.idx) if dma else eng
            b.rd[key] = o
        for b in writes:
            for bb in [b] + b.al:
                bb.lw = o
                bb.rd = {}
        self.ops[eng].append(o)
        return o

    def emit(self, nc):
        NDS = 12
        sems = {}
        with nc.Block() as block:
            import contextlib
            with contextlib.ExitStack() as st:
                esem = {e: st.enter_context(nc.semaphore("s_" + e)) for e in self.ENGS}
                dsem = {e: [st.enter_context(nc.semaphore("d_%s%d" % (e, k))) for k in range(NDS)]
                        for e in ("pool", "sp")}
                for e in self.ENGS:
                    cnt = 0
                    dcnt = [0] * NDS
                    k = 0
                    for o in self.ops[e]:
                        if o.dma:
                            dcnt[k] += 16
                            o.sig = (dsem[e][k], dcnt[k])
                            k = (k + 1) % NDS
                        elif o.flag:
                            cnt += 1
                            o.sig = (esem[e], cnt)

                def run(engname, eng):
                    waited = {}
                    for o in self.ops[engname]:
                        need = {}
                        for d in o.deps:
                            sem, val = d.sig
                            if waited.get(id(sem), 0) < val and need.get(id(sem), (None, 0))[1] < val:
                                need[id(sem)] = (sem, val)
                        if o.dma:
                            sem, val = o.sig
                            if val > 16 and waited.get(id(sem), 0) < val - 16 and need.get(id(sem), (None, 0))[1] < val - 16:
                                need[id(sem)] = (sem, val - 16)
                        need = list(need.values())
                        embed = None
                        if need and EMBED_WAITS and (not o.dma or engname == "sp"):
                            embed = need.pop()
                        for sem, val in need:
                            eng.wait_ge(sem, val)
                            waited[id(sem)] = val
                        n0 = nc.n_instructions() if embed is not None else 0
                        ins = o.fn(eng)
                        if embed is not None:
                            sem, val = embed
                            if nc.n_instructions() - n0 == 1:
                                ins._wait_ge(sem, val)
                            else:
                                raise RuntimeError("multi-instruction op cannot carry an embedded wait")
                            waited[id(sem)] = val
                        if o.dma:
                            ins.then_inc(o.sig[0], 16)
                        elif o.flag:
                            ins.then_inc(o.sig[0], 1)

                @block.tensor
                def _(e):
                    run("pe", e)

                @block.scalar
                def _(e):
                    run("act", e)

                @block.vector
                def _(e):
                    run("dve", e)

                @block.gpsimd
                def _(e):
                    run("pool", e)

                @block.sync
                def _(e):
                    run("sp", e)


class Tile:
    def __init__(self, ap, buf, off, nbytes):
        self.ap, self.buf, self.off, self.nbytes = ap, buf, off, nbytes


class Arena:
    def __init__(self, P, arena_ap, nbytes):
        self.P, self.a, self.nbytes = P, arena_ap, nbytes
        self.free = [(0, nbytes)]
        self.live = {}

    def alloc(self, name, shape, dtype, tracked=True, top=False):
        esz = 4 if dtype == F32 else 2
        n = 1
        for s in shape:
            n *= s
        nb = (n * esz + 63) // 64 * 64
        order = list(enumerate(self.free))
        if top:
            order = order[::-1]
        for k, (lo, hi) in order:
            if hi - lo >= nb:
                off = hi - nb if top else lo
                if hi - lo == nb:
                    self.free.pop(k)
                elif top:
                    self.free[k] = (lo, hi - nb)
                else:
                    self.free[k] = (lo + nb, hi)
                break
        else:
            raise RuntimeError("arena OOM for %s (%d B); free=%s" % (name, nb, self.free))
        ap = self.a[:, off // 2: off // 2 + n * esz // 2]
        if dtype == F32:
            ap = ap.bitcast(F32)
        if len(shape) == 2:
            ap = ap.rearrange("p (a b) -> p a b", a=shape[0])
        elif len(shape) == 3:
            ap = ap.rearrange("p (a b c) -> p a b c", a=shape[0], b=shape[1])
        t = Tile(ap, self.P.buf(name, off, off + nb) if tracked else None, off, nb)
        t.esz = esz
        t.name = name
        t.subs = []
        self.live[name] = t
        return t

    def sub(self, t, name, elem_lo, elem_hi):
        assert t.buf is None, "tiles with sub-buffers must be allocated with tracked=False"
        b = self.P.buf(name, t.off + elem_lo * t.esz, t.off + elem_hi * t.esz)
        t.subs.append(b)
        return b

    def release(self, *tiles):
        for t in tiles:
            del self.live[t.name]
            self.free.append((t.off, t.off + t.nbytes))
            if t.buf is not None:
                self.P.kill(t.buf)
            for b in t.subs:
                self.P.kill(b)
        self.free.sort()
        merged = []
        for lo, hi in self.free:
            if merged and merged[-1][1] == lo:
                merged[-1] = (merged[-1][0], hi)
            else:
                merged.append((lo, hi))
        self.free = merged


def build_program(phases_upto=99):
    nc = bass.Bass("TRN2", target_bir_lowering=False)
    P = Prog()

    def din(name, shape, dt=F32):
        return nc.dram_tensor(name, list(shape), dt, kind="ExternalInput").ap()

    xT_all = din("xT_all", [D, S])
    xT_own = din("xT_own", [D, TT])
    xT_hb = din("xT_hb", [D, 512])
    x_own = din("x_own", [TT, D])
    w_in = din("w_in", [D, 4608])
    w_bra = din("w_bra", [512, D])
    w_brb = din("w_brb", [512, D])
    w_out = din("w_out", [D, D])
    w_up = din("w_up", [D, 2 * DFF])
    w_down = din("w_down", [DFF, D])
    bgate_c = din("bgate_c", [128, 16])
    convp_c = din("convp_c", [128, 4 * 44])
    flags_c = din("flags_c", [128, NHALO])
    lnsg_g = din("lnsg_g", [128, 512])
    lnsg_b = din("lnsg_b", [128, 512])
    wsT = din("wsT", [128, 8, 128])
    trilT = din("trilT", [128, 128])
    bsp_c = din("bsp_c", [128, 4, 128])
    rowp = {k: din(k, [128, D]) for k in ("b_out", "ln1_g", "ln1_b", "b_down", "ln2_g", "ln2_b")}
    ident_c = din("ident_c", [128, 128], BF16)
    U_c = din("U_c", [128, 128], BF16)
    L_c = din("L_c", [128, 128], BF16)
    maskM_c = din("maskM_c", [128, 8, 512], BF16)
    maskH_c = din("maskH_c", [128, NKB_H, 64], BF16)
    out_d = nc.dram_tensor("out", [T, D], F32, kind="ExternalOutput").ap()
    h1s = nc.dram_tensor("h1s", [TT, D], F32).ap()
    h1s_buf = [P.buf("h1s%d" % r) for r in range(17)]
    dbg_outs = {}

    import contextlib
    with contextlib.ExitStack() as st:
        arena_t = st.enter_context(nc.sbuf_tensor("arena", [128, ARENA_BYTES // 2], BF16))
        A = Arena(P, arena_t[:], ARENA_BYTES)
        psum = []
        psall_t = st.enter_context(nc.psum_tensor("psall", [128, 4096], F32))
        psall = psall_t[:]
        for k in range(8):
            psum.append((psall[:, k * 512:(k + 1) * 512], P.buf("ps%d" % k)))
            psum[-1][1].excl = True

        def nfree_of(ap):
            n = 1
            for s in ap.shape[1:]:
                n *= s
            return n

        def mm(psb, out, lhsT, rhs, start, stop, reads):
            P.op("pe", lambda e: e.matmul(out, lhsT, rhs, start=start, stop=stop),
                 reads=reads, writes=[psb])

        def act(out, in_, func, reads, writes, bias=None, scale=None):
            kw = {}
            if bias is not None:
                kw["bias"] = bias
            if scale is not None:
                kw["scale"] = scale
            P.op("act", lambda e: e.activation(out, in_, func, **kw), reads=reads, writes=writes, nfree=nfree_of(out))

        def tt(eng, out, in0, in1, op, reads, writes):
            P.op(eng, lambda e: e.tensor_tensor(out, in0, in1, op), reads=reads, writes=writes, nfree=nfree_of(out))

        def ts(eng, out, in0, s1, s2, op0, op1, reads, writes):
            if s2 is None:
                P.op(eng, lambda e: e.tensor_scalar(out, in0, s1, None, op0), reads=reads, writes=writes, nfree=nfree_of(out))
            else:
                P.op(eng, lambda e: e.tensor_scalar(out, in0, s1, s2, op0, op1), reads=reads, writes=writes, nfree=nfree_of(out))

        def stt(eng, out, in0, scalar, in1, op0, op1, reads, writes):
            P.op(eng, lambda e: e.scalar_tensor_tensor(out, in0, scalar, in1, op0, op1),
                 reads=reads, writes=writes, nfree=nfree_of(out))

        def cp(eng, out, in_, reads, writes):
            if eng == "act":
                P.op("act", lambda e: e.copy(out, in_), reads=reads, writes=writes, nfree=nfree_of(out))
            else:
                P.op(eng, lambda e: e.tensor_copy(out, in_), reads=reads, writes=writes, nfree=nfree_of(out))

        def dma(eng, out, in_, reads, writes):
            return P.op(eng, lambda e: e.dma_start(out=out, in_=in_), reads=reads, writes=writes, dma=True)

        def dump(name, tile_ap, shape, dtype, reads):
            d = nc.dram_tensor("dbg_" + name, list(shape), dtype, kind="ExternalOutput").ap()
            dbg_outs[name] = dma("sp", d, tile_ap, reads, [P.buf("dbg_" + name)])

        def load_const(name, src, shape, dtype, eng="sp", parts=128):
            t = A.alloc(name, shape, dtype)
            dma(eng, t.ap[0:parts] if parts != 128 else t.ap, src, [], [t.buf])
            return t

        evac_rr = [0]

        def evac(out, in_, reads, writes):
            evac_rr[0] ^= 1
            cp("dve" if evac_rr[0] else "act", out, in_, reads, writes)

        pbank = [0]

        def next_bank():
            pbank[0] = (pbank[0] + 1) % 8
            return psum[pbank[0]]

        segs = [(i * SL, SL) for i in range(NSLOT)] + [(T, NHALO)]
        out_stores = []

        ident = A.alloc("ident", [128], BF16)
        Umat = A.alloc("U", [128], BF16)
        Lmat = A.alloc("L", [128], BF16)
        maskM = A.alloc("maskM", [8, 512], BF16)
        maskH = A.alloc("maskH", [NKB_H, 64], BF16)
        bgate = A.alloc("bgate", [16], F32)
        convp = A.alloc("convp", [4 * 44], F32)
        flags = A.alloc("flags", [NHALO], F32)

        def load_persistent_consts():
            for t_, s_ in ((ident, ident_c), (Umat, U_c), (Lmat, L_c), (maskM, maskM_c), (maskH, maskH_c),
                           (bgate, bgate_c), (convp, convp_c), (flags, flags_c)):
                dma("sp", t_.ap, s_, [], [t_.buf])

        kT = A.alloc("kT", [4, S], BF16, tracked=False)
        Vt = A.alloc("V", [32, 512], BF16, tracked=False)
        kT_b = [[A.sub(kT, "kT%d_%d" % (fc, t8), fc * S + t8 * 512, fc * S + t8 * 512 + 512)
                 for t8 in range(8)] for fc in range(4)]
        V_b = [A.sub(Vt, "V%d" % k, k * 512, k * 512 + 512) for k in range(32)]
        wk = A.alloc("wk", [8, 512], BF16)
        wv = A.alloc("wv", [8, 512], BF16)
        xa = [A.alloc("xa%d" % k, [8, 512], BF16) for k in range(2)]
        xTa = xT_all.rearrange("(c p) t -> p c t", p=128)
        w_in_r0 = w_in.rearrange("(c p) f -> p c f", p=128)
        dma("pool", wk.ap, w_in_r0[:, :, 512:1024], [], [wk.buf])
        xo = A.alloc("xo", [8, TT], BF16)
        wq = A.alloc("wq", [8, 512], BF16)
        NFRONT = 6 if phases_upto >= 3 else 8
        for t8 in range(NFRONT if phases_upto >= 1 else 0):
            x_ = xa[t8 % 2]
            dma("pool", x_.ap, xTa[:, :, t8 * 512:(t8 + 1) * 512], [], [x_.buf])
            if t8 == 0:
                dma("pool", wv.ap, w_in_r0[:, :, 1024:1536], [], [wv.buf])
            if t8 == 1:
                load_persistent_consts()
            xTo = xT_own.rearrange("(c p) t -> p c t", p=128)
            if 2 <= t8 <= 5:
                s_ = t8 - 2
                dma("pool", xo.ap[:, :, s_ * SL:(s_ + 1) * SL], xTo[:, :, s_ * SL:(s_ + 1) * SL], [], [xo.buf])
            if t8 == 5:
                dma("pool", xo.ap[:, :, T:TT], xTo[:, :, T:TT], [], [xo.buf])
                dma("pool", wq.ap, w_in_r0[:, :, 0:512], [], [wq.buf])
            for fc in range(4):
                ps, psb = next_bank()
                for kc in range(8):
                    mm(psb, ps, wk.ap[:, kc, fc * 128:(fc + 1) * 128], x_.ap[:, kc, :],
                       kc == 0, kc == 7, [wk.buf, x_.buf])
                evac(kT.ap[:, fc, t8 * 512:(t8 + 1) * 512], ps, [psb], [kT_b[fc][t8]])
            for tb in range(4):
                ps, psb = next_bank()
                for kc in range(8):
                    mm(psb, ps, x_.ap[:, kc, tb * 128:(tb + 1) * 128], wv.ap[:, kc, :],
                       kc == 0, kc == 7, [wv.buf, x_.buf])
                evac(Vt.ap[:, t8 * 4 + tb, :], ps, [psb], [V_b[t8 * 4 + tb]])
        A.release(wk, wv, xa[0], xa[1])
        if "kT" in DEBUG:
            dump("kT", kT.ap, [128, 4, S], BF16, [b for r in kT_b for b in r])
            dump("V", Vt.ap, [128, 32, 512], BF16, V_b)

        qT = A.alloc("qT", [4, TT], BF16, tracked=False)
        qT_b = [[A.sub(qT, "q%d_%d" % (fc, si), fc * TT + c0, fc * TT + c0 + n) for si, (c0, n) in enumerate(segs)]
                for fc in range(4)]
        side_jobs = []
        side_bank = [0]

        def q_group_jobs(si, fc, bank=None):
            c0, n = segs[si]
            holder = {}

            def job(kc):
                if kc == 0:
                    holder["b"] = psum[2 + side_bank[0]] if bank is None else bank
                    side_bank[0] ^= 1
                ps, psb = holder["b"]
                mm(psb, ps[:, :n], wq.ap[:, kc, fc * 128:(fc + 1) * 128], xo.ap[:, kc, c0:c0 + n],
                   kc == 0, kc == 7, [wq.buf, xo.buf])
                if kc == 7:
                    ts("dve", qT.ap[:, fc, c0:c0 + n], ps[:, :n], 0.125, None, ALU.mult, None, [psb], [qT_b[fc][si]])
            return [lambda p=p: job(p) for p in range(8)]

        if phases_upto >= 2:
            for fc in range(4):
                for j_ in q_group_jobs(0, fc, bank=next_bank()):
                    j_()
            late_segs = [1, 2, 3, 4] if phases_upto >= 3 else []
            for si in late_segs:
                for fc in range(4):
                    side_jobs.extend(q_group_jobs(si, fc))
            late = {}

            def kv_switch():
                A.release(xo, wq)
                late["wk"] = A.alloc("wk2", [8, 512], BF16)
                late["wv"] = A.alloc("wv2", [8, 512], BF16)
                late["xa"] = [A.alloc("xa2_%d" % k, [8, 512], BF16) for k in range(2)]
                dma("pool", late["wk"].ap, w_in_r0[:, :, 512:1024], [], [late["wk"].buf])
                dma("pool", late["xa"][0].ap, xTa[:, :, 6 * 512:7 * 512], [], [late["xa"][0].buf])
                dma("pool", late["wv"].ap, w_in_r0[:, :, 1024:1536], [], [late["wv"].buf])
                dma("pool", late["xa"][1].ap, xTa[:, :, 7 * 512:8 * 512], [], [late["xa"][1].buf])

            def kv_group_jobs(t8, grp):
                holder = {}

                def job(kc):
                    x_ = late["xa"][t8 % 2]
                    if kc == 0:
                        holder["b"] = psum[2 + side_bank[0]]
                        side_bank[0] ^= 1
                    ps, psb = holder["b"]
                    if grp < 4:
                        fc = grp
                        mm(psb, ps, late["wk"].ap[:, kc, fc * 128:(fc + 1) * 128], x_.ap[:, kc, :],
                           kc == 0, kc == 7, [late["wk"].buf, x_.buf])
                        if kc == 7:
                            cp("dve", kT.ap[:, fc, t8 * 512:(t8 + 1) * 512], ps, [psb], [kT_b[fc][t8]])
                    else:
                        tb = grp - 4
                        mm(psb, ps, x_.ap[:, kc, tb * 128:(tb + 1) * 128], late["wv"].ap[:, kc, :],
                           kc == 0, kc == 7, [late["wv"].buf, x_.buf])
                        if kc == 7:
                            cp("dve", Vt.ap[:, t8 * 4 + tb, :], ps, [psb], [V_b[t8 * 4 + tb]])
                return [lambda p=p: job(p) for p in range(8)]

            def kv_finish():
                A.release(late["wk"], late["wv"], late["xa"][0], late["xa"][1])
                late["xo"] = A.alloc("xo_b", [8, TT], BF16)
                dma("pool", late["xo"].ap, xT_own.rearrange("(c p) t -> p c t", p=128), [], [late["xo"].buf])

            if late_segs and NFRONT < 8:
                side_jobs.append(kv_switch)
                side_jobs.extend([(lambda: None)] * 24)
                for t8_ in range(NFRONT, 8):
                    for grp in range(8):
                        side_jobs.extend(kv_group_jobs(t8_, grp))
                side_jobs.append(kv_finish)
        if phases_upto < 3:
            A.release(wq)
        if "qT" in DEBUG:
            dump("qT", qT.ap, [128, 4, TT], BF16, [b for r in qT_b for b in r])

        ya = A.alloc("ya", [4, TT], BF16, tracked=False)
        ya_pb = [[A.sub(ya, "ya%d_%d" % (hp, si), hp * TT + c0, hp * TT + c0 + n) for si, (c0, n) in enumerate(segs)]
                 for hp in range(4)]
        ya_b = [ya_pb[h // 2] for h in range(8)]
        NE = 4
        e3 = [A.alloc("e3_%d" % k, [1024], F32) for k in range(NE)]
        qpad = [[A.alloc("qpad%d%d" % (b_, par), [SL], BF16) for par in range(2)] for b_ in range(2)]
        qh = A.alloc("qh", [8, NHALO], BF16)
        for b_ in range(2):
            for par in range(2):
                P.op("pool", lambda e, t=qpad[b_][par]: e.memset(t.ap, 0.0), reads=[], writes=[qpad[b_][par].buf], nfree=SL)
        P.op("pool", lambda e: e.memset(qh.ap, 0.0), reads=[], writes=[qh.buf], nfree=64)
        Zmat = A.alloc("Zmat", [128], BF16)
        P.op("pool", lambda e: e.memset(Zmat.ap, 0.0), reads=[], writes=[Zmat.buf], nfree=128)
        sp2 = [A.alloc("sp2_%d" % k, [1024], BF16) for k in range(2)]
        ex2 = [A.alloc("ex2_%d" % k, [1024], F32) for k in range(2)]
        w2 = [A.alloc("w2_%d" % k, [1024], BF16) for k in range(2)]

        def attn_stream(steps):
            N = len(steps)
            ZB = 1

            def nch(i):
                return len(steps[i]["chains"])

            def TW(i):
                return 1024 if nch(i) == 2 else steps[i]["chains"][0]["W"]

            def CS(i):
                return steps[i].get("cs", 0)

            def vw(ap, i):
                if nch(i) == 1:
                    return ap[:, :TW(i)]
                cs = CS(i)
                if cs == 0:
                    return ap[:, :1024]
                return ap[:, :1024].rearrange("p (a b) -> p a b", a=2)[:, :, cs:]

            def Z(i):
                s = steps[i]
                cs = CS(i)
                if s.get("pre") is not None:
                    s["pre"]()
                for ci, ch in enumerate(s["chains"]):
                    ps, psb = psum[2 * (i % ZB) + ci]
                    for (h, q_ap, q_buf, n, col0) in ch["groups"]:
                        mm(psb, ps[:, col0 + cs:col0 + n], kT.ap[:, h // 2, s["kb"] * 128:(s["kb"] + 1) * 128],
                           q_ap[:, cs:n], True, True, [kT_b[h // 2][s["kb"] // 4], q_buf])

            def EXP(i):
                p = i % ZB
                act(vw(e3[i % NE].ap, i), vw(psall[:, 1024 * p:1024 * p + 1024], i), AF.Exp,
                    [psum[2 * p + ci][1] for ci in range(nch(i))], [e3[i % NE].buf])

            def MASK(i):
                m = steps[i]["mask"]
                if m is None:
                    return
                cs = CS(i)
                for ci, ch in enumerate(steps[i]["chains"]):
                    W = ch["W"]
                    ev = e3[i % NE].ap[:, ci * 512 + cs:ci * 512 + W]
                    tt("dve", ev, ev, m[0][:, cs:W], ALU.mult, [e3[i % NE].buf, m[1]], [e3[i % NE].buf])

            def LN(i):
                act(vw(sp2[i % 2].ap, i), vw(e3[i % NE].ap, i), AF.Ln, [e3[i % NE].buf], [sp2[i % 2].buf], bias=1.0)

            def ZERO(i, base):
                s = steps[i]
                for ci, ch in enumerate(s["chains"]):
                    W = ch["W"]
                    pz, pzb = psum[base + ci]
                    mm(pzb, pz[:, :W], Zmat.ap, Umat.ap[:, :W] if W <= 128 else maskM.ap[:, 0, :W], True, False,
                       [Zmat.buf, Umat.buf, maskM.buf])

            def U(i):
                s = steps[i]
                cs = CS(i)
                if s["first"]:
                    ZERO(i, 4)
                for ci, ch in enumerate(s["chains"]):
                    W = ch["W"]
                    pc, pcb = psum[4 + ci]
                    mm(pcb, pc[:, cs:W], Umat.ap, sp2[i % 2].ap[:, ci * 512 + cs:ci * 512 + W], False, s["last"],
                       [Umat.buf, sp2[i % 2].buf])

            def EXPC(i):
                act(vw(ex2[i % 2].ap, i), vw(psall[:, 2048:3072], i), AF.Exp,
                    [psum[4 + ci][1] for ci in range(nch(i))], [ex2[i % 2].buf], scale=-1.0)

            def L(i):
                s = steps[i]
                cs = CS(i)
                if s["last"]:
                    return
                for ci, ch in enumerate(s["chains"]):
                    W = ch["W"]
                    pc, pcb = psum[4 + ci]
                    mm(pcb, pc[:, cs:W], Lmat.ap, sp2[i % 2].ap[:, ci * 512 + cs:ci * 512 + W], False, False,
                       [Lmat.buf, sp2[i % 2].buf])

            def WW(i):
                tt("dve", vw(w2[i % 2].ap, i), vw(ex2[i % 2].ap, i), vw(e3[i % NE].ap, i), ALU.mult,
                   [ex2[i % 2].buf, e3[i % NE].buf], [w2[i % 2].buf])

            def PV(i):
                s = steps[i]
                cs = CS(i)
                if s["first"]:
                    ZERO(i, 6)
                for ci, ch in enumerate(s["chains"]):
                    po, pob = psum[6 + ci]
                    for (h, q_ap, q_buf, n, col0) in ch["groups"]:
                        mm(pob, po[:, col0 + cs:col0 + n], Vt.ap[:, s["kb"], (h // 2) * 128:(h // 2 + 1) * 128],
                           w2[i % 2].ap[:, ci * 512 + col0 + cs:ci * 512 + col0 + n], False, s["last"],
                           [V_b[s["kb"]], w2[i % 2].buf])
                if s["last"]:
                    for ci, ch in enumerate(s["chains"]):
                        po, pob = psum[6 + ci]
                        for (h, si, c0, n, col0) in ch["out"]:
                            pr = (h % 2) * 64
                            cp("dve", ya.ap[pr:pr + 64, h // 2, c0:c0 + n], po[pr:pr + 64, col0:col0 + n], [pob], [ya_b[h][si]])

            Z(0)
            EXP(0)
            MASK(0)
            Z(1)
            EXP(1)
            MASK(1)
            Z(2)
            LN(0)
            U(0)
            for i in range(N):
                if i + 2 < N:
                    EXP(i + 2)
                if i + 3 < N:
                    Z(i + 3)
                if side_jobs:
                    side_jobs.pop(0)()
                if i + 2 < N:
                    MASK(i + 2)
                EXPC(i)
                WW(i)
                L(i)
                if side_jobs:
                    side_jobs.pop(0)()
                if i + 1 < N:
                    LN(i + 1)
                    U(i + 1)
                PV(i)

        if phases_upto >= 3:
            steps = []
            gidx = 0
            for i in range(NSLOT):
                nkb = 8 * (i + 1)
                for hp in range(4):
                    b_ = gidx % 2
                    gidx += 1
                    chains = []
                    for par in range(2):
                        h = 2 * hp + par
                        chains.append(dict(groups=[(h, qpad[b_][par].ap, qpad[b_][par].buf, SL, 0)], W=SL,
                                           out=[(h, i, i * SL, SL, 0)]))

                    def pre(i=i, hp=hp, b_=b_):
                        for par in range(2):
                            pr = par * 64
                            cp("dve", qpad[b_][par].ap[pr:pr + 64, :], qT.ap[pr:pr + 64, hp, i * SL:(i + 1) * SL],
                               [qT_b[hp][i]], [qpad[b_][par].buf])

                    for kb in range(nkb - 1, -1, -1):
                        steps.append(dict(chains=chains, kb=kb, first=(kb == nkb - 1), last=(kb == 0),
                                          pre=(pre if kb == nkb - 1 else None),
                                          cs=(128 * max(0, kb - (nkb - 8) - 4) if not DEBUG.get("notrim") else 0),
                                          mask=((maskM.ap[:, kb - (nkb - 8), :], maskM.buf) if kb >= nkb - 8 else None)))
            halo_chain = dict(groups=[(h, qh.ap[:, h, :], qh.buf, NHALO, h * NHALO) for h in range(8)], W=64,
                              out=[(h, 4, T, NHALO, h * NHALO) for h in range(8)])

            def pre_h():
                for par in range(2):
                    pr = par * 64
                    cp("dve", qh.ap[pr:pr + 64].rearrange("p (a two) b -> p a two b", two=2)[:, :, par, :],
                       qT.ap[pr:pr + 64, :, T:TT], [qT_b[fc][4] for fc in range(4)], [qh.buf])

            early = {}

            def post_q():
                A.release(qT)
                early["qT_released"] = True
                if phases_upto >= 4:
                    early["wuv"] = A.alloc("wuv", [8, 1024], BF16)
                    dma("pool", early["wuv"].ap, w_in.rearrange("(c p) f -> p c f", p=128)[:, :, 1536:2560], [],
                        [early["wuv"].buf])

            ih = 4 * 8 + 4 * 16
            slot1_pre = steps[ih]["pre"]
            steps[ih]["pre"] = lambda: (pre_h(), slot1_pre())
            for kb in range(NKB_H - 1, -1, -1):
                steps.append(dict(chains=[halo_chain], kb=kb, first=(kb == NKB_H - 1), last=(kb == 0),
                                  pre=(post_q if kb == NKB_H - 1 else None),
                                  mask=(maskH.ap[:, kb, :], maskH.buf)))
            attn_stream(steps)
            assert not side_jobs
            if "xo" in late:
                xo = late["xo"]
            else:
                A.release(wq)
        A.release(*e3, *sp2, *ex2, *w2, qh, *qpad[0], *qpad[1], Zmat)
        if phases_upto >= 3 and early.get("qT_released"):
            A.release(kT, Vt)
        else:
            A.release(kT, Vt, qT)
        if "ya" in DEBUG:
            dump("ya", ya.ap, [128, 4, TT], BF16, [b for r in ya_pb for b in r])

        yb = A.alloc("yb", [4, TT], BF16, tracked=False)
        yb_pb = [[A.sub(yb, "yb%d_%d" % (gp, si), gp * TT + c0, gp * TT + c0 + n) for si, (c0, n) in enumerate(segs)]
                 for gp in range(4)]
        if phases_upto >= 5:
            wg = [A.alloc("wg%d" % k, [2, 8, 128], BF16, top=True) for k in range(2)]
            wb = [A.alloc("wb%d" % k, [2, 4, 128], BF16, top=True) for k in range(2)]
            w_in_r = w_in.rearrange("(c p) f -> p c f", p=128)

            def load_wgb(fc):
                k = fc % 2
                dma("pool", wg[k].ap[:, 0], w_in_r[:, :, 2560 + fc * 128:2560 + (fc + 1) * 128], [], [wg[k].buf])
                dma("pool", wg[k].ap[:, 1], w_in_r[:, :, 3584 + fc * 128:3584 + (fc + 1) * 128], [], [wg[k].buf])
                dma("pool", wb[k].ap[:, 0], w_bra.rearrange("(c p) f -> p c f", p=128)[:, :, fc * 128:(fc + 1) * 128],
                    [], [wb[k].buf])
                dma("pool", wb[k].ap[:, 1], w_brb.rearrange("(c p) f -> p c f", p=128)[:, :, fc * 128:(fc + 1) * 128],
                    [], [wb[k].buf])
        if phases_upto >= 4:
            if phases_upto >= 3 and "wuv" in early:
                wuv = early["wuv"]
            else:
                wuv = A.alloc("wuv", [8, 1024], BF16)
                dma("pool", wuv.ap, w_in.rearrange("(c p) f -> p c f", p=128)[:, :, 1536:2560], [], [wuv.buf])
            xhb = A.alloc("xhb", [8, 512], BF16)
            dma("pool", xhb.ap, xT_hb.rearrange("(c p) t -> p c t", p=128), [], [xhb.buf])
            lg = load_const("lnsg_g", lnsg_g, [512], F32)
            lb = load_const("lnsg_b", lnsg_b, [512], F32)
            bsp = load_const("bsp", bsp_c, [4, 128], F32)
            wsf = load_const("wsf", wsT, [8, 128], F32)
            tril = load_const("tril", trilT, [128], F32)
            wsb = A.alloc("wsb", [8, 128], BF16)
            for g in range(8):
                tt("dve", wsb.ap[:, g, :], wsf.ap[:, g, :], tril.ap, ALU.mult, [wsf.buf, tril.buf], [wsb.buf])
            if phases_upto >= 5:
                load_wgb(0)
            ug2 = [A.alloc("ug_%d" % k, [4, SL], F32, tracked=False) for k in range(2)]
            ug_b2 = [[A.sub(ug2[k], "ug%d_%d" % (k, g), g * SL, (g + 1) * SL) for g in range(4)] for k in range(2)]
            gt = [A.alloc("gt%d" % k, [512], F32) for k in range(2)]
            vg2 = [[A.alloc("vg%d_%d" % (k, b_), [512], F32) for b_ in range(4)] for k in range(2)]
            vn = [A.alloc("vn%d" % k, [512], BF16) for k in range(4)]
            st2 = [(A.alloc("st6_%d" % k, [4, 6], F32), A.alloc("mv_%d" % k, [4, 2], F32), A.alloc("rs_%d" % k, [4], F32))
                   for k in range(2)]

            def D_proj(si):
                c0, n = segs[si]
                ug, ug_b, vg = ug2[si % 2], ug_b2[si % 2], vg2[si % 2]
                stt6, mv, rs = st2[si % 2]
                for g in range(4):
                    ps, psb = next_bank()
                    for kc in range(8):
                        mm(psb, ps[:, :n], wuv.ap[:, kc, g * 128:(g + 1) * 128], xo.ap[:, kc, c0:c0 + n],
                           kc == 0, kc == 7, [wuv.buf, xo.buf])
                    act(ug.ap[:, g, :n], ps[:, :n], AF.Gelu_apprx_tanh, [psb], [ug_b[g]])
                for bi in range(4):
                    xsrc, xbuf, col0 = (xo.ap, xo.buf, c0 + bi * 128) if si < 4 else (xhb.ap, xhb.buf, bi * 128)
                    ps, psb = next_bank()
                    for kc in range(8):
                        mm(psb, ps, xsrc[:, kc, col0:col0 + 128], wuv.ap[:, kc, 512:1024], kc == 0, kc == 7,
                           [xbuf, wuv.buf])
                    act(vg[bi].ap, ps, AF.Gelu_apprx_tanh, [psb], [vg[bi].buf])
                    P.op("dve", lambda e, bi=bi: e.bn_stats(stt6.ap[:, bi, :], vg[bi].ap), reads=[vg[bi].buf],
                         writes=[stt6.buf])
                    P.op("dve", lambda e, bi=bi: e.bn_aggr(mv.ap[:, bi, :], stt6.ap[:, bi, :]), reads=[stt6.buf],
                         writes=[mv.buf])
                act(rs.ap, mv.ap[:, :, 1], AF.Ln, [mv.buf], [rs.buf], bias=LN_EPS)
                act(rs.ap, rs.ap, AF.Exp, [rs.buf], [rs.buf], scale=-0.5)

            def D_ln(si):
                vg = vg2[si % 2]
                stt6, mv, rs = st2[si % 2]
                for bi in range(4):
                    stt("dve", vg[bi].ap, vg[bi].ap, mv.ap[:, bi, 0:1], lg.ap, ALU.subtract, ALU.mult,
                        [vg[bi].buf, mv.buf, lg.buf], [vg[bi].buf])
                    stt("dve", vn[bi].ap, vg[bi].ap, rs.ap[:, bi:bi + 1], lb.ap, ALU.mult, ALU.add,
                        [vg[bi].buf, rs.buf, lb.buf], [vn[bi].buf])

            def D_mix(si):
                c0, n = segs[si]
                ug, ug_b, vg = ug2[si % 2], ug_b2[si % 2], vg2[si % 2]
                for bi in range(4):
                    k = bi
                    t0, tn = (0, 128) if si < 4 else (126, 2)
                    for half in range(2):
                        ps, psb = next_bank()
                        for gg in range(4):
                            g = half * 4 + gg
                            mm(psb, ps[:, gg * tn:(gg + 1) * tn], vn[k].ap[:, (g // 2) * 128:(g // 2 + 1) * 128],
                               wsb.ap[:, g, t0:t0 + tn], True, True, [vn[k].buf, wsb.buf])
                        t1 = gt[half]
                        p0 = half * 2
                        psv = ps[:, :4 * tn].rearrange("p (a two b) -> p a two b", two=2, b=tn)
                        t1v = t1.ap[:, :2 * tn].rearrange("p (a b) -> p a b", a=2)
                        for par in range(2):
                            pr = par * 64
                            tt("dve", t1v[pr:pr + 64], psv[pr:pr + 64, :, par, :], bsp.ap[pr:pr + 64, p0:p0 + 2, t0:t0 + tn],
                               ALU.add, [psb, bsp.buf], [t1.buf])
                        tt("dve", yb.ap[:, p0:p0 + 2, c0 + bi * tn:c0 + (bi + 1) * tn], t1v,
                           ug.ap[:, p0:p0 + 2, bi * tn:(bi + 1) * tn], ALU.mult,
                           [t1.buf] + [ug_b[p0 + q] for q in range(2)], [yb_pb[p0 + q][si] for q in range(2)])

            D_proj(0)
            for si in range(len(segs)):
                D_ln(si)
                if si + 1 < len(segs):
                    D_proj(si + 1)
                D_mix(si)
            ug = ug2[0]
            vg = vg2[0] + vg2[1] + [ug2[1]]
            st6 = [t for tri in st2 for t in tri]
            A.release(wuv, xhb, lg, lb, bsp, wsf, tril, wsb, ug, *gt, *vg, *vn, *st6)
        if "yb" in DEBUG:
            dump("yb", yb.ap, [128, 4, TT], BF16, [b for r in yb_pb for b in r])

        mg = A.alloc("mg", [8, TT], BF16, tracked=False)
        mg_b = [[A.sub(mg, "mg%d_%d" % (fc, si), fc * TT + c0, fc * TT + c0 + n) for si, (c0, n) in enumerate(segs)]
                for fc in range(8)]
        if phases_upto >= 6:
            wo = A.alloc("wo", [8, D], BF16)
            dma("pool", wo.ap, w_out.rearrange("(c p) f -> p c f", p=128), [], [wo.buf])
            pb_out = load_const("b_out", rowp["b_out"], [D], F32)
            pg1 = load_const("ln1_g", rowp["ln1_g"], [D], F32)
            pb1 = load_const("ln1_b", rowp["ln1_b"], [D], F32)
        if phases_upto >= 5:
            ga = [A.alloc("ga%d" % k, [512], F32) for k in range(2)]
            gb = [A.alloc("gb%d" % k, [512], F32) for k in range(2)]
            it = 0
            for fc in range(8):
                k = fc % 2
                if fc + 1 < 8:
                    load_wgb(fc + 1)
                for si, (c0, n) in enumerate(segs):
                    kk = it % 2
                    it += 1
                    pga, pgb, pba, pbb = [next_bank() for _ in range(4)]
                    for kc in range(8):
                        mm(pga[1], pga[0][:, :n], wg[k].ap[:, 0, kc, :], xo.ap[:, kc, c0:c0 + n], kc == 0, kc == 7,
                           [wg[k].buf, xo.buf])
                    for kc in range(8):
                        mm(pgb[1], pgb[0][:, :n], wg[k].ap[:, 1, kc, :], xo.ap[:, kc, c0:c0 + n], kc == 0, kc == 7,
                           [wg[k].buf, xo.buf])
                    for hp in range(4):
                        mm(pba[1], pba[0][:, :n], wb[k].ap[:, 0, hp, :], ya.ap[:, hp, c0:c0 + n], hp == 0, hp == 3,
                           [wb[k].buf, ya_pb[hp][si]])
                    for hp in range(4):
                        mm(pbb[1], pbb[0][:, :n], wb[k].ap[:, 1, hp, :], yb.ap[:, hp, c0:c0 + n], hp == 0, hp == 3,
                           [wb[k].buf, yb_pb[hp][si]])
                    act(ga[kk].ap[:, :n], pga[0][:, :n], AF.Sigmoid, [pga[1], bgate.buf], [ga[kk].buf],
                        bias=bgate.ap[:, fc:fc + 1])
                    act(gb[kk].ap[:, :n], pgb[0][:, :n], AF.Sigmoid, [pgb[1], bgate.buf], [gb[kk].buf],
                        bias=bgate.ap[:, 8 + fc:9 + fc])
                    tt("dve", ga[kk].ap[:, :n], ga[kk].ap[:, :n], pba[0][:, :n], ALU.mult, [ga[kk].buf, pba[1]], [ga[kk].buf])
                    tt("dve", gb[kk].ap[:, :n], gb[kk].ap[:, :n], pbb[0][:, :n], ALU.mult, [gb[kk].buf, pbb[1]], [gb[kk].buf])
                    tt("dve", mg.ap[:, fc, c0:c0 + n], ga[kk].ap[:, :n], gb[kk].ap[:, :n], ALU.add,
                       [ga[kk].buf, gb[kk].buf], [mg_b[fc][si]])
            A.release(*wg, *wb, *ga, *gb)
        A.release(xo, ya, yb)
        if "mg" in DEBUG:
            dump("mg", mg.ap, [128, 8, TT], BF16, [b for r in mg_b for b in r])

        h1T = A.alloc("h1T", [8, TT], BF16, tracked=False, top=True)
        blocks = [(r * 128, 128) for r in range(16)] + [(T, NHALO)]
        h1T_b = [[A.sub(h1T, "h1T%d_%d" % (kc, r), kc * TT + r0, kc * TT + r0 + nr) for r, (r0, nr) in enumerate(blocks)]
                 for kc in range(8)]

        def ln_head(r_t, nr, st_t):
            for hlf in range(2):
                P.op("dve", lambda e, hlf=hlf: e.bn_stats(st_t.ap[:nr, hlf * 6:hlf * 6 + 6],
                                                         r_t.ap[:nr, hlf * 512:(hlf + 1) * 512]),
                     reads=[r_t.buf], writes=[st_t.buf])
            P.op("dve", lambda e: e.bn_aggr(st_t.ap[:nr, 12:14], st_t.ap[:nr, 0:12]), reads=[st_t.buf], writes=[st_t.buf])
            act(st_t.ap[:nr, 13:14], st_t.ap[:nr, 13:14], AF.Ln, [st_t.buf], [st_t.buf], bias=LN_EPS)
            act(st_t.ap[:nr, 13:14], st_t.ap[:nr, 13:14], AF.Exp, [st_t.buf], [st_t.buf], scale=-0.5)

        def ln_tail(r_t, nr, g_t, b_t, st_t, out_ap, out_bufs):
            stt("dve", r_t.ap[:nr], r_t.ap[:nr], st_t.ap[:nr, 12:13], g_t.ap[:nr], ALU.subtract, ALU.mult,
                [r_t.buf, st_t.buf, g_t.buf], [r_t.buf])
            stt("dve", out_ap, r_t.ap[:nr], st_t.ap[:nr, 13:14], b_t.ap[:nr], ALU.mult, ALU.add,
                [r_t.buf, st_t.buf, b_t.buf], out_bufs)

        if phases_upto >= 7:
            wu = [A.alloc("wu%d" % k, [2, 8, 256], BF16, top=True) for k in range(2)]
            w_up_r = w_up.rearrange("(c p) f -> p c f", p=128)

            def load_wu(jj):
                k = jj % 2
                dma("pool", wu[k].ap[:, 0], w_up_r[:, :, jj * 256:(jj + 1) * 256], [], [wu[k].buf])
                dma("pool", wu[k].ap[:, 1], w_up_r[:, :, DFF + jj * 256:DFF + (jj + 1) * 256], [], [wu[k].buf])

            load_wu(0)
        if phases_upto >= 6:
            NB3 = 3
            xb = [A.alloc("xb%d" % k, [D], F32) for k in range(NB3)]
            rr = [A.alloc("rr%d" % k, [D], F32) for k in range(NB3)]
            hh = [A.alloc("hh%d" % k, [D], F32) for k in range(2)]
            hb = [A.alloc("hb%d" % k, [D], BF16) for k in range(2)]
            stt_ = [A.alloc("stF%d" % k, [16], F32) for k in range(NB3)]

            ones1 = A.alloc("ones1", [128], BF16)
            bhi = A.alloc("bhi", [D], BF16)
            blo = A.alloc("blo", [D], BF16)
            P.op("dve", lambda e: e.memset(ones1.ap[0:1], 1.0), reads=[], writes=[ones1.buf], nfree=128)
            cp("dve", bhi.ap[0:1], pb_out.ap[0:1], [pb_out.buf], [bhi.buf])
            tt("dve", blo.ap[0:1], pb_out.ap[0:1], bhi.ap[0:1], ALU.subtract, [pb_out.buf, bhi.buf], [blo.buf])
            Fps = {}

            def F_mm(r):
                r0, nr = blocks[r]
                k = r % NB3
                dma("sp", xb[k].ap[:nr], x_own[r0:r0 + nr, :], [], [xb[k].buf])
                Fps[r] = []
                for hlf in range(2):
                    ps, psb = next_bank()
                    Fps[r].append((ps, psb))
                    for kc in range(8):
                        mm(psb, ps[:nr, :], mg.ap[:, kc, r0:r0 + nr], wo.ap[:, kc, hlf * 512:(hlf + 1) * 512],
                           kc == 0, False, [mg_b[kc][min(r // 4, 4)], wo.buf])
                    mm(psb, ps[:nr, :], ones1.ap[0:1, :nr], bhi.ap[0:1, hlf * 512:(hlf + 1) * 512], False, False,
                       [ones1.buf, bhi.buf])
                    mm(psb, ps[:nr, :], ones1.ap[0:1, :nr], blo.ap[0:1, hlf * 512:(hlf + 1) * 512], False, True,
                       [ones1.buf, blo.buf])

            def F_head(r):
                r0, nr = blocks[r]
                k = r % NB3
                for hlf in range(2):
                    ps, psb = Fps[r][hlf]
                    stt("dve", rr[k].ap[:nr, hlf * 512:(hlf + 1) * 512], xb[k].ap[:nr, hlf * 512:(hlf + 1) * 512], ALPHA,
                        ps[:nr, :], ALU.mult, ALU.add, [xb[k].buf, psb], [rr[k].buf])
                ln_head(rr[k], nr, stt_[k])

            def F_tail(r):
                r0, nr = blocks[r]
                k, k2 = r % NB3, r % 2
                ln_tail(rr[k], nr, pg1, pb1, stt_[k], hh[k2].ap[:nr], [hh[k2].buf])
                dma("sp", h1s[r0:r0 + nr, :], hh[k2].ap[:nr], [hh[k2].buf], [h1s_buf[r]])
                cp("act", hb[k2].ap[:nr], hh[k2].ap[:nr], [hh[k2].buf], [hb[k2].buf])

            def F_tr(r):
                r0, nr = blocks[r]
                k2 = r % 2
                for grp in range(2):
                    ps, psb = next_bank()
                    psv = ps.bitcast(BF16)
                    for q4 in range(4):
                        kc = grp * 4 + q4
                        P.op("pe", lambda e, kc=kc, q4=q4, psv=psv, k2=k2, nr=nr: e.transpose(
                            psv[:, q4 * 128:q4 * 128 + nr], hb[k2].ap[:nr, kc * 128:(kc + 1) * 128], ident.ap[:nr, :nr]),
                            reads=[hb[k2].buf, ident.buf], writes=[psb])
                    src_v = psv[:, 0:512].rearrange("p (a b) -> p a b", a=4)[:, :, :nr]
                    cp("act", h1T.ap[:, grp * 4:(grp + 1) * 4, r0:r0 + nr], src_v, [psb],
                                        [h1T_b[grp * 4 + q][r] for q in range(4)])

            nblk = len(blocks)
            F_mm(0)
            F_mm(1)
            F_head(0)
            for r in range(nblk):
                if r + 2 < nblk:
                    F_mm(r + 2)
                F_tail(r)
                if r + 1 < nblk:
                    F_head(r + 1)
                F_tr(r)
            A.release(wo, pb_out, pg1, pb1, *xb, *rr, *hh, *hb, *stt_, ones1, bhi, blo)
        A.release(mg)
        if "h1T" in DEBUG:
            dump("h1T", h1T.ap, [128, 8, TT], BF16, [b for r in h1T_b for b in r])

        actT = A.alloc("actT", [NCH, T], BF16, tracked=False)
        act_b = [[A.sub(actT, "act%d_%d" % (j, i), j * T + i * SL, j * T + (i + 1) * SL) for i in range(NSLOT)]
                 for j in range(NCH)]
        if phases_upto >= 8:
            wdA = A.alloc("wdA", [NCH // 2, D], BF16)
        if phases_upto >= 7:
            uh = A.alloc("uh", [44, NHALO], F32, tracked=False)
            bnd = A.alloc("bnd", [44, NHALO], F32, tracked=False)
            uh_b = [A.sub(uh, "uh%d" % c, c * NHALO, (c + 1) * NHALO) for c in range(44)]
            bnd_b = [A.sub(bnd, "bnd%d" % c, c * NHALO, (c + 1) * NHALO) for c in range(44)]
            cv = [[A.alloc("cv%d%d" % (s_, k), [SL], F32) for k in range(2)] for s_ in range(2)]
            sa = [A.alloc("sa%d" % k, [SL], F32) for k in range(2)]
            it = 0
            if phases_upto >= 8:
                dma("pool", wdA.ap, w_down.rearrange("(c p) f -> p c f", p=128)[:, 0:NCH // 2, :], [], [wdA.buf])
            for jj in range(NCH // 2):
                k = jj % 2
                if jj + 1 < NCH // 2:
                    load_wu(jj + 1)
                for sub in range(2):
                    j = 2 * jj + sub
                    for s_ in range(2):
                        c = s_ * NCH + j
                        ps, psb = next_bank()
                        for kc in range(8):
                            mm(psb, ps[:, :NHALO], wu[k].ap[:, s_, kc, sub * 128:(sub + 1) * 128],
                               h1T.ap[:, kc, T:TT], kc == 0, kc == 7, [wu[k].buf, h1T_b[kc][16]])
                        tt("dve", uh.ap[:, c, :], ps[:, :NHALO], flags.ap, ALU.mult, [psb, flags.buf], [uh_b[c]])
                        uv = uh.ap[:, c, :].rearrange("p (a b) -> p a b", b=2)
                        bv_ = bnd.ap[:, c, :].rearrange("p (a b) -> p a b", b=2)
                        ts("dve", bv_[:, :, 1:2], uv[:, :, 1:2], convp.ap[:, c:c + 1], None, ALU.mult, None,
                           [uh_b[c], convp.buf], [bnd_b[c]])
                        ts("dve", bv_[:, :, 0:1], uv[:, :, 1:2], convp.ap[:, 44 + c:45 + c], None, ALU.mult, None,
                           [uh_b[c], convp.buf], [bnd_b[c]])
                        stt("dve", bv_[:, :, 0:1], uv[:, :, 0:1], convp.ap[:, c:c + 1], bv_[:, :, 0:1], ALU.mult, ALU.add,
                            [uh_b[c], convp.buf, bnd_b[c]], [bnd_b[c]])
                for i in range(NSLOT):
                    for sub in range(2):
                        j = 2 * jj + sub
                        kk = it % 2
                        it += 1
                        for s_ in range(2):
                            c = s_ * NCH + j
                            ps, psb = next_bank()
                            for kc in range(8):
                                mm(psb, ps, wu[k].ap[:, s_, kc, sub * 128:(sub + 1) * 128],
                                   h1T.ap[:, kc, i * SL:(i + 1) * SL], kc == 0, kc == 7,
                                   [wu[k].buf] + [h1T_b[kc][4 * i + q] for q in range(4)])
                            c_ = cv[s_][kk]
                            act(c_.ap, ps, AF.Identity, [psb, convp.buf], [c_.buf],
                                bias=convp.ap[:, 132 + c:133 + c], scale=convp.ap[:, 88 + c:89 + c])
                            stt("dve", c_.ap[:, 1:SL], ps[:, 0:SL - 1], convp.ap[:, 44 + c:45 + c], c_.ap[:, 1:SL],
                                ALU.mult, ALU.add, [psb, convp.buf, c_.buf], [c_.buf])
                            stt("dve", c_.ap[:, 2:SL], ps[:, 0:SL - 2], convp.ap[:, c:c + 1], c_.ap[:, 2:SL],
                                ALU.mult, ALU.add, [psb, convp.buf, c_.buf], [c_.buf])
                            tt("dve", c_.ap[:, 0:2], c_.ap[:, 0:2], bnd.ap[:, c, 2 * i:2 * i + 2], ALU.add,
                               [c_.buf, bnd_b[c]], [c_.buf])
                        ca, cbv = cv[0][kk], cv[1][kk]
                        act(sa[kk].ap, ca.ap, AF.Silu, [ca.buf], [sa[kk].buf])
                        tt("pool", actT.ap[:, j, i * SL:(i + 1) * SL], sa[kk].ap, cbv.ap, ALU.mult, [sa[kk].buf, cbv.buf],
                           [act_b[j][i]])
            A.release(*wu, uh, bnd, *cv[0], *cv[1], *sa)
        A.release(h1T)
        if "act" in DEBUG:
            dump("act", actT.ap, [128, NCH, T], BF16, [b for r in act_b for b in r])

        if phases_upto >= 8:
            wdB = A.alloc("wdB", [NCH // 2, D], BF16)
            dma("pool", wdB.ap, w_down.rearrange("(c p) f -> p c f", p=128)[:, NCH // 2:NCH, :], [], [wdB.buf])
            pb_dn = load_const("b_down", rowp["b_down"], [D], F32)
            pg2 = load_const("ln2_g", rowp["ln2_g"], [D], F32)
            pb2 = load_const("ln2_b", rowp["ln2_b"], [D], F32)
            NB3 = 3
            hr = [A.alloc("hr%d" % k, [D], F32) for k in range(NB3)]
            rr = [A.alloc("rr%d" % k, [D], F32) for k in range(NB3)]
            oo = [A.alloc("oo%d" % k, [D], F32) for k in range(2)]
            stt_ = [A.alloc("stG%d" % k, [16], F32) for k in range(NB3)]

            def G_load(r):
                dma("sp", hr[r % NB3].ap, h1s[r * 128:(r + 1) * 128, :], [h1s_buf[r]], [hr[r % NB3].buf])

            Gps = {}

            def G_mm_a(r):
                r0 = r * 128
                Gps[r] = []
                for hlf in range(2):
                    ps, psb = next_bank()
                    Gps[r].append((ps, psb))
                    for j in range(NCH // 2):
                        mm(psb, ps, actT.ap[:, j, r0:r0 + 128], wdA.ap[:, j, hlf * 512:(hlf + 1) * 512],
                           j == 0, False, [act_b[j][r // 4], wdA.buf])

            def G_mm_b(r):
                k = r % NB3
                r0 = r * 128
                for hlf in range(2):
                    ps, psb = Gps[r][hlf]
                    for j in range(NCH // 2, NCH):
                        mm(psb, ps, actT.ap[:, j, r0:r0 + 128], wdB.ap[:, j - NCH // 2, hlf * 512:(hlf + 1) * 512],
                           False, j == NCH - 1, [act_b[j][r // 4], wdB.buf])
                    tt("dve", rr[k].ap[:, hlf * 512:(hlf + 1) * 512], ps, pb_dn.ap[:, hlf * 512:(hlf + 1) * 512],
                       ALU.add, [psb, pb_dn.buf], [rr[k].buf])
                stt("dve", rr[k].ap, hr[k].ap, ALPHA, rr[k].ap, ALU.mult, ALU.add, [hr[k].buf, rr[k].buf], [rr[k].buf])
                ln_head(rr[k], 128, stt_[k])

            def G_tail(r):
                k, k2 = r % NB3, r % 2
                ln_tail(rr[k], 128, pg2, pb2, stt_[k], oo[k2].ap, [oo[k2].buf])
                out_stores.append(dma("sp", out_d[r * 128:(r + 1) * 128, :], oo[k2].ap, [oo[k2].buf], [P.buf("out%d" % r)]))

            G_load(0)
            G_load(1)
            NPRE = 4
            for r in range(NPRE):
                G_mm_a(r)
            for r in range(16):
                if r + 2 < 16:
                    G_load(r + 2)
                if r >= NPRE:
                    G_mm_a(r)
                G_mm_b(r)
                if r >= 1:
                    G_tail(r - 1)
            G_tail(15)
        else:
            z_t = A.alloc("zt", [D], F32)
            P.op("dve", lambda e: e.memset(z_t.ap, 0.0), reads=[], writes=[z_t.buf])
            for r in range(16):
                out_stores.append(dma("sp", out_d[r * 128:(r + 1) * 128, :], z_t.ap, [z_t.buf], [P.buf("out%d" % r)]))

        fence_bufs = []
        fin = Op()
        fin.eng, fin.dma, fin.flag, fin.sig, fin.idx = "sp", False, False, None, P.n
        fin.deps = list(out_stores) + list(dbg_outs.values())
        fin.fn = lambda e: e.nop()
        P.ops["sp"].append(fin)

        P.emit(nc)
    return nc, list(dbg_outs.keys())


def _bf16(a):
    return np.asarray(a, dtype=np.float32).astype(ml_dtypes.bfloat16)


def make_in_maps(x, w_in, b_gate, ln_sg_g, ln_sg_b, w_spatial, b_spatial, w_branch_a, w_branch_b, w_out, b_out,
                 ln1_g, ln1_b, w_up, conv_w, conv_b, w_down, b_down, ln2_g, ln2_b):
    f = lambda a: np.ascontiguousarray(np.asarray(a, dtype=np.float32))
    x = f(x)
    common = {
        "w_in": f(w_in[0]), "w_bra": f(w_branch_a[0]), "w_brb": f(w_branch_b[0]), "w_out": f(w_out[0]),
        "w_up": f(w_up[0]), "w_down": f(w_down[0]),
        "bgate_c": f(np.asarray(b_gate[0]).reshape(16, 128).T),
        "lnsg_g": f(np.broadcast_to(np.asarray(ln_sg_g[0])[None, :], (128, 512))),
        "lnsg_b": f(np.broadcast_to(np.asarray(ln_sg_b[0])[None, :], (128, 512))),
        "wsT": f(np.transpose(np.asarray(w_spatial[0]), (2, 0, 1))),
        "bsp_c": f(np.repeat(np.asarray(b_spatial[0]).reshape(4, 2, 128).transpose(1, 0, 2), 64, axis=0)),
    }
    cw = np.asarray(conv_w[0])
    cb = np.asarray(conv_b[0])
    cols = [cw[0].reshape(44, 128).T, cw[1].reshape(44, 128).T, cw[2].reshape(44, 128).T, cb.reshape(44, 128).T]
    common["convp_c"] = f(np.concatenate(cols, axis=1))
    for k, v in (("b_out", b_out), ("ln1_g", ln1_g), ("ln1_b", ln1_b), ("b_down", b_down), ("ln2_g", ln2_g),
                 ("ln2_b", ln2_b)):
        common[k] = f(np.broadcast_to(np.asarray(v[0])[None, :], (128, D)))
    p = np.arange(128)
    common["ident_c"] = _bf16(np.eye(128))
    common["U_c"] = _bf16((p[:, None] >= p[None, :]))
    common["L_c"] = _bf16((p[:, None] < p[None, :]))
    common["trilT"] = f((p[:, None] <= p[None, :]))
    in_maps = []
    for c in range(8):
        b, j = c // 2, c % 2
        starts = [1024 * i + 512 * j for i in range(NSLOT)]
        own = np.concatenate([np.arange(s0, s0 + SL) for s0 in starts])
        halo, hvalid, hb = [], [], []
        for s0 in starts:
            if s0 >= 2:
                halo += [s0 - 2, s0 - 1]
                hvalid += [1.0, 1.0]
                hb.append(np.arange(s0 - 128, s0))
            else:
                halo += [0, 1]
                hvalid += [0.0, 0.0]
                hb.append(np.arange(0, 128))
        halo = np.array(halo)
        cols_all = np.concatenate([own, halo])
        xb = x[b]
        m = dict(common)
        m["xT_all"] = np.ascontiguousarray(xb.T)
        m["xT_own"] = np.ascontiguousarray(xb[cols_all].T)
        m["xT_hb"] = np.ascontiguousarray(xb[np.concatenate(hb)].T)
        m["x_own"] = np.ascontiguousarray(xb[cols_all])
        m["flags_c"] = f(np.broadcast_to(np.array(hvalid, dtype=np.float32)[None, :], (128, NHALO)))
        r = np.arange(8)
        cq = np.arange(512)
        mm_ = (128 * r[None, :, None] + p[:, None, None]) < (512 * j + cq[None, None, :])
        m["maskM_c"] = _bf16(mm_)
        kb = np.arange(NKB_H)
        tq = np.where(np.array(hvalid) > 0, halo, -1)
        mh = (128 * kb[None, :, None] + p[:, None, None]) < tq[None, None, :]
        mh = np.broadcast_to(mh[:, :, None, :], (128, NKB_H, 8, NHALO)).reshape(128, NKB_H, 64)
        m["maskH_c"] = _bf16(mh)
        in_maps.append(m)
    return in_maps


_CACHE = {}


def kernel(**inputs):
    in_maps = make_in_maps(**inputs)
    if "nc" not in _CACHE:
        _CACHE["nc"] = build_program()
    nc, dbg = _CACHE["nc"]
    res = run_bass_kernel_spmd(nc, in_maps, core_ids=list(range(8)))
    out = np.zeros((4, S, D), dtype=np.float32)
    for c in range(8):
        b, j = c // 2, c % 2
        o = np.asarray(res.results[c]["out"], dtype=np.float32)
        for i in range(NSLOT):
            s0 = 1024 * i + 512 * j
            out[b, s0:s0 + SL] = o[i * SL:(i + 1) * SL]
    return out
```

```python
import numpy as np
import ml_dtypes
import concourse.bass as bass
import concourse.mybir as mybir
from concourse.bass_utils import run_bass_kernel_spmd

F32 = mybir.dt.float32
BF16 = mybir.dt.bfloat16
AF = mybir.ActivationFunctionType
ALU = mybir.AluOpType

D = 1024
S = 4096
NSLOT = 4
SL = 512
T = NSLOT * SL
NHALO = 8
TT = T + NHALO
DFF = 2816
NCH = DFF // 128
ALPHA = 2.0 ** 0.25
LN_EPS = 1e-5
GELU_C = 1.5957691216057308
NKB_H = 28
ARENA_BYTES = 190 * 1024

DEBUG = {}
EMBED_WAITS = True


class Buf:
    __slots__ = ("name", "lo", "hi", "lw", "rd", "al", "inherit", "excl")

    def __init__(self, name, lo, hi):
        self.name, self.lo, self.hi = name, lo, hi
        self.lw = None
        self.rd = {}
        self.al = []
        self.inherit = None
        self.excl = False


class Op:
    __slots__ = ("eng", "fn", "deps", "sig", "flag", "dma", "idx", "nfree")


class Prog:
    ENGS = ("pe", "act", "dve", "pool", "sp")

    def __init__(self):
        self.ops = {e: [] for e in self.ENGS}
        self.abufs = []
        self.dead = []
        self.n = 0

    def buf(self, name, lo=None, hi=None):
        b = Buf(name, lo, hi)
        if lo is not None:
            inh = set()
            for d in self.dead:
                if d.lo < hi and lo < d.hi:
                    if d.lw is not None:
                        inh.add(d.lw)
                    inh.update(d.rd.values())
            b.inherit = inh or None
            for o in self.abufs:
                if o.lo < hi and lo < o.hi:
                    o.al.append(b)
                    b.al.append(o)
            self.abufs.append(b)
        return b

    def kill(self, b):
        if b.lo is None:
            return
        self.abufs.remove(b)
        for o in b.al:
            o.al.remove(b)
        b.al = []
        self.dead.append(b)

    def op(self, eng, fn, reads=(), writes=(), dma=False, nfree=0):
        o = Op()
        o.eng, o.fn, o.dma, o.flag, o.sig = eng, fn, dma, False, None
        o.nfree = nfree
        o.idx = self.n
        self.n += 1
        raw, oth = set(), set()
        for b in list(reads) + list(writes):
            if b.inherit:
                oth |= b.inherit
                b.inherit = None
        for b in reads:
            for bb in [b] + b.al:
                if bb.lw is not None:
                    raw.add(bb.lw)
            if b.excl:
                for r in b.rd.values():
                    if r.eng != eng:
                        oth.add(r)
        for b in writes:
            for bb in [b] + b.al:
                if bb.lw is not None:
                    oth.add(bb.lw)
                for r in bb.rd.values():
                    oth.add(r)
        deps = []
        for d in raw | oth:
            if d is o:
                continue
            if d.dma or o.dma:
                deps.append(d)
            elif d.eng != eng:
                deps.append(d)
            elif eng == "pool":
                deps.append(d)
            elif eng != "pe" and d in raw and d.nfree < 256:
                deps.append(d)
        best = {}
        out = []
        for d in deps:
            if d.dma:
                out.append(d)
            else:
                if d.eng not in best or best[d.eng].idx < d.idx:
                    best[d.eng] = d
        out.extend(best.values())
        o.deps = out
        for d in out:
            d.flag = True
        for b in reads:
            key = ("dma", o.idx) if dma else eng
            b.rd[key] = o
        for b in writes:
            for bb in [b] + b.al:
                bb.lw = o
                bb.rd = {}
        self.ops[eng].append(o)
        return o

    def emit(self, nc):
        NDS = 12
        sems = {}
        with nc.Block() as block:
            import contextlib
            with contextlib.ExitStack() as st:
                esem = {e: st.enter_context(nc.semaphore("s_" + e)) for e in self.ENGS}
                dsem = {e: [st.enter_context(nc.semaphore("d_%s%d" % (e, k))) for k in range(NDS)]
                        for e in ("pool", "sp")}
                for e in self.ENGS:
                    cnt = 0
                    dcnt = [0] * NDS
                    k = 0
                    for o in self.ops[e]:
                        if o.dma:
                            dcnt[k] += 16
                            o.sig = (dsem[e][k], dcnt[k])
                            k = (k + 1) % NDS
                        elif o.flag:
                            cnt += 1
                            o.sig = (esem[e], cnt)

                def run(engname, eng):
                    waited = {}
                    for o in self.ops[engname]:
                        need = {}
                        for d in o.deps:
                            sem, val = d.sig
                            if waited.get(id(sem), 0) < val and need.get(id(sem), (None, 0))[1] < val:
                                need[id(sem)] = (sem, val)
                        if o.dma:
                            sem, val = o.sig
                            if val > 16 and waited.get(id(sem), 0) < val - 16 and need.get(id(sem), (None, 0))[1] < val - 16:
                                need[id(sem)] = (sem, val - 16)
                        need = list(need.values())
                        embed = None
                        if need and not o.dma and EMBED_WAITS:
                            embed = need.pop()
                        for sem, val in need:
                            eng.wait_ge(sem, val)
                            waited[id(sem)] = val
                        n0 = nc.n_instructions() if embed is not None else 0
                        ins = o.fn(eng)
                        if embed is not None:
                            sem, val = embed
                            if nc.n_instructions() - n0 == 1:
                                ins._wait_ge(sem, val)
                            else:
                                raise RuntimeError("multi-instruction op cannot carry an embedded wait")
                            waited[id(sem)] = val
                        if o.dma:
                            ins.then_inc(o.sig[0], 16)
                        elif o.flag:
                            ins.then_inc(o.sig[0], 1)

                @block.tensor
                def _(e):
                    run("pe", e)

                @block.scalar
                def _(e):
                    run("act", e)

                @block.vector
                def _(e):
                    run("dve", e)

                @block.gpsimd
                def _(e):
                    run("pool", e)

                @block.sync
                def _(e):
                    run("sp", e)


class Tile:
    def __init__(self, ap, buf, off, nbytes):
        self.ap, self.buf, self.off, self.nbytes = ap, buf, off, nbytes


class Arena:
    def __init__(self, P, arena_ap, nbytes):
        self.P, self.a, self.nbytes = P, arena_ap, nbytes
        self.free = [(0, nbytes)]
        self.live = {}

    def alloc(self, name, shape, dtype, tracked=True, top=False):
        esz = 4 if dtype == F32 else 2
        n = 1
        for s in shape:
            n *= s
        nb = (n * esz + 63) // 64 * 64
        order = list(enumerate(self.free))
        if top:
            order = order[::-1]
        for k, (lo, hi) in order:
            if hi - lo >= nb:
                off = hi - nb if top else lo
                if hi - lo == nb:
                    self.free.pop(k)
                elif top:
                    self.free[k] = (lo, hi - nb)
                else:
                    self.free[k] = (lo + nb, hi)
                break
        else:
            raise RuntimeError("arena OOM for %s (%d B); free=%s" % (name, nb, self.free))
        ap = self.a[:, off // 2: off // 2 + n * esz // 2]
        if dtype == F32:
            ap = ap.bitcast(F32)
        if len(shape) == 2:
            ap = ap.rearrange("p (a b) -> p a b", a=shape[0])
        elif len(shape) == 3:
            ap = ap.rearrange("p (a b c) -> p a b c", a=shape[0], b=shape[1])
        t = Tile(ap, self.P.buf(name, off, off + nb) if tracked else None, off, nb)
        t.esz = esz
        t.name = name
        t.subs = []
        self.live[name] = t
        return t

    def sub(self, t, name, elem_lo, elem_hi):
        assert t.buf is None, "tiles with sub-buffers must be allocated with tracked=False"
        b = self.P.buf(name, t.off + elem_lo * t.esz, t.off + elem_hi * t.esz)
        t.subs.append(b)
        return b

    def release(self, *tiles):
        for t in tiles:
            del self.live[t.name]
            self.free.append((t.off, t.off + t.nbytes))
            if t.buf is not None:
                self.P.kill(t.buf)
            for b in t.subs:
                self.P.kill(b)
        self.free.sort()
        merged = []
        for lo, hi in self.free:
            if merged and merged[-1][1] == lo:
                merged[-1] = (merged[-1][0], hi)
            else:
                merged.append((lo, hi))
        self.free = merged


def build_program(phases_upto=99):
    nc = bass.Bass("TRN2", target_bir_lowering=False)
    P = Prog()

    def din(name, shape, dt=F32):
        return nc.dram_tensor(name, list(shape), dt, kind="ExternalInput").ap()

    xT_all = din("xT_all", [D, S])
    xT_own = din("xT_own", [D, TT])
    xT_hb = din("xT_hb", [D, 512])
    x_own = din("x_own", [TT, D])
    w_in = din("w_in", [D, 4608])
    w_bra = din("w_bra", [512, D])
    w_brb = din("w_brb", [512, D])
    w_out = din("w_out", [D, D])
    w_up = din("w_up", [D, 2 * DFF])
    w_down = din("w_down", [DFF, D])
    bgate_c = din("bgate_c", [128, 16])
    convp_c = din("convp_c", [128, 4 * 44])
    flags_c = din("flags_c", [128, NHALO])
    lnsg_g = din("lnsg_g", [128, 512])
    lnsg_b = din("lnsg_b", [128, 512])
    wsT = din("wsT", [128, 8, 128])
    trilT = din("trilT", [128, 128])
    bsp_c = din("bsp_c", [128, 4, 128])
    rowp = {k: din(k, [128, D]) for k in ("b_out", "ln1_g", "ln1_b", "b_down", "ln2_g", "ln2_b")}
    ident_c = din("ident_c", [128, 128], BF16)
    U_c = din("U_c", [128, 128], BF16)
    L_c = din("L_c", [128, 128], BF16)
    maskM_c = din("maskM_c", [128, 8, 512], BF16)
    maskH_c = din("maskH_c", [128, NKB_H, 64], BF16)
    out_d = nc.dram_tensor("out", [T, D], F32, kind="ExternalOutput").ap()
    h1s = nc.dram_tensor("h1s", [TT, D], F32).ap()
    h1s_buf = [P.buf("h1s%d" % r) for r in range(17)]
    dbg_outs = {}

    import contextlib
    with contextlib.ExitStack() as st:
        arena_t = st.enter_context(nc.sbuf_tensor("arena", [128, ARENA_BYTES // 2], BF16))
        A = Arena(P, arena_t[:], ARENA_BYTES)
        psum = []
        psall_t = st.enter_context(nc.psum_tensor("psall", [128, 4096], F32))
        psall = psall_t[:]
        for k in range(8):
            psum.append((psall[:, k * 512:(k + 1) * 512], P.buf("ps%d" % k)))
            psum[-1][1].excl = True

        def nfree_of(ap):
            n = 1
            for s in ap.shape[1:]:
                n *= s
            return n

        def mm(psb, out, lhsT, rhs, start, stop, reads):
            P.op("pe", lambda e: e.matmul(out, lhsT, rhs, start=start, stop=stop),
                 reads=reads, writes=[psb])

        def act(out, in_, func, reads, writes, bias=None, scale=None):
            kw = {}
            if bias is not None:
                kw["bias"] = bias
            if scale is not None:
                kw["scale"] = scale
            P.op("act", lambda e: e.activation(out, in_, func, **kw), reads=reads, writes=writes, nfree=nfree_of(out))

        def tt(eng, out, in0, in1, op, reads, writes):
            P.op(eng, lambda e: e.tensor_tensor(out, in0, in1, op), reads=reads, writes=writes, nfree=nfree_of(out))

        def ts(eng, out, in0, s1, s2, op0, op1, reads, writes):
            if s2 is None:
                P.op(eng, lambda e: e.tensor_scalar(out, in0, s1, None, op0), reads=reads, writes=writes, nfree=nfree_of(out))
            else:
                P.op(eng, lambda e: e.tensor_scalar(out, in0, s1, s2, op0, op1), reads=reads, writes=writes, nfree=nfree_of(out))

        def stt(eng, out, in0, scalar, in1, op0, op1, reads, writes):
            P.op(eng, lambda e: e.scalar_tensor_tensor(out, in0, scalar, in1, op0, op1),
                 reads=reads, writes=writes, nfree=nfree_of(out))

        def cp(eng, out, in_, reads, writes):
            if eng == "act":
                P.op("act", lambda e: e.copy(out, in_), reads=reads, writes=writes, nfree=nfree_of(out))
            else:
                P.op(eng, lambda e: e.tensor_copy(out, in_), reads=reads, writes=writes, nfree=nfree_of(out))

        def dma(eng, out, in_, reads, writes):
            return P.op(eng, lambda e: e.dma_start(out=out, in_=in_), reads=reads, writes=writes, dma=True)

        def dump(name, tile_ap, shape, dtype, reads):
            d = nc.dram_tensor("dbg_" + name, list(shape), dtype, kind="ExternalOutput").ap()
            dbg_outs[name] = dma("sp", d, tile_ap, reads, [P.buf("dbg_" + name)])

        def load_const(name, src, shape, dtype, eng="sp", parts=128):
            t = A.alloc(name, shape, dtype)
            dma(eng, t.ap[0:parts] if parts != 128 else t.ap, src, [], [t.buf])
            return t

        evac_rr = [0]

        def evac(out, in_, reads, writes):
            evac_rr[0] ^= 1
            cp("dve" if evac_rr[0] else "act", out, in_, reads, writes)

        pbank = [0]

        def next_bank():
            pbank[0] = (pbank[0] + 1) % 8
            return psum[pbank[0]]

        segs = [(i * SL, SL) for i in range(NSLOT)] + [(T, NHALO)]
        out_stores = []

        ident = A.alloc("ident", [128], BF16)
        Umat = A.alloc("U", [128], BF16)
        Lmat = A.alloc("L", [128], BF16)
        maskM = A.alloc("maskM", [8, 512], BF16)
        maskH = A.alloc("maskH", [NKB_H, 64], BF16)
        bgate = A.alloc("bgate", [16], F32)
        convp = A.alloc("convp", [4 * 44], F32)
        flags = A.alloc("flags", [NHALO], F32)

        def load_persistent_consts():
            for t_, s_ in ((ident, ident_c), (Umat, U_c), (Lmat, L_c), (maskM, maskM_c), (maskH, maskH_c),
                           (bgate, bgate_c), (convp, convp_c), (flags, flags_c)):
                dma("sp", t_.ap, s_, [], [t_.buf])

        kT = A.alloc("kT", [4, S], BF16, tracked=False)
        Vt = A.alloc("V", [32, 512], BF16, tracked=False)
        kT_b = [[A.sub(kT, "kT%d_%d" % (fc, t8), fc * S + t8 * 512, fc * S + t8 * 512 + 512)
                 for t8 in range(8)] for fc in range(4)]
        V_b = [A.sub(Vt, "V%d" % k, k * 512, k * 512 + 512) for k in range(32)]
        wk = A.alloc("wk", [8, 512], BF16)
        wv = A.alloc("wv", [8, 512], BF16)
        xa = [A.alloc("xa%d" % k, [8, 512], BF16) for k in range(2)]
        xTa = xT_all.rearrange("(c p) t -> p c t", p=128)
        w_in_r0 = w_in.rearrange("(c p) f -> p c f", p=128)
        wk_fc = [P.buf("wk_fc%d" % fc) for fc in range(4)]
        dma("pool", wk.ap[:, :, 0:128], w_in_r0[:, :, 512:640], [], [wk_fc[0]])
        xo = A.alloc("xo", [8, TT], BF16)
        wq = A.alloc("wq", [8, 512], BF16)
        NFRONT = 6 if phases_upto >= 3 else 8
        for t8 in range(NFRONT if phases_upto >= 1 else 0):
            x_ = xa[t8 % 2]
            dma("pool", x_.ap, xTa[:, :, t8 * 512:(t8 + 1) * 512], [], [x_.buf])
            if t8 == 0:
                for fc_ in range(1, 4):
                    dma("pool", wk.ap[:, :, fc_ * 128:(fc_ + 1) * 128], w_in_r0[:, :, 512 + fc_ * 128:512 + (fc_ + 1) * 128],
                        [], [wk_fc[fc_]])
                dma("pool", wv.ap, w_in_r0[:, :, 1024:1536], [], [wv.buf])
            if t8 == 1:
                load_persistent_consts()
            xTo = xT_own.rearrange("(c p) t -> p c t", p=128)
            if 2 <= t8 <= 5:
                s_ = t8 - 2
                dma("pool", xo.ap[:, :, s_ * SL:(s_ + 1) * SL], xTo[:, :, s_ * SL:(s_ + 1) * SL], [], [xo.buf])
            if t8 == 5:
                dma("pool", xo.ap[:, :, T:TT], xTo[:, :, T:TT], [], [xo.buf])
                dma("pool", wq.ap, w_in_r0[:, :, 0:512], [], [wq.buf])
            for fc in range(4):
                ps, psb = next_bank()
                for kc in range(8):
                    mm(psb, ps, wk.ap[:, kc, fc * 128:(fc + 1) * 128], x_.ap[:, kc, :],
                       kc == 0, kc == 7, [wk.buf, wk_fc[fc], x_.buf])
                evac(kT.ap[:, fc, t8 * 512:(t8 + 1) * 512], ps, [psb], [kT_b[fc][t8]])
            for tb in range(4):
                ps, psb = next_bank()
                for kc in range(8):
                    mm(psb, ps, x_.ap[:, kc, tb * 128:(tb + 1) * 128], wv.ap[:, kc, :],
                       kc == 0, kc == 7, [wv.buf, x_.buf])
                evac(Vt.ap[:, t8 * 4 + tb, :], ps, [psb], [V_b[t8 * 4 + tb]])
        A.release(wk, wv, xa[0], xa[1])
        if "kT" in DEBUG:
            dump("kT", kT.ap, [128, 4, S], BF16, [b for r in kT_b for b in r])
            dump("V", Vt.ap, [128, 32, 512], BF16, V_b)

        qT = A.alloc("qT", [4, TT], BF16, tracked=False)
        qT_b = [[A.sub(qT, "q%d_%d" % (fc, si), fc * TT + c0, fc * TT + c0 + n) for si, (c0, n) in enumerate(segs)]
                for fc in range(4)]
        side_jobs = []
        side_bank = [0]

        def q_group_jobs(si, fc, bank=None):
            c0, n = segs[si]
            holder = {}

            def job(kc):
                if kc == 0:
                    holder["b"] = psum[2 + side_bank[0]] if bank is None else bank
                    side_bank[0] ^= 1
                ps, psb = holder["b"]
                mm(psb, ps[:, :n], wq.ap[:, kc, fc * 128:(fc + 1) * 128], xo.ap[:, kc, c0:c0 + n],
                   kc == 0, kc == 7, [wq.buf, xo.buf])
                if kc == 7:
                    ts("dve", qT.ap[:, fc, c0:c0 + n], ps[:, :n], 0.125, None, ALU.mult, None, [psb], [qT_b[fc][si]])
            return [lambda p=p: job(p) for p in range(8)]

        if phases_upto >= 2:
            for fc in range(4):
                for j_ in q_group_jobs(0, fc, bank=next_bank()):
                    j_()
            late_segs = [1, 2, 3, 4] if phases_upto >= 3 else []
            for si in late_segs:
                for fc in range(4):
                    side_jobs.extend(q_group_jobs(si, fc))
            late = {}

            def kv_switch():
                A.release(xo, wq)
                late["wk"] = A.alloc("wk2", [8, 512], BF16)
                late["wv"] = A.alloc("wv2", [8, 512], BF16)
                late["xa"] = [A.alloc("xa2_%d" % k, [8, 512], BF16) for k in range(2)]
                dma("pool", late["wk"].ap, w_in_r0[:, :, 512:1024], [], [late["wk"].buf])
                dma("pool", late["xa"][0].ap, xTa[:, :, 6 * 512:7 * 512], [], [late["xa"][0].buf])
                dma("pool", late["wv"].ap, w_in_r0[:, :, 1024:1536], [], [late["wv"].buf])
                dma("pool", late["xa"][1].ap, xTa[:, :, 7 * 512:8 * 512], [], [late["xa"][1].buf])

            def kv_group_jobs(t8, grp):
                holder = {}

                def job(kc):
                    x_ = late["xa"][t8 % 2]
                    if kc == 0:
                        holder["b"] = psum[2 + side_bank[0]]
                        side_bank[0] ^= 1
                    ps, psb = holder["b"]
                    if grp < 4:
                        fc = grp
                        mm(psb, ps, late["wk"].ap[:, kc, fc * 128:(fc + 1) * 128], x_.ap[:, kc, :],
                           kc == 0, kc == 7, [late["wk"].buf, x_.buf])
                        if kc == 7:
                            cp("dve", kT.ap[:, fc, t8 * 512:(t8 + 1) * 512], ps, [psb], [kT_b[fc][t8]])
                    else:
                        tb = grp - 4
                        mm(psb, ps, x_.ap[:, kc, tb * 128:(tb + 1) * 128], late["wv"].ap[:, kc, :],
                           kc == 0, kc == 7, [late["wv"].buf, x_.buf])
                        if kc == 7:
                            cp("dve", Vt.ap[:, t8 * 4 + tb, :], ps, [psb], [V_b[t8 * 4 + tb]])
                return [lambda p=p: job(p) for p in range(8)]

            def kv_finish():
                A.release(late["wk"], late["wv"], late["xa"][0], late["xa"][1])
                late["xo"] = A.alloc("xo_b", [8, TT], BF16)
                dma("pool", late["xo"].ap, xT_own.rearrange("(c p) t -> p c t", p=128), [], [late["xo"].buf])

            if late_segs and NFRONT < 8:
                side_jobs.append(kv_switch)
                side_jobs.extend([(lambda: None)] * 24)
                for t8_ in range(NFRONT, 8):
                    for grp in range(8):
                        side_jobs.extend(kv_group_jobs(t8_, grp))
                side_jobs.append(kv_finish)
        if phases_upto < 3:
            A.release(wq)
        if "qT" in DEBUG:
            dump("qT", qT.ap, [128, 4, TT], BF16, [b for r in qT_b for b in r])

        ya = A.alloc("ya", [4, TT], BF16, tracked=False)
        ya_pb = [[A.sub(ya, "ya%d_%d" % (hp, si), hp * TT + c0, hp * TT + c0 + n) for si, (c0, n) in enumerate(segs)]
                 for hp in range(4)]
        ya_b = [ya_pb[h // 2] for h in range(8)]
        NE = 4
        e3 = [A.alloc("e3_%d" % k, [1024], F32) for k in range(NE)]
        qpad = [[A.alloc("qpad%d%d" % (b_, par), [SL], BF16) for par in range(2)] for b_ in range(2)]
        qh = A.alloc("qh", [8, NHALO], BF16)
        for b_ in range(2):
            for par in range(2):
                P.op("pool", lambda e, t=qpad[b_][par]: e.memset(t.ap, 0.0), reads=[], writes=[qpad[b_][par].buf], nfree=SL)
        P.op("pool", lambda e: e.memset(qh.ap, 0.0), reads=[], writes=[qh.buf], nfree=64)
        Zmat = A.alloc("Zmat", [128], BF16)
        P.op("pool", lambda e: e.memset(Zmat.ap, 0.0), reads=[], writes=[Zmat.buf], nfree=128)
        sp2 = [A.alloc("sp2_%d" % k, [1024], BF16) for k in range(2)]
        ex2 = [A.alloc("ex2_%d" % k, [1024], F32) for k in range(2)]
        w2 = [A.alloc("w2_%d" % k, [1024], BF16) for k in range(2)]

        def attn_stream(steps):
            N = len(steps)
            ZB = 1

            def nch(i):
                return len(steps[i]["chains"])

            def TW(i):
                return 1024 if nch(i) == 2 else steps[i]["chains"][0]["W"]

            def CS(i):
                return steps[i].get("cs", 0)

            def vw(ap, i):
                if nch(i) == 1:
                    return ap[:, :TW(i)]
                cs = CS(i)
                if cs == 0:
                    return ap[:, :1024]
                return ap[:, :1024].rearrange("p (a b) -> p a b", a=2)[:, :, cs:]

            def Z(i):
                s = steps[i]
                cs = CS(i)
                if s.get("pre") is not None:
                    s["pre"]()
                for ci, ch in enumerate(s["chains"]):
                    ps, psb = psum[2 * (i % ZB) + ci]
                    for (h, q_ap, q_buf, n, col0) in ch["groups"]:
                        mm(psb, ps[:, col0 + cs:col0 + n], kT.ap[:, h // 2, s["kb"] * 128:(s["kb"] + 1) * 128],
                           q_ap[:, cs:n], True, True, [kT_b[h // 2][s["kb"] // 4], q_buf])

            def EXP(i):
                p = i % ZB
                act(vw(e3[i % NE].ap, i), vw(psall[:, 1024 * p:1024 * p + 1024], i), AF.Exp,
                    [psum[2 * p + ci][1] for ci in range(nch(i))], [e3[i % NE].buf])

            def MASK(i):
                m = steps[i]["mask"]
                if m is None:
                    return
                cs = CS(i)
                for ci, ch in enumerate(steps[i]["chains"]):
                    W = ch["W"]
                    ev = e3[i % NE].ap[:, ci * 512 + cs:ci * 512 + W]
                    tt("dve", ev, ev, m[0][:, cs:W], ALU.mult, [e3[i % NE].buf, m[1]], [e3[i % NE].buf])

            def LN(i):
                act(vw(sp2[i % 2].ap, i), vw(e3[i % NE].ap, i), AF.Ln, [e3[i % NE].buf], [sp2[i % 2].buf], bias=1.0)

            def ZERO(i, base):
                s = steps[i]
                for ci, ch in enumerate(s["chains"]):
                    W = ch["W"]
                    pz, pzb = psum[base + ci]
                    mm(pzb, pz[:, :W], Zmat.ap, Umat.ap[:, :W] if W <= 128 else maskM.ap[:, 0, :W], True, False,
                       [Zmat.buf, Umat.buf, maskM.buf])

            def U(i):
                s = steps[i]
                cs = CS(i)
                if s["first"]:
                    ZERO(i, 4)
                for ci, ch in enumerate(s["chains"]):
                    W = ch["W"]
                    pc, pcb = psum[4 + ci]
                    mm(pcb, pc[:, cs:W], Umat.ap, sp2[i % 2].ap[:, ci * 512 + cs:ci * 512 + W], False, s["last"],
                       [Umat.buf, sp2[i % 2].buf])

            def EXPC(i):
                act(vw(ex2[i % 2].ap, i), vw(psall[:, 2048:3072], i), AF.Exp,
                    [psum[4 + ci][1] for ci in range(nch(i))], [ex2[i % 2].buf], scale=-1.0)

            def L(i):
                s = steps[i]
                cs = CS(i)
                if s["last"]:
                    return
                for ci, ch in enumerate(s["chains"]):
                    W = ch["W"]
                    pc, pcb = psum[4 + ci]
                    mm(pcb, pc[:, cs:W], Lmat.ap, sp2[i % 2].ap[:, ci * 512 + cs:ci * 512 + W], False, False,
                       [Lmat.buf, sp2[i % 2].buf])

            def WW(i):
                tt("dve", vw(w2[i % 2].ap, i), vw(ex2[i % 2].ap, i), vw(e3[i % NE].ap, i), ALU.mult,
                   [ex2[i % 2].buf, e3[i % NE].buf], [w2[i % 2].buf])

            def PV(i):
                s = steps[i]
                cs = CS(i)
                if s["first"]:
                    ZERO(i, 6)
                for ci, ch in enumerate(s["chains"]):
                    po, pob = psum[6 + ci]
                    for (h, q_ap, q_buf, n, col0) in ch["groups"]:
                        mm(pob, po[:, col0 + cs:col0 + n], Vt.ap[:, s["kb"], (h // 2) * 128:(h // 2 + 1) * 128],
                           w2[i % 2].ap[:, ci * 512 + col0 + cs:ci * 512 + col0 + n], False, s["last"],
                           [V_b[s["kb"]], w2[i % 2].buf])
                if s["last"]:
                    for ci, ch in enumerate(s["chains"]):
                        po, pob = psum[6 + ci]
                        for (h, si, c0, n, col0) in ch["out"]:
                            pr = (h % 2) * 64
                            cp("dve", ya.ap[pr:pr + 64, h // 2, c0:c0 + n], po[pr:pr + 64, col0:col0 + n], [pob], [ya_b[h][si]])

            Z(0)
            EXP(0)
            MASK(0)
            Z(1)
            EXP(1)
            MASK(1)
            Z(2)
            LN(0)
            U(0)
            for i in range(N):
                if i + 2 < N:
                    EXP(i + 2)
                if i + 3 < N:
                    Z(i + 3)
                if side_jobs:
                    side_jobs.pop(0)()
                if i + 2 < N:
                    MASK(i + 2)
                EXPC(i)
                WW(i)
                L(i)
                if side_jobs:
                    side_jobs.pop(0)()
                if i + 1 < N:
                    LN(i + 1)
                    U(i + 1)
                PV(i)

        if phases_upto >= 3:
            steps = []
            gidx = 0
            for i in range(NSLOT):
                nkb = 8 * (i + 1)
                for hp in range(4):
                    b_ = gidx % 2
                    gidx += 1
                    chains = []
                    for par in range(2):
                        h = 2 * hp + par
                        chains.append(dict(groups=[(h, qpad[b_][par].ap, qpad[b_][par].buf, SL, 0)], W=SL,
                                           out=[(h, i, i * SL, SL, 0)]))

                    def pre(i=i, hp=hp, b_=b_):
                        for par in range(2):
                            pr = par * 64
                            cp("dve", qpad[b_][par].ap[pr:pr + 64, :], qT.ap[pr:pr + 64, hp, i * SL:(i + 1) * SL],
                               [qT_b[hp][i]], [qpad[b_][par].buf])

                    for kb in range(nkb - 1, -1, -1):
                        steps.append(dict(chains=chains, kb=kb, first=(kb == nkb - 1), last=(kb == 0),
                                          pre=(pre if kb == nkb - 1 else None),
                                          cs=(128 * max(0, kb - (nkb - 8) - 4) if not DEBUG.get("notrim") else 0),
                                          mask=((maskM.ap[:, kb - (nkb - 8), :], maskM.buf) if kb >= nkb - 8 else None)))
            halo_chain = dict(groups=[(h, qh.ap[:, h, :], qh.buf, NHALO, h * NHALO) for h in range(8)], W=64,
                              out=[(h, 4, T, NHALO, h * NHALO) for h in range(8)])

            def pre_h():
                for par in range(2):
                    pr = par * 64
                    cp("dve", qh.ap[pr:pr + 64].rearrange("p (a two) b -> p a two b", two=2)[:, :, par, :],
                       qT.ap[pr:pr + 64, :, T:TT], [qT_b[fc][4] for fc in range(4)], [qh.buf])

            early = {}

            def post_q():
                A.release(qT)
                early["qT_released"] = True
                if phases_upto >= 4:
                    early["wuv"] = A.alloc("wuv", [8, 1024], BF16)
                    dma("pool", early["wuv"].ap, w_in.rearrange("(c p) f -> p c f", p=128)[:, :, 1536:2560], [],
                        [early["wuv"].buf])

            ih = 4 * 8 + 4 * 16
            slot1_pre = steps[ih]["pre"]
            steps[ih]["pre"] = lambda: (pre_h(), slot1_pre())
            for kb in range(NKB_H - 1, -1, -1):
                steps.append(dict(chains=[halo_chain], kb=kb, first=(kb == NKB_H - 1), last=(kb == 0),
                                  pre=(post_q if kb == NKB_H - 1 else None),
                                  mask=(maskH.ap[:, kb, :], maskH.buf)))
            attn_stream(steps)
            assert not side_jobs
            if "xo" in late:
                xo = late["xo"]
            else:
                A.release(wq)
        A.release(*e3, *sp2, *ex2, *w2, qh, *qpad[0], *qpad[1], Zmat)
        if phases_upto >= 3 and early.get("qT_released"):
            A.release(kT, Vt)
        else:
            A.release(kT, Vt, qT)
        if "ya" in DEBUG:
            dump("ya", ya.ap, [128, 4, TT], BF16, [b for r in ya_pb for b in r])

        yb = A.alloc("yb", [4, TT], BF16, tracked=False)
        yb_pb = [[A.sub(yb, "yb%d_%d" % (gp, si), gp * TT + c0, gp * TT + c0 + n) for si, (c0, n) in enumerate(segs)]
                 for gp in range(4)]
        if phases_upto >= 5:
            wg = [A.alloc("wg%d" % k, [2, 8, 128], BF16, top=True) for k in range(2)]
            wb = [A.alloc("wb%d" % k, [2, 4, 128], BF16, top=True) for k in range(2)]
            w_in_r = w_in.rearrange("(c p) f -> p c f", p=128)

            def load_wgb(fc):
                k = fc % 2
                dma("pool", wg[k].ap[:, 0], w_in_r[:, :, 2560 + fc * 128:2560 + (fc + 1) * 128], [], [wg[k].buf])
                dma("pool", wg[k].ap[:, 1], w_in_r[:, :, 3584 + fc * 128:3584 + (fc + 1) * 128], [], [wg[k].buf])
                dma("pool", wb[k].ap[:, 0], w_bra.rearrange("(c p) f -> p c f", p=128)[:, :, fc * 128:(fc + 1) * 128],
                    [], [wb[k].buf])
                dma("pool", wb[k].ap[:, 1], w_brb.rearrange("(c p) f -> p c f", p=128)[:, :, fc * 128:(fc + 1) * 128],
                    [], [wb[k].buf])
        if phases_upto >= 4:
            if phases_upto >= 3 and "wuv" in early:
                wuv = early["wuv"]
            else:
                wuv = A.alloc("wuv", [8, 1024], BF16)
                dma("pool", wuv.ap, w_in.rearrange("(c p) f -> p c f", p=128)[:, :, 1536:2560], [], [wuv.buf])
            xhb = A.alloc("xhb", [8, 512], BF16)
            dma("pool", xhb.ap, xT_hb.rearrange("(c p) t -> p c t", p=128), [], [xhb.buf])
            lg = load_const("lnsg_g", lnsg_g, [512], F32)
            lb = load_const("lnsg_b", lnsg_b, [512], F32)
            bsp = load_const("bsp", bsp_c, [4, 128], F32)
            wsf = load_const("wsf", wsT, [8, 128], F32)
            tril = load_const("tril", trilT, [128], F32)
            wsb = A.alloc("wsb", [8, 128], BF16)
            for g in range(8):
                tt("dve", wsb.ap[:, g, :], wsf.ap[:, g, :], tril.ap, ALU.mult, [wsf.buf, tril.buf], [wsb.buf])
            if phases_upto >= 5:
                load_wgb(0)
            ug2 = [A.alloc("ug_%d" % k, [4, SL], F32, tracked=False) for k in range(2)]
            ug_b2 = [[A.sub(ug2[k], "ug%d_%d" % (k, g), g * SL, (g + 1) * SL) for g in range(4)] for k in range(2)]
            gt = [A.alloc("gt%d" % k, [512], F32) for k in range(2)]
            vg2 = [[A.alloc("vg%d_%d" % (k, b_), [512], F32) for b_ in range(4)] for k in range(2)]
            vn = [A.alloc("vn%d" % k, [512], BF16) for k in range(4)]
            st2 = [(A.alloc("st6_%d" % k, [4, 6], F32), A.alloc("mv_%d" % k, [4, 2], F32), A.alloc("rs_%d" % k, [4], F32))
                   for k in range(2)]

            def D_proj(si):
                c0, n = segs[si]
                ug, ug_b, vg = ug2[si % 2], ug_b2[si % 2], vg2[si % 2]
                stt6, mv, rs = st2[si % 2]
                for g in range(4):
                    ps, psb = next_bank()
                    for kc in range(8):
                        mm(psb, ps[:, :n], wuv.ap[:, kc, g * 128:(g + 1) * 128], xo.ap[:, kc, c0:c0 + n],
                           kc == 0, kc == 7, [wuv.buf, xo.buf])
                    act(ug.ap[:, g, :n], ps[:, :n], AF.Gelu_apprx_tanh, [psb], [ug_b[g]])
                for bi in range(4):
                    xsrc, xbuf, col0 = (xo.ap, xo.buf, c0 + bi * 128) if si < 4 else (xhb.ap, xhb.buf, bi * 128)
                    ps, psb = next_bank()
                    for kc in range(8):
                        mm(psb, ps, xsrc[:, kc, col0:col0 + 128], wuv.ap[:, kc, 512:1024], kc == 0, kc == 7,
                           [xbuf, wuv.buf])
                    act(vg[bi].ap, ps, AF.Gelu_apprx_tanh, [psb], [vg[bi].buf])
                    P.op("dve", lambda e, bi=bi: e.bn_stats(stt6.ap[:, bi, :], vg[bi].ap), reads=[vg[bi].buf],
                         writes=[stt6.buf])
                    P.op("dve", lambda e, bi=bi: e.bn_aggr(mv.ap[:, bi, :], stt6.ap[:, bi, :]), reads=[stt6.buf],
                         writes=[mv.buf])
                act(rs.ap, mv.ap[:, :, 1], AF.Ln, [mv.buf], [rs.buf], bias=LN_EPS)
                act(rs.ap, rs.ap, AF.Exp, [rs.buf], [rs.buf], scale=-0.5)

            def D_ln(si):
                vg = vg2[si % 2]
                stt6, mv, rs = st2[si % 2]
                for bi in range(4):
                    stt("dve", vg[bi].ap, vg[bi].ap, mv.ap[:, bi, 0:1], lg.ap, ALU.subtract, ALU.mult,
                        [vg[bi].buf, mv.buf, lg.buf], [vg[bi].buf])
                    stt("dve", vn[bi].ap, vg[bi].ap, rs.ap[:, bi:bi + 1], lb.ap, ALU.mult, ALU.add,
                        [vg[bi].buf, rs.buf, lb.buf], [vn[bi].buf])

            def D_mix(si):
                c0, n = segs[si]
                ug, ug_b, vg = ug2[si % 2], ug_b2[si % 2], vg2[si % 2]
                for bi in range(4):
                    k = bi
                    t0, tn = (0, 128) if si < 4 else (126, 2)
                    for half in range(2):
                        ps, psb = next_bank()
                        for gg in range(4):
                            g = half * 4 + gg
                            mm(psb, ps[:, gg * tn:(gg + 1) * tn], vn[k].ap[:, (g // 2) * 128:(g // 2 + 1) * 128],
                               wsb.ap[:, g, t0:t0 + tn], True, True, [vn[k].buf, wsb.buf])
                        t1 = gt[half]
                        p0 = half * 2
                        psv = ps[:, :4 * tn].rearrange("p (a two b) -> p a two b", two=2, b=tn)
                        t1v = t1.ap[:, :2 * tn].rearrange("p (a b) -> p a b", a=2)
                        for par in range(2):
                            pr = par * 64
                            tt("dve", t1v[pr:pr + 64], psv[pr:pr + 64, :, par, :], bsp.ap[pr:pr + 64, p0:p0 + 2, t0:t0 + tn],
                               ALU.add, [psb, bsp.buf], [t1.buf])
                        tt("dve", yb.ap[:, p0:p0 + 2, c0 + bi * tn:c0 + (bi + 1) * tn], t1v,
                           ug.ap[:, p0:p0 + 2, bi * tn:(bi + 1) * tn], ALU.mult,
                           [t1.buf] + [ug_b[p0 + q] for q in range(2)], [yb_pb[p0 + q][si] for q in range(2)])

            D_proj(0)
            for si in range(len(segs)):
                D_ln(si)
                if si + 1 < len(segs):
                    D_proj(si + 1)
                D_mix(si)
            ug = ug2[0]
            vg = vg2[0] + vg2[1] + [ug2[1]]
            st6 = [t for tri in st2 for t in tri]
            A.release(wuv, xhb, lg, lb, bsp, wsf, tril, wsb, ug, *gt, *vg, *vn, *st6)
        if "yb" in DEBUG:
            dump("yb", yb.ap, [128, 4, TT], BF16, [b for r in yb_pb for b in r])

        mg = A.alloc("mg", [8, TT], BF16, tracked=False)
        mg_b = [[A.sub(mg, "mg%d_%d" % (fc, si), fc * TT + c0, fc * TT + c0 + n) for si, (c0, n) in enumerate(segs)]
                for fc in range(8)]
        if phases_upto >= 6:
            wo = A.alloc("wo", [8, D], BF16)
            dma("pool", wo.ap, w_out.rearrange("(c p) f -> p c f", p=128), [], [wo.buf])
            pb_out = load_const("b_out", rowp["b_out"], [D], F32)
            pg1 = load_const("ln1_g", rowp["ln1_g"], [D], F32)
            pb1 = load_const("ln1_b", rowp["ln1_b"], [D], F32)
        if phases_upto >= 5:
            ga = [A.alloc("ga%d" % k, [512], F32) for k in range(2)]
            gb = [A.alloc("gb%d" % k, [512], F32) for k in range(2)]
            it = 0
            for fc in range(8):
                k = fc % 2
                if fc + 1 < 8:
                    load_wgb(fc + 1)
                for si, (c0, n) in enumerate(segs):
                    kk = it % 2
                    it += 1
                    pga, pgb, pba, pbb = [next_bank() for _ in range(4)]
                    for kc in range(8):
                        mm(pga[1], pga[0][:, :n], wg[k].ap[:, 0, kc, :], xo.ap[:, kc, c0:c0 + n], kc == 0, kc == 7,
                           [wg[k].buf, xo.buf])
                    for kc in range(8):
                        mm(pgb[1], pgb[0][:, :n], wg[k].ap[:, 1, kc, :], xo.ap[:, kc, c0:c0 + n], kc == 0, kc == 7,
                           [wg[k].buf, xo.buf])
                    for hp in range(4):
                        mm(pba[1], pba[0][:, :n], wb[k].ap[:, 0, hp, :], ya.ap[:, hp, c0:c0 + n], hp == 0, hp == 3,
                           [wb[k].buf, ya_pb[hp][si]])
                    for hp in range(4):
                        mm(pbb[1], pbb[0][:, :n], wb[k].ap[:, 1, hp, :], yb.ap[:, hp, c0:c0 + n], hp == 0, hp == 3,
                           [wb[k].buf, yb_pb[hp][si]])
                    act(ga[kk].ap[:, :n], pga[0][:, :n], AF.Sigmoid, [pga[1], bgate.buf], [ga[kk].buf],
                        bias=bgate.ap[:, fc:fc + 1])
                    act(gb[kk].ap[:, :n], pgb[0][:, :n], AF.Sigmoid, [pgb[1], bgate.buf], [gb[kk].buf],
                        bias=bgate.ap[:, 8 + fc:9 + fc])
                    tt("dve", ga[kk].ap[:, :n], ga[kk].ap[:, :n], pba[0][:, :n], ALU.mult, [ga[kk].buf, pba[1]], [ga[kk].buf])
                    tt("dve", gb[kk].ap[:, :n], gb[kk].ap[:, :n], pbb[0][:, :n], ALU.mult, [gb[kk].buf, pbb[1]], [gb[kk].buf])
                    tt("dve", mg.ap[:, fc, c0:c0 + n], ga[kk].ap[:, :n], gb[kk].ap[:, :n], ALU.add,
                       [ga[kk].buf, gb[kk].buf], [mg_b[fc][si]])
            A.release(*wg, *wb, *ga, *gb)
        A.release(xo, ya, yb)
        if "mg" in DEBUG:
            dump("mg", mg.ap, [128, 8, TT], BF16, [b for r in mg_b for b in r])

        h1T = A.alloc("h1T", [8, TT], BF16, tracked=False, top=True)
        blocks = [(r * 128, 128) for r in range(16)] + [(T, NHALO)]
        h1T_b = [[A.sub(h1T, "h1T%d_%d" % (kc, r), kc * TT + r0, kc * TT + r0 + nr) for r, (r0, nr) in enumerate(blocks)]
                 for kc in range(8)]

        def ln_head(r_t, nr, st_t):
            for hlf in range(2):
                P.op("dve", lambda e, hlf=hlf: e.bn_stats(st_t.ap[:nr, hlf * 6:hlf * 6 + 6],
                                                         r_t.ap[:nr, hlf * 512:(hlf + 1) * 512]),
                     reads=[r_t.buf], writes=[st_t.buf])
            P.op("dve", lambda e: e.bn_aggr(st_t.ap[:nr, 12:14], st_t.ap[:nr, 0:12]), reads=[st_t.buf], writes=[st_t.buf])
            act(st_t.ap[:nr, 13:14], st_t.ap[:nr, 13:14], AF.Ln, [st_t.buf], [st_t.buf], bias=LN_EPS)
            act(st_t.ap[:nr, 13:14], st_t.ap[:nr, 13:14], AF.Exp, [st_t.buf], [st_t.buf], scale=-0.5)

        def ln_tail(r_t, nr, g_t, b_t, st_t, out_ap, out_bufs):
            stt("dve", r_t.ap[:nr], r_t.ap[:nr], st_t.ap[:nr, 12:13], g_t.ap[:nr], ALU.subtract, ALU.mult,
                [r_t.buf, st_t.buf, g_t.buf], [r_t.buf])
            stt("dve", out_ap, r_t.ap[:nr], st_t.ap[:nr, 13:14], b_t.ap[:nr], ALU.mult, ALU.add,
                [r_t.buf, st_t.buf, b_t.buf], out_bufs)

        if phases_upto >= 7:
            wu = [A.alloc("wu%d" % k, [2, 8, 256], BF16, top=True) for k in range(2)]
            w_up_r = w_up.rearrange("(c p) f -> p c f", p=128)

            def load_wu(jj):
                k = jj % 2
                dma("pool", wu[k].ap[:, 0], w_up_r[:, :, jj * 256:(jj + 1) * 256], [], [wu[k].buf])
                dma("pool", wu[k].ap[:, 1], w_up_r[:, :, DFF + jj * 256:DFF + (jj + 1) * 256], [], [wu[k].buf])

            load_wu(0)
        if phases_upto >= 6:
            NB3 = 3
            xb = [A.alloc("xb%d" % k, [D], F32) for k in range(NB3)]
            rr = [A.alloc("rr%d" % k, [D], F32) for k in range(NB3)]
            hh = [A.alloc("hh%d" % k, [D], F32) for k in range(2)]
            hb = [A.alloc("hb%d" % k, [D], BF16) for k in range(2)]
            stt_ = [A.alloc("stF%d" % k, [16], F32) for k in range(NB3)]

            ones1 = A.alloc("ones1", [128], BF16)
            bhi = A.alloc("bhi", [D], BF16)
            blo = A.alloc("blo", [D], BF16)
            P.op("dve", lambda e: e.memset(ones1.ap[0:1], 1.0), reads=[], writes=[ones1.buf], nfree=128)
            cp("dve", bhi.ap[0:1], pb_out.ap[0:1], [pb_out.buf], [bhi.buf])
            tt("dve", blo.ap[0:1], pb_out.ap[0:1], bhi.ap[0:1], ALU.subtract, [pb_out.buf, bhi.buf], [blo.buf])
            Fps = {}

            def F_mm(r):
                r0, nr = blocks[r]
                k = r % NB3
                dma("sp", xb[k].ap[:nr], x_own[r0:r0 + nr, :], [], [xb[k].buf])
                Fps[r] = []
                for hlf in range(2):
                    ps, psb = next_bank()
                    Fps[r].append((ps, psb))
                    for kc in range(8):
                        mm(psb, ps[:nr, :], mg.ap[:, kc, r0:r0 + nr], wo.ap[:, kc, hlf * 512:(hlf + 1) * 512],
                           kc == 0, False, [mg_b[kc][min(r // 4, 4)], wo.buf])
                    mm(psb, ps[:nr, :], ones1.ap[0:1, :nr], bhi.ap[0:1, hlf * 512:(hlf + 1) * 512], False, False,
                       [ones1.buf, bhi.buf])
                    mm(psb, ps[:nr, :], ones1.ap[0:1, :nr], blo.ap[0:1, hlf * 512:(hlf + 1) * 512], False, True,
                       [ones1.buf, blo.buf])

            def F_head(r):
                r0, nr = blocks[r]
                k = r % NB3
                for hlf in range(2):
                    ps, psb = Fps[r][hlf]
                    stt("dve", rr[k].ap[:nr, hlf * 512:(hlf + 1) * 512], xb[k].ap[:nr, hlf * 512:(hlf + 1) * 512], ALPHA,
                        ps[:nr, :], ALU.mult, ALU.add, [xb[k].buf, psb], [rr[k].buf])
                ln_head(rr[k], nr, stt_[k])

            def F_tail(r):
                r0, nr = blocks[r]
                k, k2 = r % NB3, r % 2
                ln_tail(rr[k], nr, pg1, pb1, stt_[k], hh[k2].ap[:nr], [hh[k2].buf])
                dma("sp", h1s[r0:r0 + nr, :], hh[k2].ap[:nr], [hh[k2].buf], [h1s_buf[r]])
                cp("act", hb[k2].ap[:nr], hh[k2].ap[:nr], [hh[k2].buf], [hb[k2].buf])

            def F_tr(r):
                r0, nr = blocks[r]
                k2 = r % 2
                for grp in range(2):
                    ps, psb = next_bank()
                    psv = ps.bitcast(BF16)
                    for q4 in range(4):
                        kc = grp * 4 + q4
                        P.op("pe", lambda e, kc=kc, q4=q4, psv=psv, k2=k2, nr=nr: e.transpose(
                            psv[:, q4 * 128:q4 * 128 + nr], hb[k2].ap[:nr, kc * 128:(kc + 1) * 128], ident.ap[:nr, :nr]),
                            reads=[hb[k2].buf, ident.buf], writes=[psb])
                    src_v = psv[:, 0:512].rearrange("p (a b) -> p a b", a=4)[:, :, :nr]
                    cp("act", h1T.ap[:, grp * 4:(grp + 1) * 4, r0:r0 + nr], src_v, [psb],
                                        [h1T_b[grp * 4 + q][r] for q in range(4)])

            nblk = len(blocks)
            F_mm(0)
            F_mm(1)
            F_head(0)
            for r in range(nblk):
                if r + 2 < nblk:
                    F_mm(r + 2)
                F_tail(r)
                if r + 1 < nblk:
                    F_head(r + 1)
                F_tr(r)
            A.release(wo, pb_out, pg1, pb1, *xb, *rr, *hh, *hb, *stt_, ones1, bhi, blo)
        A.release(mg)
        if "h1T" in DEBUG:
            dump("h1T", h1T.ap, [128, 8, TT], BF16, [b for r in h1T_b for b in r])

        actT = A.alloc("actT", [NCH, T], BF16, tracked=False)
        act_b = [[A.sub(actT, "act%d_%d" % (j, i), j * T + i * SL, j * T + (i + 1) * SL) for i in range(NSLOT)]
                 for j in range(NCH)]
        if phases_upto >= 8:
            wdA = A.alloc("wdA", [NCH // 2, D], BF16)
        if phases_upto >= 7:
            uh = A.alloc("uh", [44, NHALO], F32, tracked=False)
            bnd = A.alloc("bnd", [44, NHALO], F32, tracked=False)
            uh_b = [A.sub(uh, "uh%d" % c, c * NHALO, (c + 1) * NHALO) for c in range(44)]
            bnd_b = [A.sub(bnd, "bnd%d" % c, c * NHALO, (c + 1) * NHALO) for c in range(44)]
            cv = [[A.alloc("cv%d%d" % (s_, k), [SL], F32) for k in range(2)] for s_ in range(2)]
            sa = [A.alloc("sa%d" % k, [SL], F32) for k in range(2)]
            it = 0
            if phases_upto >= 8:
                dma("pool", wdA.ap, w_down.rearrange("(c p) f -> p c f", p=128)[:, 0:NCH // 2, :], [], [wdA.buf])
            for jj in range(NCH // 2):
                k = jj % 2
                if jj + 1 < NCH // 2:
                    load_wu(jj + 1)
                for sub in range(2):
                    j = 2 * jj + sub
                    for s_ in range(2):
                        c = s_ * NCH + j
                        ps, psb = next_bank()
                        for kc in range(8):
                            mm(psb, ps[:, :NHALO], wu[k].ap[:, s_, kc, sub * 128:(sub + 1) * 128],
                               h1T.ap[:, kc, T:TT], kc == 0, kc == 7, [wu[k].buf, h1T_b[kc][16]])
                        tt("dve", uh.ap[:, c, :], ps[:, :NHALO], flags.ap, ALU.mult, [psb, flags.buf], [uh_b[c]])
                        uv = uh.ap[:, c, :].rearrange("p (a b) -> p a b", b=2)
                        bv_ = bnd.ap[:, c, :].rearrange("p (a b) -> p a b", b=2)
                        ts("dve", bv_[:, :, 1:2], uv[:, :, 1:2], convp.ap[:, c:c + 1], None, ALU.mult, None,
                           [uh_b[c], convp.buf], [bnd_b[c]])
                        ts("dve", bv_[:, :, 0:1], uv[:, :, 1:2], convp.ap[:, 44 + c:45 + c], None, ALU.mult, None,
                           [uh_b[c], convp.buf], [bnd_b[c]])
                        stt("dve", bv_[:, :, 0:1], uv[:, :, 0:1], convp.ap[:, c:c + 1], bv_[:, :, 0:1], ALU.mult, ALU.add,
                            [uh_b[c], convp.buf, bnd_b[c]], [bnd_b[c]])
                for i in range(NSLOT):
                    for sub in range(2):
                        j = 2 * jj + sub
                        kk = it % 2
                        it += 1
                        for s_ in range(2):
                            c = s_ * NCH + j
                            ps, psb = next_bank()
                            for kc in range(8):
                                mm(psb, ps, wu[k].ap[:, s_, kc, sub * 128:(sub + 1) * 128],
                                   h1T.ap[:, kc, i * SL:(i + 1) * SL], kc == 0, kc == 7,
                                   [wu[k].buf] + [h1T_b[kc][4 * i + q] for q in range(4)])
                            c_ = cv[s_][kk]
                            act(c_.ap, ps, AF.Identity, [psb, convp.buf], [c_.buf],
                                bias=convp.ap[:, 132 + c:133 + c], scale=convp.ap[:, 88 + c:89 + c])
                            stt("dve", c_.ap[:, 1:SL], ps[:, 0:SL - 1], convp.ap[:, 44 + c:45 + c], c_.ap[:, 1:SL],
                                ALU.mult, ALU.add, [psb, convp.buf, c_.buf], [c_.buf])
                            stt("dve", c_.ap[:, 2:SL], ps[:, 0:SL - 2], convp.ap[:, c:c + 1], c_.ap[:, 2:SL],
                                ALU.mult, ALU.add, [psb, convp.buf, c_.buf], [c_.buf])
                            tt("dve", c_.ap[:, 0:2], c_.ap[:, 0:2], bnd.ap[:, c, 2 * i:2 * i + 2], ALU.add,
                               [c_.buf, bnd_b[c]], [c_.buf])
                        ca, cbv = cv[0][kk], cv[1][kk]
                        act(sa[kk].ap, ca.ap, AF.Silu, [ca.buf], [sa[kk].buf])
                        tt("pool", actT.ap[:, j, i * SL:(i + 1) * SL], sa[kk].ap, cbv.ap, ALU.mult, [sa[kk].buf, cbv.buf],
                           [act_b[j][i]])
            A.release(*wu, uh, bnd, *cv[0], *cv[1], *sa)
        A.release(h1T)
        if "act" in DEBUG:
            dump("act", actT.ap, [128, NCH, T], BF16, [b for r in act_b for b in r])

        if phases_upto >= 8:
            wdB = A.alloc("wdB", [NCH // 2, D], BF16)
            dma("pool", wdB.ap, w_down.rearrange("(c p) f -> p c f", p=128)[:, NCH // 2:NCH, :], [], [wdB.buf])
            pb_dn = load_const("b_down", rowp["b_down"], [D], F32)
            pg2 = load_const("ln2_g", rowp["ln2_g"], [D], F32)
            pb2 = load_const("ln2_b", rowp["ln2_b"], [D], F32)
            NB3 = 3
            hr = [A.alloc("hr%d" % k, [D], F32) for k in range(NB3)]
            rr = [A.alloc("rr%d" % k, [D], F32) for k in range(NB3)]
            oo = [A.alloc("oo%d" % k, [D], F32) for k in range(2)]
            stt_ = [A.alloc("stG%d" % k, [16], F32) for k in range(NB3)]

            def G_load(r):
                dma("sp", hr[r % NB3].ap, h1s[r * 128:(r + 1) * 128, :], [h1s_buf[r]], [hr[r % NB3].buf])

            Gps = {}

            def G_mm_a(r):
                r0 = r * 128
                Gps[r] = []
                for hlf in range(2):
                    ps, psb = next_bank()
                    Gps[r].append((ps, psb))
                    for j in range(NCH // 2):
                        mm(psb, ps, actT.ap[:, j, r0:r0 + 128], wdA.ap[:, j, hlf * 512:(hlf + 1) * 512],
                           j == 0, False, [act_b[j][r // 4], wdA.buf])

            def G_mm_b(r):
                k = r % NB3
                r0 = r * 128
                for hlf in range(2):
                    ps, psb = Gps[r][hlf]
                    for j in range(NCH // 2, NCH):
                        mm(psb, ps, actT.ap[:, j, r0:r0 + 128], wdB.ap[:, j - NCH // 2, hlf * 512:(hlf + 1) * 512],
                           False, j == NCH - 1, [act_b[j][r // 4], wdB.buf])
                    tt("dve", rr[k].ap[:, hlf * 512:(hlf + 1) * 512], ps, pb_dn.ap[:, hlf * 512:(hlf + 1) * 512],
                       ALU.add, [psb, pb_dn.buf], [rr[k].buf])
                stt("dve", rr[k].ap, hr[k].ap, ALPHA, rr[k].ap, ALU.mult, ALU.add, [hr[k].buf, rr[k].buf], [rr[k].buf])
                ln_head(rr[k], 128, stt_[k])

            def G_tail(r):
                k, k2 = r % NB3, r % 2
                ln_tail(rr[k], 128, pg2, pb2, stt_[k], oo[k2].ap, [oo[k2].buf])
                out_stores.append(dma("sp", out_d[r * 128:(r + 1) * 128, :], oo[k2].ap, [oo[k2].buf], [P.buf("out%d" % r)]))

            G_load(0)
            G_load(1)
            NPRE = 4
            for r in range(NPRE):
                G_mm_a(r)
            for r in range(16):
                if r + 2 < 16:
                    G_load(r + 2)
                if r >= NPRE:
                    G_mm_a(r)
                G_mm_b(r)
                if r >= 1:
                    G_tail(r - 1)
            G_tail(15)
        else:
            z_t = A.alloc("zt", [D], F32)
            P.op("dve", lambda e: e.memset(z_t.ap, 0.0), reads=[], writes=[z_t.buf])
            for r in range(16):
                out_stores.append(dma("sp", out_d[r * 128:(r + 1) * 128, :], z_t.ap, [z_t.buf], [P.buf("out%d" % r)]))

        fence_bufs = []
        fin = Op()
        fin.eng, fin.dma, fin.flag, fin.sig, fin.idx = "sp", False, False, None, P.n
        fin.deps = list(out_stores) + list(dbg_outs.values())
        fin.fn = lambda e: e.nop()
        P.ops["sp"].append(fin)

        P.emit(nc)
    return nc, list(dbg_outs.keys())


def _bf16(a):
    return np.asarray(a, dtype=np.float32).astype(ml_dtypes.bfloat16)


def make_in_maps(x, w_in, b_gate, ln_sg_g, ln_sg_b, w_spatial, b_spatial, w_branch_a, w_branch_b, w_out, b_out,
                 ln1_g, ln1_b, w_up, conv_w, conv_b, w_down, b_down, ln2_g, ln2_b):
    f = lambda a: np.ascontiguousarray(np.asarray(a, dtype=np.float32))
    x = f(x)
    common = {
        "w_in": f(w_in[0]), "w_bra": f(w_branch_a[0]), "w_brb": f(w_branch_b[0]), "w_out": f(w_out[0]),
        "w_up": f(w_up[0]), "w_down": f(w_down[0]),
        "bgate_c": f(np.asarray(b_gate[0]).reshape(16, 128).T),
        "lnsg_g": f(np.broadcast_to(np.asarray(ln_sg_g[0])[None, :], (128, 512))),
        "lnsg_b": f(np.broadcast_to(np.asarray(ln_sg_b[0])[None, :], (128, 512))),
        "wsT": f(np.transpose(np.asarray(w_spatial[0]), (2, 0, 1))),
        "bsp_c": f(np.repeat(np.asarray(b_spatial[0]).reshape(4, 2, 128).transpose(1, 0, 2), 64, axis=0)),
    }
    cw = np.asarray(conv_w[0])
    cb = np.asarray(conv_b[0])
    cols = [cw[0].reshape(44, 128).T, cw[1].reshape(44, 128).T, cw[2].reshape(44, 128).T, cb.reshape(44, 128).T]
    common["convp_c"] = f(np.concatenate(cols, axis=1))
    for k, v in (("b_out", b_out), ("ln1_g", ln1_g), ("ln1_b", ln1_b), ("b_down", b_down), ("ln2_g", ln2_g),
                 ("ln2_b", ln2_b)):
        common[k] = f(np.broadcast_to(np.asarray(v[0])[None, :], (128, D)))
    p = np.arange(128)
    common["ident_c"] = _bf16(np.eye(128))
    common["U_c"] = _bf16((p[:, None] >= p[None, :]))
    common["L_c"] = _bf16((p[:, None] < p[None, :]))
    common["trilT"] = f((p[:, None] <= p[None, :]))
    in_maps = []
    for c in range(8):
        b, j = c // 2, c % 2
        starts = [1024 * i + 512 * j for i in range(NSLOT)]
        own = np.concatenate([np.arange(s0, s0 + SL) for s0 in starts])
        halo, hvalid, hb = [], [], []
        for s0 in starts:
            if s0 >= 2:
                halo += [s0 - 2, s0 - 1]
                hvalid += [1.0, 1.0]
                hb.append(np.arange(s0 - 128, s0))
            else:
                halo += [0, 1]
                hvalid += [0.0, 0.0]
                hb.append(np.arange(0, 128))
        halo = np.array(halo)
        cols_all = np.concatenate([own, halo])
        xb = x[b]
        m = dict(common)
        m["xT_all"] = np.ascontiguousarray(xb.T)
        m["xT_own"] = np.ascontiguousarray(xb[cols_all].T)
        m["xT_hb"] = np.ascontiguousarray(xb[np.concatenate(hb)].T)
        m["x_own"] = np.ascontiguousarray(xb[cols_all])
        m["flags_c"] = f(np.broadcast_to(np.array(hvalid, dtype=np.float32)[None, :], (128, NHALO)))
        r = np.arange(8)
        cq = np.arange(512)
        mm_ = (128 * r[None, :, None] + p[:, None, None]) < (512 * j + cq[None, None, :])
        m["maskM_c"] = _bf16(mm_)
        kb = np.arange(NKB_H)
        tq = np.where(np.array(hvalid) > 0, halo, -1)
        mh = (128 * kb[None, :, None] + p[:, None, None]) < tq[None, None, :]
        mh = np.broadcast_to(mh[:, :, None, :], (128, NKB_H, 8, NHALO)).reshape(128, NKB_H, 64)
        m["maskH_c"] = _bf16(mh)
        in_maps.append(m)
    return in_maps


_CACHE = {}


def kernel(**inputs):
    in_maps = make_in_maps(**inputs)
    if "nc" not in _CACHE:
        _CACHE["nc"] = build_program()
    nc, dbg = _CACHE["nc"]
    res = run_bass_kernel_spmd(nc, in_maps, core_ids=list(range(8)))
    out = np.zeros((4, S, D), dtype=np.float32)
    for c in range(8):
        b, j = c // 2, c % 2
        o = np.asarray(res.results[c]["out"], dtype=np.float32)
        for i in range(NSLOT):
            s0 = 1024 * i + 512 * j
            out[b, s0:s0 + SL] = o[i * SL:(i + 1) * SL]
    return out
```
